# Optimizing a Trainium2 kernel written in Bass

```python
import functools
import math
import jax
import jax.numpy as jnp
from jax import lax
import numpy as np

D_MODEL = 1024
BATCH = 4
SEQ = 8192
DEPTH = 1
DEC_BATCH = 8
DEC_SEQ = 64
PAST_LEN = 1024

CHUNK = 64
HEAD_DIM = 64
N_HEADS_A = 8
N_HEADS_IDX = 4
IDX_DIM = 64
TOPK_MAX = 256
IDX_QBLK = CHUNK
IDX_SCALE = (IDX_DIM * N_HEADS_IDX) ** -0.5
N_HEADS_B = 8
BAND_CHUNKS = 8
REL_BACK = 128
REL_SIZE = REL_BACK + CHUNK
T5_BUCKETS = 32
T5_MAX_DIST = 128
D_FF = 2816
LN_EPS = 1e-5
NEG_INF = -1e30
WIDTH_A = N_HEADS_A * HEAD_DIM
WIDTH_B = N_HEADS_B * HEAD_DIM
IN_SIZES = (WIDTH_A, WIDTH_A, WIDTH_A, N_HEADS_IDX * IDX_DIM, IDX_DIM, N_HEADS_IDX,
            WIDTH_B, WIDTH_B, WIDTH_B, D_MODEL, D_MODEL)
N_IN = sum(IN_SIZES)

kernel_name = 'chunk_stream_dsa_band_hybrid'


def _layer_norm(x, g, b):
    x32 = x.astype(jnp.float32)
    mu = x32.mean(-1, keepdims=True)
    var = jnp.square(x32 - mu).mean(-1, keepdims=True)
    y = (x32 - mu) * lax.rsqrt(var + LN_EPS) * g.astype(jnp.float32) + b.astype(jnp.float32)
    return y.astype(x.dtype)


def _swiglu(x, wi, wo):
    a, u = jnp.split(x @ wi, 2, axis=-1)
    return (jax.nn.silu(a) * u) @ wo


def _t5_bucket(rel):
    half = T5_BUCKETS // 2
    exact = half // 2
    n = jnp.abs(rel)
    log_ratio = jnp.log(jnp.maximum(n, 1).astype(jnp.float32) / exact) / math.log(T5_MAX_DIST / exact)
    large = jnp.minimum(exact + (log_ratio * (half - exact)).astype(jnp.int32), half - 1)
    return (rel > 0).astype(jnp.int32) * half + jnp.where(n < exact, n, large)


def _split_in(h, w_in):
    b, t = h.shape[:2]
    z = h @ w_in
    offs, acc = [], 0
    for s in IN_SIZES[:-1]:
        acc += s
        offs.append(acc)
    qa, ka, va, qi, ki, wi, qb, kb, vb, ga, gb = jnp.split(z, offs, axis=-1)
    heads = lambda a, n: a.reshape(b, t, n, -1)
    return (heads(qa, N_HEADS_A), heads(ka, N_HEADS_A), heads(va, N_HEADS_A),
            heads(qi, N_HEADS_IDX), ki, wi,
            heads(qb, N_HEADS_B), heads(kb, N_HEADS_B), heads(vb, N_HEADS_B), ga, gb)


def _dsa_attend(q, qi, wi, q_pos, k, v, ki, k_pos, n_sel, t5_bias):
    f32 = jnp.float32
    dots = jnp.einsum('bqhd,bsd->bqhs', qi, ki).astype(f32)
    score = jnp.einsum('bqh,bqhs->bqs', wi.astype(f32), jax.nn.relu(dots)) * IDX_SCALE
    q_chunk = q_pos // CHUNK
    adm = (k_pos // CHUNK)[None, :] <= q_chunk[:, None]
    score = jnp.where(adm[None], score, -jnp.inf)
    _, sel = lax.top_k(score, n_sel)
    sel_pos = k_pos[sel]
    valid = (sel_pos // CHUNK) <= q_chunk[None, :, None]
    kg = jax.vmap(lambda kb_, ib: kb_[ib])(k, sel)
    vg = jax.vmap(lambda vb_, ib: vb_[ib])(v, sel)
    bias = t5_bias.T[_t5_bucket(sel_pos - q_pos[None, :, None])]
    logits = jnp.einsum('bqhd,bqkhd->bqkh', q, kg).astype(f32) * (HEAD_DIM ** -0.5) + bias.astype(f32)
    logits = jnp.where(valid[..., None], logits, NEG_INF)
    p = jax.nn.softmax(logits, axis=2).astype(v.dtype)
    return jnp.einsum('bqkh,bqkhd->bqhd', p, vg)


def _dsa_prompt(q, qi, wi, k, v, ki, t5_bias):
    b, t = q.shape[:2]
    nb = t // IDX_QBLK
    n_sel = min(TOPK_MAX, t // 4)
    pos = jnp.arange(t, dtype=jnp.int32)
    blk = lambda a: jnp.moveaxis(a.reshape((b, nb, IDX_QBLK) + a.shape[2:]), 1, 0)

    def one(args):
        qb, qib, wib, pb = args
        return _dsa_attend(qb, qib, wib, pb, k, v, ki, pos, n_sel, t5_bias)

    out = lax.map(one, (blk(q), blk(qi), blk(wi), pos.reshape(nb, IDX_QBLK)))
    return jnp.moveaxis(out, 0, 1).reshape(b, t, N_HEADS_A, HEAD_DIM)


def _band_attend(q, q_pos, k, v, k_pos, rel_bias):
    f32 = jnp.float32
    qc = q_pos // CHUNK
    kc = k_pos // CHUNK
    mask = ((k_pos >= 0)[None, :] & (kc[None, :] <= qc[:, None])
            & (kc[None, :] >= qc[:, None] - BAND_CHUNKS))
    ridx = jnp.clip(k_pos[None, :] - q_pos[:, None], -REL_BACK, CHUNK - 1) + REL_BACK
    bias = rel_bias[:, ridx]
    logits = jnp.einsum('bqhd,bkhd->bhqk', q, k).astype(f32) * (HEAD_DIM ** -0.5) + bias[None].astype(f32)
    logits = jnp.where(mask[None, None], logits, NEG_INF)
    p = jax.nn.softmax(logits, axis=-1).astype(v.dtype)
    return jnp.einsum('bhqk,bkhd->bqhd', p, v)


def _band_prompt(q, k, v, rel_bias):
    b, t = q.shape[:2]
    nc = t // CHUNK
    pad = BAND_CHUNKS * CHUNK
    kp = jnp.pad(k, ((0, 0), (pad, 0), (0, 0), (0, 0)))
    vp = jnp.pad(v, ((0, 0), (pad, 0), (0, 0), (0, 0)))
    qc = jnp.moveaxis(q.reshape(b, nc, CHUNK, N_HEADS_B, HEAD_DIM), 1, 0)

    def one(args):
        c, qb = args
        start = c * CHUNK
        kb = lax.dynamic_slice_in_dim(kp, start, pad + CHUNK, axis=1)
        vb = lax.dynamic_slice_in_dim(vp, start, pad + CHUNK, axis=1)
        q_pos = start + jnp.arange(CHUNK, dtype=jnp.int32)
        k_pos = start - pad + jnp.arange(pad + CHUNK, dtype=jnp.int32)
        return _band_attend(qb, q_pos, kb, vb, k_pos, rel_bias)

    out = lax.map(one, (jnp.arange(nc, dtype=jnp.int32), qc))
    return jnp.moveaxis(out, 0, 1).reshape(b, t, N_HEADS_B, HEAD_DIM)


def _prompt_mixer(h, w_in, t5_bias, rel_bias):
    qa, ka, va, qi, ki, wi, qb, kb, vb, ga, gb = _split_in(h, w_in)
    oa = _dsa_prompt(qa, qi, wi, ka, va, ki, t5_bias)
    ob = _band_prompt(qb, kb, vb, rel_bias)
    keep = min(BAND_CHUNKS * CHUNK, h.shape[1])
    return oa, ob, ga, gb, (ka, va, ki, kb[:, -keep:], vb[:, -keep:])


def _sample_mixer(h, w_in, t5_bias, rel_bias, c_k_a, c_v_a, c_kidx_a, c_k_b, c_v_b):
    qa, ka, va, qi, ki, wi, qb, kb, vb, ga, gb = _split_in(h, w_in)
    past, n = c_k_a.shape[1], h.shape[1]
    q_pos = past + jnp.arange(n, dtype=jnp.int32)
    ka_all = jnp.concatenate([c_k_a.astype(ka.dtype), ka], axis=1)
    va_all = jnp.concatenate([c_v_a.astype(va.dtype), va], axis=1)
    ki_all = jnp.concatenate([c_kidx_a.astype(ki.dtype), ki], axis=1)
    ka_pos = jnp.arange(past + n, dtype=jnp.int32)
    n_sel = min(TOPK_MAX, (past + n) // 4)
    oa = _dsa_attend(qa, qi, wi, q_pos, ka_all, va_all, ki_all, ka_pos, n_sel, t5_bias)
    band = c_k_b.shape[1]
    kb_all = jnp.concatenate([c_k_b.astype(kb.dtype), kb], axis=1)
    vb_all = jnp.concatenate([c_v_b.astype(vb.dtype), vb], axis=1)
    kb_pos = (past - band) + jnp.arange(band + n, dtype=jnp.int32)
    ob = _band_attend(qb, q_pos, kb_all, vb_all, kb_pos, rel_bias)
    return oa, ob, ga, gb, (ka, va, ki, kb_all[:, -band:], vb_all[:, -band:])


def _layer(x, mixer, ln1_g, ln1_b, ffn1_wi, ffn1_wo, ln2_g, ln2_b, w_branch_a, w_branch_b,
           w_out, ln3_g, ln3_b, ffn2_wi, ffn2_wo):
    alpha = (2.0 * DEPTH) ** 0.25
    h = _layer_norm(alpha * x + 0.5 * _swiglu(x, ffn1_wi, ffn1_wo), ln1_g, ln1_b)
    oa, ob, ga, gb, states = mixer(h)
    b, t = h.shape[:2]
    ya = oa.reshape(b, t, WIDTH_A) @ w_branch_a
    yb = ob.reshape(b, t, WIDTH_B) @ w_branch_b
    mix = (jax.nn.sigmoid(ga) * ya + jax.nn.sigmoid(gb) * yb) @ w_out
    h = _layer_norm(alpha * h + mix, ln2_g, ln2_b)
    h = _layer_norm(alpha * h + 0.5 * _swiglu(h, ffn2_wi, ffn2_wo), ln3_g, ln3_b)
    return h, states


def setup_inputs(seed: int = 0) -> dict:
    key = jax.random.key(seed)
    ks = jax.random.split(key, 24)
    f32 = jnp.float32
    beta = (8.0 * DEPTH) ** -0.25
    band = min(BAND_CHUNKS * CHUNK, PAST_LEN)

    def nrm(k, shape, scale):
        return jax.random.normal(k, shape, f32) * scale

    col_scale = jnp.concatenate([jnp.full((s,), beta if i in (2, 8) else 1.0, f32)
                                 for i, s in enumerate(IN_SIZES)])
    return {
        'x_prompt': nrm(ks[0], (BATCH, SEQ, D_MODEL), 1.0),
        'x_sample': nrm(ks[1], (DEC_BATCH, DEC_SEQ, D_MODEL), 1.0),
        'cache_k_a': nrm(ks[2], (DEPTH, DEC_BATCH, PAST_LEN, N_HEADS_A, HEAD_DIM), 1.0),
        'cache_v_a': nrm(ks[3], (DEPTH, DEC_BATCH, PAST_LEN, N_HEADS_A, HEAD_DIM), 1.0),
        'cache_kidx_a': nrm(ks[4], (DEPTH, DEC_BATCH, PAST_LEN, IDX_DIM), 1.0),
        'cache_k_b': nrm(ks[5], (DEPTH, DEC_BATCH, band, N_HEADS_B, HEAD_DIM), 1.0),
        'cache_v_b': nrm(ks[6], (DEPTH, DEC_BATCH, band, N_HEADS_B, HEAD_DIM), 1.0),
        't5_bias': nrm(ks[7], (N_HEADS_A, T5_BUCKETS), 0.5),
        'ln1_g': 1.0 + nrm(ks[8], (DEPTH, D_MODEL), 0.05),
        'ln1_b': nrm(ks[9], (DEPTH, D_MODEL), 0.02),
        'ffn1_wi': nrm(ks[10], (DEPTH, D_MODEL, 2 * D_FF), D_MODEL ** -0.5),
        'ffn1_wo': nrm(ks[11], (DEPTH, D_FF, D_MODEL), beta * D_FF ** -0.5),
        'ln2_g': 1.0 + nrm(ks[12], (DEPTH, D_MODEL), 0.05),
        'ln2_b': nrm(ks[13], (DEPTH, D_MODEL), 0.02),
        'w_in': nrm(ks[14], (DEPTH, D_MODEL, N_IN), D_MODEL ** -0.5) * col_scale,
        'rel_bias_b': nrm(ks[15], (DEPTH, N_HEADS_B, REL_SIZE), 0.5),
        'w_branch_a': nrm(ks[16], (DEPTH, WIDTH_A, D_MODEL), beta * WIDTH_A ** -0.5),
        'w_branch_b': nrm(ks[17], (DEPTH, WIDTH_B, D_MODEL), beta * WIDTH_B ** -0.5),
        'w_out': nrm(ks[18], (DEPTH, D_MODEL, D_MODEL), beta * D_MODEL ** -0.5),
        'ln3_g': 1.0 + nrm(ks[19], (DEPTH, D_MODEL), 0.05),
        'ln3_b': nrm(ks[20], (DEPTH, D_MODEL), 0.02),
        'ffn2_wi': nrm(ks[21], (DEPTH, D_MODEL, 2 * D_FF), D_MODEL ** -0.5),
        'ffn2_wo': nrm(ks[22], (DEPTH, D_FF, D_MODEL), beta * D_FF ** -0.5),
    }


def reference(x_prompt, x_sample, cache_k_a, cache_v_a, cache_kidx_a, cache_k_b, cache_v_b,
              t5_bias, ln1_g, ln1_b, ffn1_wi, ffn1_wo, ln2_g, ln2_b, w_in, rel_bias_b,
              w_branch_a, w_branch_b, w_out, ln3_g, ln3_b, ffn2_wi, ffn2_wo):
    y_prompt, y_sample = x_prompt, x_sample
    st_p, st_s = [], []
    for l in range(DEPTH):
        shared = (ln1_g[l], ln1_b[l], ffn1_wi[l], ffn1_wo[l], ln2_g[l], ln2_b[l],
                  w_branch_a[l], w_branch_b[l], w_out[l], ln3_g[l], ln3_b[l],
                  ffn2_wi[l], ffn2_wo[l])
        mix_p = functools.partial(_prompt_mixer, w_in=w_in[l], t5_bias=t5_bias,
                                  rel_bias=rel_bias_b[l])
        mix_s = functools.partial(_sample_mixer, w_in=w_in[l], t5_bias=t5_bias,
                                  rel_bias=rel_bias_b[l], c_k_a=cache_k_a[l],
                                  c_v_a=cache_v_a[l], c_kidx_a=cache_kidx_a[l],
                                  c_k_b=cache_k_b[l], c_v_b=cache_v_b[l])
        y_prompt, sp = _layer(y_prompt, mix_p, *shared)
        y_sample, ss = _layer(y_sample, mix_s, *shared)
        st_p.append(sp)
        st_s.append(ss)
    new_k_a_prompt = jnp.stack([s[0] for s in st_p])
    new_v_a_prompt = jnp.stack([s[1] for s in st_p])
    new_kidx_a_prompt = jnp.stack([s[2] for s in st_p])
    new_k_b_prompt = jnp.stack([s[3] for s in st_p])
    new_v_b_prompt = jnp.stack([s[4] for s in st_p])
    new_k_a_sample = jnp.stack([s[0] for s in st_s])
    new_v_a_sample = jnp.stack([s[1] for s in st_s])
    new_kidx_a_sample = jnp.stack([s[2] for s in st_s])
    new_k_b_sample = jnp.stack([s[3] for s in st_s])
    new_v_b_sample = jnp.stack([s[4] for s in st_s])
    return (y_prompt, y_sample, new_k_a_prompt, new_v_a_prompt, new_kidx_a_prompt,
            new_k_b_prompt, new_v_b_prompt, new_k_a_sample, new_v_a_sample,
            new_kidx_a_sample, new_k_b_sample, new_v_b_sample)
```

```python
import numpy as np
import concourse.bass as bass
import concourse.mybir as mybir
from concourse.bass_utils import run_bass_kernel_spmd
from contextlib import ExitStack

F32 = mybir.dt.float32
BF16 = mybir.dt.bfloat16
I32 = mybir.dt.int32
AF = mybir.ActivationFunctionType
ALU = mybir.AluOpType
AX = mybir.AxisListType
ENGS = ['pe', 'act', 'dve', 'pool', 'sp']

D = 1024
FF = 2816
NFF = 22
HD = 64
NH = 8
CHUNK = 64
PAST = 1024
BAND = 512
TOPK = 256
LN_EPS = 1e-5
ALPHA = 2.0 ** 0.25
NEG = -1e30
MASKNEG = -30000.0
N_BISECT = 22
ACT_SHARE = 0.30
C_QA, C_KA, C_VA, C_QI, C_KI, C_WI, C_QB, C_KB, C_VB, C_GA, C_GB = (
    0, 512, 1024, 1536, 1792, 1856, 1860, 2372, 2884, 3396, 4420)
N_IN = 5444


class G:
    NDS = {'sp': 40, 'pool': 24, 'act': 4, 'pe': 0, 'dve': 0}

    def __init__(self, nc, es):
        self.nc = nc
        self.es = es
        self.ops = {e: [] for e in ENGS}
        self.last_w = {}
        self.readers = {}
        self.nsb = 0
        self.sems = {}
        for e in ENGS:
            self.sems[('c', e)] = es.enter_context(nc.semaphore(f"c_{e}"))
            for j in range(self.NDS[e]):
                self.sems[('d', e, j)] = es.enter_context(nc.semaphore(f"d_{e}_{j}"))
        self.ccount = {e: 0 for e in ENGS}
        self.dnum = {e: 0 for e in ENGS}
        self.rr = 0

    def sb(self, shape, dt, name=None):
        self.nsb += 1
        return self.es.enter_context(self.nc.sbuf_tensor(name or f"sb{self.nsb}", list(shape), dt))

    def ps(self, shape, dt=F32, name=None):
        self.nsb += 1
        return self.es.enter_context(self.nc.psum_tensor(name or f"ps{self.nsb}", list(shape), dt))

    def op(self, eng, fn, reads=(), writes=(), dma=False):
        idx = len(self.ops[eng])
        deps = set()
        for k in reads:
            w = self.last_w.get(k)
            if w is not None:
                deps.add(w)
        for k in writes:
            w = self.last_w.get(k)
            if w is not None:
                deps.add(w)
            for r in self.readers.get(k, ()):
                deps.add(r)
        deps.discard((eng, idx))
        for k in writes:
            self.last_w[k] = (eng, idx)
            self.readers[k] = []
        for k in reads:
            self.readers.setdefault(k, []).append((eng, idx))
        self.ops[eng].append(dict(fn=fn, deps=deps, dma=dma, signal=False, hval=None, pre=None))
        return (eng, idx)

    def emit(self):
        nc = self.nc
        sems = self.sems
        ops = self.ops
        for e in ENGS:
            for o in ops[e]:
                for (de, di) in o['deps']:
                    d = ops[de][di]
                    if de == 'pe' and e == 'pe' and not d['dma']:
                        continue
                    d['signal'] = True
            for o in reversed(ops[e]):
                if not o['dma']:
                    o['signal'] = True
                    break
        for e in ENGS:
            for o in ops[e]:
                if o['dma']:
                    m = self.dnum[e]
                    self.dnum[e] += 1
                    j, rnd = m % self.NDS[e], m // self.NDS[e]
                    o['hval'] = (('d', e, j), 16 * (rnd + 1))
                    o['pre'] = (('d', e, j), 16 * rnd) if rnd > 0 else None
                elif o['signal']:
                    self.ccount[e] += 1
                    o['hval'] = (('c', e), self.ccount[e])
        ccount = dict(self.ccount)
        dfinal = {}
        for e in ENGS:
            for j in range(self.NDS[e]):
                n_j = (self.dnum[e] - j + self.NDS[e] - 1) // self.NDS[e] if self.dnum[e] > j else 0
                if n_j > 0:
                    dfinal[('d', e, j)] = 16 * n_j
        with nc.Block() as block:
            def mk(e):
                def body(engine):
                    seen = {}

                    def wait(sk, v):
                        if v <= 0 or seen.get(sk, 0) >= v:
                            return
                        engine.wait_ge(sems[sk], v)
                        seen[sk] = v
                    for o in ops[e]:
                        for (de, di) in sorted(o['deps']):
                            d = ops[de][di]
                            if de == 'pe' and e == 'pe' and not d['dma']:
                                continue
                            wait(*d['hval'])
                        if o['pre'] is not None:
                            wait(*o['pre'])
                        ins = o['fn'](engine)
                        if o['dma']:
                            ins.then_inc(sems[o['hval'][0]], 16)
                        elif o['signal']:
                            ins.then_inc(sems[o['hval'][0]], 1)
                    for e2 in ENGS:
                        if ccount[e2] > 0:
                            wait(('c', e2), ccount[e2])
                    for sk, v in dfinal.items():
                        wait(sk, v)
                return body
            block.tensor(mk('pe'))
            block.scalar(mk('act'))
            block.vector(mk('dve'))
            block.gpsimd(mk('pool'))
            block.sync(mk('sp'))
        self.ops = {e: [] for e in ENGS}
        self.last_w = {}
        self.readers = {}

    def dma(self, q, out, in_, r, w):
        return self.op(q, lambda e: e.dma_start(out=out, in_=in_), reads=r, writes=w, dma=True)

    def mm(self, out, lhsT, rhs, start, stop, r, w, skip=False):
        return self.op('pe', lambda e: e.matmul(out=out, lhsT=lhsT, rhs=rhs, start=start, stop=stop,
                                                skip_group_check=skip), reads=r, writes=w)

    def tr(self, out, in_, ident, r, w):
        return self.op('pe', lambda e: e.transpose(out=out, in_=in_, identity=ident), reads=r, writes=w)

    def act(self, out, in_, func, r, w, scale=None, bias=None, accum_out=None):
        kw = {}
        if scale is not None:
            kw['scale'] = scale
        if bias is not None:
            kw['bias'] = bias
        if accum_out is not None:
            kw['accum_out'] = accum_out
        return self.op('act', lambda e: e.activation(out=out, in_=in_, func=func, **kw), reads=r, writes=w)

    def ts(self, eng, out, in0, s1, s2, op0, op1, r, w, accum_out=None):
        kw = {}
        if op1 is not None:
            kw['op1'] = op1
        if accum_out is not None:
            kw['accum_out'] = accum_out
        return self.op(eng, lambda e: e.tensor_scalar(out=out, in0=in0, scalar1=s1, scalar2=s2, op0=op0, **kw),
                       reads=r, writes=w)

    def tt(self, eng, out, in0, in1, op, r, w):
        return self.op(eng, lambda e: e.tensor_tensor(out=out, in0=in0, in1=in1, op=op), reads=r, writes=w)

    def stt(self, out, in0, scalar, in1, op0, op1, r, w):
        return self.op('dve', lambda e: e.scalar_tensor_tensor(out=out, in0=in0, scalar=scalar, in1=in1,
                                                               op0=op0, op1=op1), reads=r, writes=w)

    def cp(self, eng, out, in_, r, w):
        if eng == 'act':
            return self.op('act', lambda e: e.activation(out=out, in_=in_, func=AF.Copy), reads=r, writes=w)
        return self.op(eng, lambda e: e.tensor_copy(out=out, in_=in_), reads=r, writes=w)

    def cast_eng(self):
        self.rr += 1
        return ['dve', 'pool', 'act'][self.rr % 3]


def load_weight_bf16(g, dst, src, kchunks, ncols, stage, stage_keys, dst_key, piece=2048):
    i = 0
    for kc in range(kchunks):
        for c0 in range(0, ncols, piece):
            n = min(piece, ncols - c0)
            st, sk = stage[i % len(stage)], stage_keys[i % len(stage)]
            i += 1
            g.dma('sp', st[:, 0:n], src[kc * 128:(kc + 1) * 128, c0:c0 + n], [], [sk])
            g.cp(g.cast_eng(), dst[:, kc, c0:c0 + n], st[:, 0:n], [sk], [dst_key + f"_{kc}_{c0}"])


def rsqrt_newton(g, var_ap, rs, small, rkeys, tag):
    v, ti, ui, b, t = small['v'], small['ti'], small['ui'], small['b'], small['t']
    g.ts('dve', v[:], var_ap, LN_EPS, None, ALU.add, None, rkeys, [tag + 'v'])
    g.ts('dve', ti[:], v[:].bitcast(I32), 1, None, ALU.arith_shift_right, None, [tag + 'v'], [tag + 'ti'])
    g.ts('dve', ui[:], ti[:], -1.0, 1597463007.0, ALU.mult, ALU.add, [tag + 'ti'], [tag + 'y'])
    y = ui[:].bitcast(F32)
    for it in range(3):
        g.stt(b[:], y, v[:], y, ALU.mult, ALU.mult, [tag + 'y', tag + 'v'], [tag + 'b'])
        g.stt(t[:], b[:], -0.5, y, ALU.mult, ALU.mult, [tag + 'b', tag + 'y'], [tag + 't'])
        dst = rs[:] if it == 2 else y
        g.stt(dst, y, 1.5, t[:], ALU.mult, ALU.add, [tag + 'y', tag + 't'], [tag + ('rs' if it == 2 else 'y')])


def make_small(g):
    return dict(st=g.sb([128, 2, 6], F32), mv=g.sb([128, 2], F32), rs=g.sb([128, 1], F32),
                nb=g.sb([128, 1], F32), v=g.sb([128, 1], F32), ti=g.sb([128, 1], I32), ui=g.sb([128, 1], I32),
                b=g.sb([128, 1], F32), t=g.sb([128, 1], F32))


def layer_norm_tile(g, y, ykey, gbc, bbc, small, tag):
    st, mv, rs, nb = small['st'], small['mv'], small['rs'], small['nb']
    for hh in range(2):
        g.op('dve', lambda e, hh=hh: e.bn_stats(out=st[:, hh, :], in_=y[:, hh * 512:(hh + 1) * 512]),
             reads=[ykey], writes=[tag + 'st'])
    g.op('dve', lambda e: e.bn_aggr(out=mv[:], in_=st[:].rearrange('p a b -> p (a b)')), reads=[tag + 'st'], writes=[tag + 'mv'])
    rsqrt_newton(g, mv[:, 1:2], rs, small, [tag + 'mv'], tag)
    g.stt(nb[:], mv[:, 0:1], -1.0, rs[:], ALU.mult, ALU.mult, [tag + 'mv', tag + 'rs'], [tag + 'nb'])
    g.act(y, y, AF.Identity, [ykey, tag + 'rs', tag + 'nb'], [ykey], scale=rs[:], bias=nb[:])
    g.tt('dve', y, y, gbc[:], ALU.mult, [ykey, 'lng'], [ykey])
    g.tt('pool', y, y, bbc[:], ALU.add, [ykey, 'lnb'], [ykey])


def ffn_phase(g, nc, es_outer, src, dst, wi, wo, lng, lnb, ident_d, ntiles, T=4):
    TN = T * 128
    with ExitStack() as es:
        g.es = es
        wi_b = g.sb([128, 8, 2 * FF], BF16)
        wo_b = g.sb([128, NFF, D], BF16)
        xin = g.sb([128, T, D], F32)
        xb = g.sb([128, D], BF16)
        xT = g.sb([128, 8, TN], BF16)
        gT = g.sb([128, NFF, TN], BF16)
        sil = g.sb([128, TN], F32)
        gbc = g.sb([128, D], F32)
        bbc = g.sb([128, D], F32)
        idf = g.sb([128, 128], F32)
        idb = g.sb([128, 128], BF16)
        small = [make_small(g) for _ in range(2)]
        up = [g.ps([128, 512], F32) for _ in range(4)]
        dn = [g.ps([128, 512], F32) for _ in range(2)]
        tp = g.ps([128, 8, 128], BF16)

        g.dma('sp', idf[:], ident_d, [], ['idf'])
        g.cp('dve', idb[:], idf[:], ['idf'], ['idb'])
        g.dma('sp', gbc[:], lng.to_broadcast([128, D]), [], ['lng'])
        g.dma('sp', bbc[:], lnb.to_broadcast([128, D]), [], ['lnb'])
        xflat = xin[:].rearrange("p a b -> p (a b)")
        nst_ = max(2, T // 2)
        stages = [xflat[:, q * 2048:(q + 1) * 2048] for q in range(T // 2)] if T >= 4 else [xflat[:, 0:2048]]
        skeys = [f"xin_t{2 * q}" for q in range(len(stages))]
        load_weight_bf16(g, wi_b, wi, 8, 2 * FF, stages, skeys, 'wi')
        load_weight_bf16(g, wo_b, wo, NFF, D, stages, skeys, 'wo')
        wi_keys = [f"wi_{kc}_{c0}" for kc in range(8) for c0 in range(0, 2 * FF, 2048)]
        wo_keys = [f"wo_{kc}_0" for kc in range(NFF)]
        g.op('pool', lambda e: e.memset(sil[:, 0:1], 0.0), reads=wi_keys + wo_keys,
             writes=[f"xin_t{j}" for j in range(T)] + ['sil'])

        nst = (ntiles + T - 1) // T
        for st_i in range(nst):
            t0 = st_i * T
            Tc = min(T, ntiles - t0)
            Tn = Tc * 128
            for j in range(Tc):
                xk = f"xin_t{j}"
                g.dma('sp', xin[:, j, :], src[(t0 + j) * 128:(t0 + j + 1) * 128, :], [], [xk])
                if T >= 4 and j % 2 == 0 and j + 1 < T:
                    pass
            for j in range(Tc):
                xk = f"xin_t{j}"
                g.cp('pool', xb[:], xin[:, j, :], [xk], ['xb'])
                g.act(xin[:, j, :], xin[:, j, :], AF.Copy, [xk, 'xb'], [xk], scale=ALPHA)
                for k in range(8):
                    g.tr(tp[:, k, :], xb[:, k * 128:(k + 1) * 128], idb[:], ['xb', 'idb'], ['tp'])
                g.cp('dve', xT[:, :, j * 128:(j + 1) * 128], tp[:], ['tp'], [f"xT{j}"])
            xTk = [f"xT{j}" for j in range(Tc)]
            for c in range(NFF):
                pa, pu = up[2 * (c % 2)], up[2 * (c % 2) + 1]
                ka, ku = f"up{2 * (c % 2)}", f"up{2 * (c % 2) + 1}"
                for k in range(8):
                    g.mm(pa[:, 0:Tn], wi_b[:, k, c * 128:(c + 1) * 128], xT[:, k, 0:Tn], k == 0, k == 7,
                         xTk + (wi_keys if st_i == 0 else []), [ka])
                for k in range(8):
                    g.mm(pu[:, 0:Tn], wi_b[:, k, FF + c * 128:FF + (c + 1) * 128], xT[:, k, 0:Tn],
                         k == 0, k == 7, xTk, [ku])
                g.act(sil[:, 0:Tn], pa[:, 0:Tn], AF.Tanh, [ka], ['sil'], scale=0.5)
                g.stt(sil[:, 0:Tn], sil[:, 0:Tn], 1.0, pa[:, 0:Tn], ALU.add, ALU.mult, ['sil', ka], ['sil'])
                g.stt(gT[:, c, 0:Tn], sil[:, 0:Tn], 0.5, pu[:, 0:Tn], ALU.mult, ALU.mult, ['sil', ku], [f"gT{c}"])
            gk = [f"gT{c}" for c in range(NFF)]
            for j in range(Tc):
                yk = f"xin_t{j}"
                for hh in range(2):
                    for c in range(NFF):
                        g.mm(dn[hh][:], gT[:, c, j * 128:(j + 1) * 128], wo_b[:, c, hh * 512:(hh + 1) * 512],
                             c == 0, c == NFF - 1, gk + (wo_keys if st_i == 0 else []), [f"dn{hh}"])
                    ysl = xin[:, j, hh * 512:(hh + 1) * 512]
                    g.stt(ysl, dn[hh][:], 0.5, ysl, ALU.mult, ALU.add, [f"dn{hh}", yk], [yk])
                layer_norm_tile(g, xin[:, j, :], yk, gbc, bbc, small[j % 2], f"ln{j % 2}")
                g.dma('pool', dst[(t0 + j) * 128:(t0 + j + 1) * 128, :], xin[:, j, :], [yk], [yk])
        g.es = es_outer
        g.emit()


def proj_tile(g, b, bufs, w_b, groups):
    hf, hb, hT, tp, idb = bufs['hf'][b], bufs['hb'][b], bufs['hT'][b], bufs['tp'], bufs['idb']
    g.cp('pool', hb[:], hf[:], [f"hf{b}"], [f"hb{b}"])
    for k in range(8):
        g.tr(tp[:, k * 128:(k + 1) * 128], hb[:, k * 128:(k + 1) * 128], idb[:], [f"hb{b}", 'idb'], ['tp'])
    g.cp('dve', hT[:].rearrange("p a b -> p (a b)"), tp[:, 0:1024], ['tp'], [f"hT{b}"])
    for (pap, key, c0, n) in groups:
        for k in range(8):
            g.mm(pap, hT[:, k, :], w_b[:, k, c0:c0 + n], k == 0, k == 7, [f"hT{b}", 'w_in'], [key])


def headT(g, src_bf, src_key, nheads, tp, idb, dstT, dst_key):
    for h in range(nheads):
        g.tr(tp[0:64, h * 128:(h + 1) * 128], src_bf[:, h * 64:(h + 1) * 64], idb[:], [src_key, 'idb'], ['tp'])
    g.cp('act', dstT[:].rearrange("p a b -> p (a b)"), tp[0:64, 0:nheads * 128], ['tp'], [dst_key])


def phase_b(g, es_outer, P):
    S, NT, NQ = P['S'], P['NT'], P['NQ']
    with ExitStack() as es:
        g.es = es
        w_b = g.sb([128, 8, N_IN], BF16)
        stage = [g.sb([128, 2048], F32) for _ in range(2)]
        idf = g.sb([128, 128], F32)
        idb = g.sb([128, 128], BF16)
        idx_sb = g.sb([128, NQ + 1], I32)
        bufs = dict(hf=[g.sb([128, D], F32) for _ in range(2)], hb=[g.sb([128, D], BF16) for _ in range(2)],
                    hT=[g.sb([128, 8, 128], BF16) for _ in range(2)], idb=idb)
        kaf = g.sb([128, 512], F32); vaf = g.sb([128, 512], F32); kif = g.sb([128, 64], F32)
        kbf = g.sb([128, 512], F32); vbf = g.sb([128, 512], F32)
        kab = g.sb([128, 512], BF16); kib = g.sb([128, 64], BF16); kbb = g.sb([128, 512], BF16)
        vaug = g.sb([128, 8, 65], BF16); vbug = g.sb([128, 8, 65], BF16)
        kaT = g.sb([64, 8, 128], BF16); kbT = g.sb([64, 8, 128], BF16); kiT = g.sb([64, 1, 128], BF16)
        qab = g.sb([128, 512], BF16); qib = g.sb([128, 256], BF16); qbb = g.sb([128, 512], BF16)
        qaT = g.sb([64, 8, 128], BF16); qiT = g.sb([64, 4, 128], BF16); qbT = g.sb([64, 8, 128], BF16)
        wif = g.sb([128, 4], F32)
        sga = g.sb([128, D], F32); sgb = g.sb([128, D], F32)
        pb = [g.ps([128, 512], F32) for _ in range(7)]
        tpf = g.ps([128, 512], F32)
        tp = tpf[:].bitcast(BF16)
        bufs['tp'] = tp

        g.dma('sp', idf[:], P['ident'], [], ['idf'])
        g.cp('dve', idb[:], idf[:], ['idf'], ['idb'])
        g.dma('sp', idx_sb[:], P['own_idx'], [], ['idx'])
        load_weight_bf16(g, w_b, P['w_in'], 8, N_IN, [st[:] for st in stage], ['stg0', 'stg1'], 'w_in_p')
        wkeys = [f"w_in_p_{kc}_{c0}" for kc in range(8) for c0 in range(0, N_IN, 2048)]
        g.op('pe', lambda e: e.nop(), reads=wkeys, writes=['w_in'])
        g.op('pool', lambda e: e.memset(vaug[:], 1.0), [], ['vaug'])
        g.op('pool', lambda e: e.memset(vbug[:], 1.0), [], ['vbug'])

        def kside_ingest(kind, t, src):
            pre = kind + '_'
            import os
            KS = os.environ.get('KS', 'ka,va,ki,kb,vb,samp')
            if kind == 's' and 'samp' not in KS:
                return
            if src.get('ka') is not None:
                ap, key = src['ka']
                g.cp('act', kaf[:], ap, [key], ['kaf'])
                g.cp('dve', kab[:], kaf[:], ['kaf'], ['kab'])
                headT(g, kab, 'kab', 8, tp, idb, kaT, 'kaT')
                g.dma('sp', P[pre + 'KaT'][t], kaT[:].rearrange("p a b -> p (a b)"), ['kaT'], ['kaT'])
                ap, key = src['va']
                g.cp('act', vaf[:], ap, [key], ['vaf'])
                if 'va' in KS:
                    g.cp('dve', vaug[:, :, 0:64], vaf[:].rearrange("p (h d) -> p h d", h=8), ['vaf'], ['vaug'])
                    g.dma('sp', P[pre + 'Va'][t], vaug[:].rearrange("p a b -> p (a b)"), ['vaug'], ['vaug'])
                ap, key = src['ki']
                g.cp('act', kif[:], ap, [key], ['kif'])
                if 'ki' in KS:
                    g.cp('dve', kib[:], kif[:], ['kif'], ['kib'])
                    g.tr(tp[0:64, 0:128], kib[:], idb[:], ['kib', 'idb'], ['tp'])
                    g.cp('act', kiT[:, 0, :], tp[0:64, 0:128], ['tp'], ['kiT'])
                    g.dma('sp', P[pre + 'kiT'][:, t * 128:(t + 1) * 128], kiT[:, 0, :], ['kiT'], ['kiT'])
            if src.get('kb') is not None and 'kb' in KS:
                tb = src['tb']
                ap, key = src['kb']
                g.cp('act', kbf[:], ap, [key], ['kbf'])
                g.cp('dve', kbb[:], kbf[:], ['kbf'], ['kbb'])
                headT(g, kbb, 'kbb', 8, tp, idb, kbT, 'kbT')
                g.dma('sp', P[pre + 'KbT'][tb], kbT[:].rearrange("p a b -> p (a b)"), ['kbT'], ['kbT'])
                ap, key = src['vb']
                g.cp('act', vbf[:], ap, [key], ['vbf'])
                g.cp('dve', vbug[:, :, 0:64], vbf[:].rearrange("p (h d) -> p h d", h=8), ['vbf'], ['vbug'])
                g.dma('sp', P[pre + 'Vb'][tb], vbug[:].rearrange("p a b -> p (a b)"), ['vbug'], ['vbug'])

        def ld(t):
            g.dma('sp', bufs['hf'][t % 2][:], P['h_all'][t * 128:(t + 1) * 128, :], [], [f"hf{t % 2}"])
        ld(0)
        for t in range(NT + 1):
            groups = [(pb[0][:], 'pb0', C_KA, 512), (pb[1][:], 'pb1', C_VA, 512), (pb[2][:, 0:64], 'pb2', C_KI, 64),
                      (pb[3][:], 'pb3', C_KB, 512), (pb[4][:], 'pb4', C_VB, 512)]
            proj_tile(g, t % 2, bufs, w_b, groups)
            if t + 1 <= NT:
                ld(t + 1)
            src = dict(ka=(pb[0][:], 'pb0'), va=(pb[1][:], 'pb1'), ki=(pb[2][:, 0:64], 'pb2'),
                       kb=(pb[3][:], 'pb3'), vb=(pb[4][:], 'pb4'))
            import os
            B1 = os.environ.get('B1', 'ingest,out')
            if 'ingest' not in B1:
                for kk_ in ['pb0', 'pb1', 'pb2', 'pb3', 'pb4']:
                    g.cp('act', kaf[:, 0:64], pb[int(kk_[2])][:, 0:64], [kk_], ['kaf'])
                continue
            if t < NT:
                src['tb'] = t
                kside_ingest('p', t, src)
                if 'out' not in B1:
                    continue
                g.dma('pool', P['kA_out'][t * 128:(t + 1) * 128, :], kaf[:], ['kaf'], ['kaf'])
                g.dma('pool', P['vA_out'][t * 128:(t + 1) * 128, :], vaf[:], ['vaf'], ['vaf'])
                g.dma('pool', P['kidx_out'][t * 128:(t + 1) * 128, :], kif[:], ['kif'], ['kif'])
                if t >= NT - 4:
                    o = (t - (NT - 4)) * 128
                    g.dma('pool', P['kB_out'][o:o + 128, :], kbf[:], ['kbf'], ['kbf'])
                    g.dma('pool', P['vB_out'][o:o + 128, :], vbf[:], ['vbf'], ['vbf'])
            else:
                src['tb'] = 4
                kside_ingest('s', 8, src)
                g.dma('pool', P['skA'], kaf[:], ['kaf'], ['kaf'])
                g.dma('pool', P['svA'], vaf[:], ['vaf'], ['vaf'])
                g.dma('pool', P['skidx'], kif[:], ['kif'], ['kif'])
                g.dma('pool', P['skB'][448:512, :], kbf[0:64, :], ['kbf'], ['kbf'])
                g.dma('pool', P['svB'][448:512, :], vbf[0:64, :], ['vbf'], ['vbf'])
        if 'roll' in P['bparts']:
            g.dma('pool', P['skB'][0:448, :], P['c_kb'][64:512, :], [], ['skB_roll'])
            g.dma('pool', P['svB'][0:448, :], P['c_vb'][64:512, :], [], ['svB_roll'])
        for t in range(8 if 'cache' in P['bparts'] else 0):
            g.dma('sp', kaf[:], P['c_ka'][t * 128:(t + 1) * 128, :], [], ['kaf'])
            g.dma('sp', vaf[:], P['c_va'][t * 128:(t + 1) * 128, :], [], ['vaf'])
            g.dma('sp', kif[:], P['c_ki'][t * 128:(t + 1) * 128, :], [], ['kif'])
            g.cp('dve', kab[:], kaf[:], ['kaf'], ['kab'])
            headT(g, kab, 'kab', 8, tp, idb, kaT, 'kaT')
            g.dma('sp', P['s_KaT'][t], kaT[:].rearrange("p a b -> p (a b)"), ['kaT'], ['kaT'])
            g.cp('pool', vaug[:, :, 0:64], vaf[:].rearrange("p (h d) -> p h d", h=8), ['vaf'], ['vaug'])
            g.dma('sp', P['s_Va'][t], vaug[:].rearrange("p a b -> p (a b)"), ['vaug'], ['vaug'])
            g.cp('dve', kib[:], kif[:], ['kif'], ['kib'])
            g.tr(tp[0:64, 0:128], kib[:], idb[:], ['kib', 'idb'], ['tp'])
            g.cp('act', kiT[:, 0, :], tp[0:64, 0:128], ['tp'], ['kiT'])
            g.dma('sp', P['s_kiT'][:, t * 128:(t + 1) * 128], kiT[:, 0, :], ['kiT'], ['kiT'])
        for t in range(4 if 'cache' in P['bparts'] else 0):
            g.dma('sp', kbf[:], P['c_kb'][t * 128:(t + 1) * 128, :], [], ['kbf'])
            g.dma('sp', vbf[:], P['c_vb'][t * 128:(t + 1) * 128, :], [], ['vbf'])
            g.cp('dve', kbb[:], kbf[:], ['kbf'], ['kbb'])
            headT(g, kbb, 'kbb', 8, tp, idb, kbT, 'kbT')
            g.dma('sp', P['s_KbT'][t], kbT[:].rearrange("p a b -> p (a b)"), ['kbT'], ['kbT'])
            g.cp('pool', vbug[:, :, 0:64], vbf[:].rearrange("p (h d) -> p h d", h=8), ['vbf'], ['vbug'])
            g.dma('sp', P['s_Vb'][t], vbug[:].rearrange("p a b -> p (a b)"), ['vbug'], ['vbug'])

        def ldq(i):
            hf = bufs['hf'][i % 2]
            g.op('pool', lambda e: e.indirect_dma_start(
                out=hf[:], out_offset=None, in_=P['h_all'],
                in_offset=bass.IndirectOffsetOnAxis(ap=idx_sb[:, i:i + 1], axis=0)),
                reads=['idx'], writes=[f"hf{i % 2}"], dma=True)
        nb2 = NQ + 1 if 'b2' in P['bparts'] else 0
        if nb2:
            ldq(0)
        for i in range(nb2):
            groups = [(pb[0][:], 'pb0', C_QA, 512), (pb[1][:, 0:256], 'pb1', C_QI, 256),
                      (pb[1][:, 256:260], 'pb1', C_WI, 4), (pb[2][:], 'pb2', C_QB, 512),
                      (pb[3][:], 'pb3', C_GA, 512), (pb[4][:], 'pb4', C_GA + 512, 512),
                      (pb[5][:], 'pb5', C_GB, 512), (pb[6][:], 'pb6', C_GB + 512, 512)]
            proj_tile(g, i % 2, bufs, w_b, groups)
            if i + 1 < nb2:
                ldq(i + 1)
            g.cp('dve', qab[:], pb[0][:], ['pb0'], ['qab'])
            headT(g, qab, 'qab', 8, tp, idb, qaT, 'qaT')
            g.dma('sp', P['q_qaT'][i], qaT[:].rearrange("p a b -> p (a b)"), ['qaT'], ['qaT'])
            g.cp('dve', qib[:], pb[1][:, 0:256], ['pb1'], ['qib'])
            g.cp('dve', wif[:], pb[1][:, 256:260], ['pb1'], ['wif'])
            headT(g, qib, 'qib', 4, tp, idb, qiT, 'qiT')
            g.dma('sp', P['q_qiT'][i], qiT[:].rearrange("p a b -> p (a b)"), ['qiT'], ['qiT'])
            g.dma('sp', P['q_wi'][i], wif[:], ['wif'], ['wif'])
            g.cp('dve', qbb[:], pb[2][:], ['pb2'], ['qbb'])
            headT(g, qbb, 'qbb', 8, tp, idb, qbT, 'qbT')
            g.dma('sp', P['q_qbT'][i], qbT[:].rearrange("p a b -> p (a b)"), ['qbT'], ['qbT'])
            for hh in range(2):
                g.act(sga[:, hh * 512:(hh + 1) * 512], pb[3 + hh][:], AF.Tanh, [f"pb{3 + hh}"], ['sga'], scale=0.5)
                g.act(sgb[:, hh * 512:(hh + 1) * 512], pb[5 + hh][:], AF.Tanh, [f"pb{5 + hh}"], ['sgb'], scale=0.5)
            g.ts('pool', sga[:], sga[:], 1.0, 0.5, ALU.add, ALU.mult, ['sga'], ['sga'])
            g.ts('pool', sgb[:], sgb[:], 1.0, 0.5, ALU.add, ALU.mult, ['sgb'], ['sgb'])
            g.dma('sp', P['q_sga'][i], sga[:], ['sga'], ['sga'])
            g.dma('sp', P['q_sgb'][i], sgb[:], ['sgb'], ['sgb'])
        g.es = es_outer
        g.emit()


def attn_core(g, bufs, kts, kt_src, vt_src, qT, qkey, table, tslot_fn, mask_fn, outT, out_key, mid_at=None,
              mid_fn=None, defer_norm=False):
    S_ps, O_ps, bc_ps = bufs['S_ps'], bufs['O_ps'], bufs['bc_ps']
    kt_t, vt_t, Sb, Pb = bufs['kt_t'], bufs['vt_t'], bufs['Sb'], bufs['Pb']
    oT, rden, ones, irep = bufs['oT'], bufs['rden'], bufs['ones'], bufs['irep']
    NBUF = len(kt_t)
    NS = len(S_ps)
    n = len(kts)
    items = [(p, hg) for p in range(n) for hg in range(2)]

    def load(p):
        b = p % NBUF
        g.dma('sp', kt_t[b][:].rearrange("p a b -> p (a b)"), kt_src(kts[p]), [], [f"kt{b}"])
        g.dma('sp', vt_t[b][:], vt_src(kts[p]), [], [f"vt{b}"])

    def stage_s(j):
        p, hg = items[j]
        kt = kts[p]
        b = p % NBUF
        sp_ = S_ps[j % NS]
        sk = f"S{j % NS}"
        mk = mask_fn(kt) if mask_fn is not None else None
        if mk is not None:
            g.mm(sp_[:, 0:512], mk[0], irep[:], True, False, [mk[1], 'irep'], [sk], skip=True)
        for hh in range(4):
            h = hg * 4 + hh
            g.mm(sp_[:, hh * 128:(hh + 1) * 128], kt_t[b][:, h, :], qT[:, h, :], mk is None and hh == 0, True,
                 [f"kt{b}", qkey], [sk], skip=True)

    def stage_e(j):
        p, hg = items[j]
        slot = tslot_fn(kts[p])
        sp_ = S_ps[j % NS]
        sk = f"S{j % NS}"
        pk = f"P{j % NS}"
        if slot is not None:
            sb_ = Sb[j % 2]
            g.stt(sb_[:], sp_[:].rearrange("p (a b) -> p a b", a=4), 0.125,
                  table[:, slot, hg * 4:(hg + 1) * 4, :], ALU.mult, ALU.add, [sk, 'table'], [f"Sb{j % 2}"])
            g.act(Pb[j % NS][:], sb_[:], AF.Exp, [f"Sb{j % 2}"], [pk])
        else:
            g.act(Pb[j % NS][:], sp_[:].rearrange("p (a b) -> p a b", a=4), AF.Exp, [sk], [pk], scale=0.125)

    def stage_v(j):
        p, hg = items[j]
        b = p % NBUF
        for hh in range(4):
            h = hg * 4 + hh
            g.mm(O_ps[hg][0:65, hh * 128:(hh + 1) * 128], vt_t[b][:, h * 65:(h + 1) * 65], Pb[j % NS][:, hh, :],
                 p == 0 and hh == 0, p == n - 1, [f"vt{b}", f"P{j % NS}"], [f"O{hg}"], skip=True)

    for p in range(min(NBUF - 1, n)):
        load(p)
    LA = NS - 1
    for j in range(min(LA, len(items))):
        stage_s(j)
    for j in range(len(items)):
        p, hg = items[j]
        if hg == 0 and p + NBUF - 1 < n:
            load(p + NBUF - 1)
        stage_e(j)
        if j + LA < len(items):
            stage_s(j + LA)
        stage_v(j)
        if mid_fn is not None and j + 1 >= min(mid_at, len(items)):
            mid_fn(len(items) - 1 - j)
    for hg in range(2):
        g.cp('act', oT[0:65, hg * 512:(hg + 1) * 512], O_ps[hg][0:65, :], [f"O{hg}"], [f"oT{hg}"])

    def norm():
        g.op('dve', lambda e: e.reciprocal(out=rden[64:65, :], in_=oT[64:65, :]), ['oT0', 'oT1'], ['rden'])
        for hg in range(2):
            g.mm(bc_ps[hg][0:64, :], ones[64:65, 0:64], rden[64:65, hg * 512:(hg + 1) * 512], True, True,
                 ['ones', 'rden'], [f"bc{hg}"])
            g.tt('dve', outT[:, hg * 4:(hg + 1) * 4, :],
                 oT[0:64, hg * 512:(hg + 1) * 512].rearrange("p (a b) -> p a b", a=4),
                 bc_ps[hg][0:64, :].rearrange("p (a b) -> p a b", a=4), ALU.mult, [f"oT{hg}", f"bc{hg}"], [out_key])
    if defer_norm:
        return norm
    norm()
    return None


def attn_bufs(g, idb):
    bufs = dict(
        S_ps=[g.ps([128, 512], F32) for _ in range(4)], O_ps=[g.ps([128, 512], F32) for _ in range(2)],
        bc_ps=[g.ps([128, 512], F32) for _ in range(2)],
        kt_t=[g.sb([64, 8, 128], BF16) for _ in range(4)], vt_t=[g.sb([128, 520], BF16) for _ in range(4)],
        Sb=[g.sb([128, 4, 128], F32) for _ in range(2)], Pb=[g.sb([128, 4, 128], BF16) for _ in range(4)],
        oT=g.sb([128, 1024], F32), rden=g.sb([128, 1024], F32), ones=g.sb([128, 64], F32),
        irep=g.sb([128, 512], BF16))
    return bufs


def attn_consts(g, bufs, idb):
    g.op('pool', lambda e: e.memset(bufs['ones'][:], 1.0), [], ['ones'])
    for r4 in range(4):
        g.cp('pool', bufs['irep'][:, r4 * 128:(r4 + 1) * 128], idb[:], ['idb'], ['irep'])


def phase_c1(g, es_outer, P):
    S, NT, NQ, NB, KSEL = P['S'], P['NT'], P['NQ'], P['NB'], P['KSEL']
    NMAX = max(NT * 128, 9 * 128)
    with ExitStack() as es:
        g.es = es
        idf = g.sb([128, 128], F32); idb = g.sb([128, 128], BF16)
        bufs = attn_bufs(g, idb)
        score = [g.sb([128, NMAX], F32) for _ in range(2)]
        junk = [g.sb([128, NMAX], BF16) for _ in range(2)]
        table = g.sb([128, 3, 8, 128], F32)
        c15 = g.sb([128, 8], F32)
        admneg = g.sb([128, 256], F32)
        pow2 = g.sb([128, NB], F32)
        Dh = g.sb([128, 4, 128], BF16)
        qaT = g.sb([64, 8, 128], BF16); qiT = g.sb([64, 4, 128], BF16); wif = g.sb([128, 4], F32)
        kiT = [g.sb([64, 512], BF16) for _ in range(2)]
        R = g.sb([128, 4, 512], BF16)
        oaT = g.sb([64, 8, 128], BF16)
        sm = {k: g.sb([128, 1], F32) for k in ['mn', 'mx', 'lo', 'w0', 't', 'cnt', 'inc', 'sacc']}
        tz = {k: g.sb([128, 1], F32) for k in ['cpos', 'cnn', 'a', 'b', 'tie', 'r', 'nt', 'thr', 'carry']}
        CH = 2048
        zc = g.sb([128, CH], BF16)
        cum = g.sb([128, CH], F32)
        onesb = g.sb([128, CH], BF16)
        hw = g.sb([128, NB], F32)
        sc_ps = bufs['O_ps'][0]
        dots = [bufs['S_ps'][0], bufs['S_ps'][1], bufs['S_ps'][2], bufs['S_ps'][3]]
        dkeys = ['S0', 'S1', 'S2', 'S3']

        g.dma('sp', idf[:], P['ident'], [], ['idf'])
        g.cp('dve', idb[:], idf[:], ['idf'], ['idb'])
        g.dma('sp', pow2[:], P['pow2'].to_broadcast([128, NB]), [], ['pow2'])
        g.dma('sp', c15[:], P['c15'].to_broadcast([128, 8]), [], ['c15'])
        attn_consts(g, bufs, idb)
        g.op('pool', lambda e: e.memset(onesb[:], 1.0), [], ['onesb'])

        def geom(i):
            samp = (i == NQ)
            nkt = 9 if samp else 2 * i + 2
            return samp, nkt, nkt * 128

        def load_tables(ti, which):
            if which == 'table':
                g.dma('sp', table[:].rearrange("p a b c -> p (a b c)"), P['tdsa'][ti], [], ['table'])
                for j in range(3):
                    g.tt('pool', table[:, j, :, :], table[:, j, :, :], c15[:].unsqueeze(2).to_broadcast([128, 8, 128]),
                         ALU.subtract, ['table', 'c15'], ['table'])
            else:
                g.dma('sp', admneg[:], P['admneg'][ti], [], ['admneg'])

        def IDX(i):
            samp, nkt, N = geom(i)
            sc, sck = score[i % 2], f"score{i % 2}"
            kis = P['s_kiT'] if samp else P['p_kiT']
            g.dma('sp', qiT[:].rearrange("p a b -> p (a b)"), P['q_qiT'][i], [], ['qiT'])
            g.dma('sp', wif[:], P['q_wi'][i], [], ['wif'])
            for h in range(4):
                g.ts('pool', Dh[:, h, :], idf[:], wif[:, h:h + 1], 1.0 / 16.0, ALU.mult, ALU.mult, ['idf', 'wif'], ['Dh'])
            ngr = (nkt + 3) // 4
            for gi in range(ngr):
                n = min(512, N - gi * 512)
                b = gi % 2
                g.dma('sp', kiT[b][:, 0:n], kis[:, gi * 512:gi * 512 + n], [], [f"kiT{b}"])
                for h in range(4):
                    g.mm(dots[h][:, 0:n], qiT[:, h, :], kiT[b][:, 0:n], True, True, [f"kiT{b}", 'qiT'], [dkeys[h]])
                for h in range(4):
                    g.act(R[:, h, 0:n], dots[h][:, 0:n], AF.Relu, [dkeys[h]], [f"R{h}"])
                for h in range(4):
                    g.mm(sc_ps[:, 0:n], Dh[:, h, :], R[:, h, 0:n], h == 0, h == 3, ['Dh', f"R{h}"], ['O0'])
                g.cp('act', sc[:, gi * 512:gi * 512 + n], sc_ps[:, 0:n], ['O0'], [sck])

        def BIS(i):
            samp, nkt, N = geom(i)
            sc, sck = score[i % 2], f"score{i % 2}"
            jk, jkk = junk[i % 2], f"junk{i % 2}"
            if i == 0 or samp:
                load_tables(1 if samp else 0, 'admneg')
            g.op('dve', lambda e: e.tensor_reduce(out=sm['mn'][:], in_=sc[:, 0:N], axis=AX.X, op=ALU.min),
                 [sck], ['mn'])
            g.op('dve', lambda e: e.tensor_reduce(out=sm['mx'][:], in_=sc[:, 0:N], axis=AX.X, op=ALU.max),
                 [sck], ['mx'])
            g.tt('dve', sc[:, N - 256:N], sc[:, N - 256:N], admneg[:], ALU.add, [sck, 'admneg', 'mn', 'mx'], [sck])
            g.ts('dve', sm['lo'][:], sm['mn'][:], -1.0, None, ALU.add, None, ['mn'], ['lo'])
            g.tt('dve', sm['w0'][:], sm['mx'][:], sm['lo'][:], ALU.subtract, ['mx', 'lo'], ['w0'])
            g.ts('dve', hw[:], pow2[:], sm['w0'][:], None, ALU.mult, None, ['pow2', 'w0'], ['hw'])
            yield
            ksel = (min(TOPK, (PAST + 64) // 4) if samp else KSEL)
            n2 = min(int(round(ACT_SHARE * N / 128.0)) * 128, 4096, N - 128)
            n1 = N - n2
            dump = cum[:].bitcast(BF16)
            for k in range(NB):
                g.tt('dve', sm['t'][:], sm['lo'][:], hw[:, k:k + 1], ALU.add, ['lo', 'hw'], ['t'])
                if n2 > 0:
                    g.act(dump[:, 0:n2], sc[:, n1:N], AF.Sign, [sck, 't'], ['cum', 'sacc'], scale=-1.0,
                          bias=sm['t'][:], accum_out=sm['sacc'][:])
                g.ts('dve', jk[:, 0:n1], sc[:, 0:n1], sm['t'][:], 0.0, ALU.is_gt, ALU.add, [sck, 't'],
                     [jkk, 'cnt'], accum_out=sm['cnt'][:])
                if n2 > 0:
                    g.stt(sm['cnt'][:], sm['sacc'][:], -0.5, sm['cnt'][:], ALU.mult, ALU.add, ['sacc', 'cnt'], ['cnt'])
                g.ts('dve', sm['inc'][:], sm['cnt'][:], ksel - 0.5 - 0.5 * n2, None, ALU.is_gt, None, ['cnt'], ['inc'])
                g.stt(sm['lo'][:], sm['inc'][:], hw[:, k:k + 1], sm['lo'][:], ALU.mult, ALU.add,
                      ['inc', 'hw', 'lo'], ['lo'])
                yield
            kf = float(ksel)
            g.ts('dve', jk[:, 0:N], sc[:, 0:N], 0.0, 0.0, ALU.is_gt, ALU.add, [sck], [jkk, 'cpos'],
                 accum_out=tz['cpos'][:])
            yield
            g.ts('dve', jk[:, 0:N], sc[:, 0:N], 0.0, 0.0, ALU.is_ge, ALU.add, [sck, 'cpos'], [jkk, 'cnn'],
                 accum_out=tz['cnn'][:])
            g.ts('dve', tz['a'][:], tz['cpos'][:], kf - 0.5, None, ALU.is_lt, None, ['cpos'], ['tz_a'])
            g.ts('dve', tz['b'][:], tz['cnn'][:], kf + 0.5, None, ALU.is_gt, None, ['cnn'], ['tz_b'])
            g.tt('dve', tz['tie'][:], tz['a'][:], tz['b'][:], ALU.mult, ['tz_a', 'tz_b'], ['tie'])
            g.ts('dve', tz['r'][:], tz['cpos'][:], -1.0, kf, ALU.mult, ALU.add, ['cpos'], ['tz_r0'])
            g.tt('dve', tz['r'][:], tz['r'][:], tz['tie'][:], ALU.mult, ['tz_r0', 'tie'], ['tz_r'])
            g.ts('dve', tz['nt'][:], tz['tie'][:], -1.0, 1.0, ALU.mult, ALU.add, ['tie'], ['tz_nt'])
            g.tt('dve', tz['thr'][:], sm['lo'][:], tz['nt'][:], ALU.mult, ['lo', 'tz_nt'], ['thr'])
            yield
            for c0 in range(0, N, CH):
                n = min(CH, N - c0)
                g.ts('dve', zc[:, 0:n], sc[:, c0:c0 + n], 0.0, None, ALU.is_equal, None, [sck], ['zc'])
                init = 0.0 if c0 == 0 else tz['carry'][:]
                g.op('dve', lambda e, n=n, init=init: e.tensor_tensor_scan(
                    out=cum[:, 0:n], data0=onesb[:, 0:n], data1=zc[:, 0:n], initial=init, op0=ALU.mult, op1=ALU.add),
                    ['zc', 'onesb', 'carry'], ['cum'])
                if c0 + n < N:
                    g.cp('dve', tz['carry'][:], cum[:, n - 1:n], ['cum'], ['carry'])
                g.stt(zc[:, 0:n], cum[:, 0:n], tz['r'][:], zc[:, 0:n], ALU.is_le, ALU.mult, ['cum', 'tz_r', 'zc'], ['zc'])
                g.stt(zc[:, 0:n], sc[:, c0:c0 + n], tz['thr'][:], zc[:, 0:n], ALU.is_gt, ALU.max, [sck, 'thr', 'zc'], ['zc'])
                g.ts('dve', jk[:, c0:c0 + n], zc[:, 0:n], -1.0, -MASKNEG, ALU.add, ALU.mult, ['zc'], [jkk])
                yield
            if 'dbg_score' in P:
                g.dma('sp', P['dbg_score'][i, :, 0:N], sc[:, 0:N], [sck], ['dbg'])
                g.dma('sp', P['dbg_junk'][i, :, 0:N], jk[:, 0:N], [jkk], ['dbg'])

        def run_all(gen):
            for _ in gen:
                pass

        def make_ticker(gen, nsteps):
            st = {'left': nsteps, 'done': False}

            def tick(items_left):
                if st['done']:
                    return
                k = st['left'] if items_left <= 0 else -(-st['left'] // (items_left + 1))
                for _ in range(max(1, k)):
                    try:
                        next(gen)
                        st['left'] = max(0, st['left'] - 1)
                    except StopIteration:
                        st['done'] = True
                        return
            return tick

        def ATT(i):
            samp, nkt, N = geom(i)
            jk, jkk = junk[i % 2], f"junk{i % 2}"
            if i == 0 or samp:
                load_tables(1 if samp else 0, 'table')
            g.dma('sp', qaT[:].rearrange("p a b -> p (a b)"), P['q_qaT'][i], [], ['qaT'])
            kas = P['s_KaT'] if samp else P['p_KaT']
            vas = P['s_Va'] if samp else P['p_Va']
            near = [kt for kt in range(nkt - 3, nkt) if kt >= 0]
            order = near + [kt for kt in range(nkt) if kt not in near]
            return attn_core(g, bufs, order, lambda kt: kas[kt], lambda kt: vas[kt], qaT, 'qaT', table,
                             lambda kt: (kt - (nkt - 3)) if kt >= nkt - 3 else None,
                             lambda kt: (jk[:, kt * 128:(kt + 1) * 128], jkk), oaT, 'oaT',
                             mid_at=2 * len(near),
                             mid_fn=make_ticker(BIS(i + 1), NB + 4 + (geom(i + 1)[2] + CH - 1) // CH) if i + 1 <= NQ else None,
                             defer_norm=True)

        IDX(0)
        run_all(BIS(0))
        if NQ >= 1:
            IDX(1)
        for i in range(NQ + 1):
            norm = ATT(i)
            if i + 2 <= NQ:
                IDX(i + 2)
            norm()
            g.dma('sp', P['q_oaT'][i], oaT[:].rearrange("p a b -> p (a b)"), ['oaT'], ['oaT'])
        g.es = es_outer
        g.emit()


def phase_c2(g, es_outer, P):
    S, NT, NQ = P['S'], P['NT'], P['NQ']
    with ExitStack() as es:
        g.es = es
        idf = g.sb([128, 128], F32); idb = g.sb([128, 128], BF16)
        bufs = attn_bufs(g, idb)
        table = g.sb([128, 6, 8, 128], F32)
        idx_sb = g.sb([128, NQ + 1], I32)
        wba = g.sb([64, 8, D], BF16); wbb = g.sb([64, 8, D], BF16); wout = g.sb([128, 8, D], BF16)
        stage = [g.sb([128, 2048], F32) for _ in range(2)]
        gbc = g.sb([128, D], F32); bbc = g.sb([128, D], F32)
        qbT = g.sb([64, 8, 128], BF16); oaT = g.sb([64, 8, 128], BF16); obT = g.sb([64, 8, 128], BF16)
        sga = g.sb([128, D], F32); sgb = g.sb([128, D], F32); hown = g.sb([128, D], F32)
        mpb = g.sb([128, D], BF16); mixT = g.sb([128, 8, 128], BF16)
        small = make_small(g)
        y_ps = [bufs['S_ps'][0], bufs['S_ps'][1]]
        tp = bufs['bc_ps'][1][:].bitcast(BF16)

        g.dma('sp', idf[:], P['ident'], [], ['idf'])
        g.cp('dve', idb[:], idf[:], ['idf'], ['idb'])
        g.dma('sp', idx_sb[:], P['own_idx'], [], ['idx'])
        g.dma('sp', gbc[:], P['ln2_g'].to_broadcast([128, D]), [], ['lng'])
        g.dma('sp', bbc[:], P['ln2_b'].to_broadcast([128, D]), [], ['lnb'])
        attn_consts(g, bufs, idb)
        for wsrc, wdst, key in [(P['w_branch_a'], wba, 'wba'), (P['w_branch_b'], wbb, 'wbb')]:
            for h in range(8):
                st = stage[h % 2]
                g.dma('sp', st[0:64, 0:D], wsrc[h * 64:(h + 1) * 64, :], [], [f"stg{h % 2}"])
                g.cp(g.cast_eng(), wdst[:, h, :], st[0:64, 0:D], [f"stg{h % 2}"], [key])
        load_weight_bf16(g, wout, P['w_out'], 8, D, [st[:] for st in stage], ['stg0', 'stg1'], 'wout_p')
        g.op('pe', lambda e: e.nop(), reads=[f"wout_p_{kc}_0" for kc in range(8)], writes=['wout'])

        for i in range(NQ + 1):
            samp = (i == NQ)
            ti = 1 if samp else 0
            if i == 0 or samp:
                g.dma('sp', table[:].rearrange("p a b c -> p (a b c)"), P['tband'][ti], [], ['table'])
            g.dma('sp', qbT[:].rearrange("p a b -> p (a b)"), P['q_qbT'][i], [], ['qbT'])
            g.dma('sp', oaT[:].rearrange("p a b -> p (a b)"), P['q_oaT'][i], [], ['oaT'])
            g.dma('sp', sga[:], P['q_sga'][i], [], ['sga'])
            g.dma('sp', sgb[:], P['q_sgb'][i], [], ['sgb'])
            g.op('pool', lambda e, i=i: e.indirect_dma_start(
                out=hown[:], out_offset=None, in_=P['h_all'],
                in_offset=bass.IndirectOffsetOnAxis(ap=idx_sb[:, i:i + 1], axis=0)),
                reads=['idx'], writes=['hown'], dma=True)
            if samp:
                kts = list(range(5)); base = 0
                kbs, vbs = P['s_KbT'], P['s_Vb']
            else:
                base = 2 * i - 4
                kts = [kt for kt in range(base, 2 * i + 2) if kt >= 0]
                kbs, vbs = P['p_KbT'], P['p_Vb']
            attn_core(g, bufs, kts, lambda kt: kbs[kt], lambda kt: vbs[kt], qbT, 'qbT', table,
                      lambda kt, base=base: kt - base, None, obT, 'obT')
            for (oT_, okey, w_, wkey, sg, sgk) in [(oaT, 'oaT', wba, 'wba', sga, 'sga'), (obT, 'obT', wbb, 'wbb', sgb, 'sgb')]:
                for hh in range(2):
                    for h in range(8):
                        g.mm(y_ps[hh][:], oT_[:, h, :], w_[:, h, hh * 512:(hh + 1) * 512], h == 0, h == 7,
                             [okey, wkey], [f"S{hh}"])
                    g.tt('dve', sg[:, hh * 512:(hh + 1) * 512], sg[:, hh * 512:(hh + 1) * 512], y_ps[hh][:], ALU.mult,
                         [sgk, f"S{hh}"], [sgk])
            g.tt('pool', mpb[:], sga[:], sgb[:], ALU.add, ['sga', 'sgb'], ['mpb'])
            for k in range(8):
                g.tr(tp[:, k * 128:(k + 1) * 128], mpb[:, k * 128:(k + 1) * 128], idb[:], ['mpb', 'idb'], ['bc1'])
            g.cp('act', mixT[:].rearrange("p a b -> p (a b)"), tp[:, 0:1024], ['bc1'], ['mixT'])
            for hh in range(2):
                for k in range(8):
                    g.mm(y_ps[hh][:], mixT[:, k, :], wout[:, k, hh * 512:(hh + 1) * 512], k == 0, k == 7,
                         ['mixT', 'wout'], [f"S{hh}"])
                hs = hown[:, hh * 512:(hh + 1) * 512]
                g.stt(hs, hs, ALPHA, y_ps[hh][:], ALU.mult, ALU.add, ['hown', f"S{hh}"], ['hown'])
            layer_norm_tile(g, hown[:], 'hown', gbc, bbc, small, 'ln2')
            g.dma('pool', P['h2_own'][i * 128:(i + 1) * 128, :], hown[:], ['hown'], ['hown'])
        g.es = es_outer
        g.emit()


def build_program(S, debug=None, nphases=5, bparts=('roll', 'cache', 'b2')):
    NT = S // 128
    NQ = S // 256
    NTA = NT + 1
    NQA = NQ + 1
    nc = bass.Bass("TRN2", target_bir_lowering=False)
    dbg = set(debug or [])

    def din(name, shape, dt=F32):
        return nc.dram_tensor(name, list(shape), dt, kind="ExternalInput").ap()

    def dout(name, shape, dt=F32):
        return nc.dram_tensor(name, list(shape), dt, kind="ExternalOutput").ap()

    def dscr(name, shape, dt=F32):
        if name in dbg:
            return dout(name, shape, dt)
        return nc.dram_tensor(name, list(shape), dt, kind="Internal").ap()

    P = dict(S=S, NT=NT, NQ=NQ, NB=N_BISECT, KSEL=min(TOPK, S // 4), bparts=set(bparts))
    P['x_all'] = din("x_all", [NTA * 128, D])
    P['ident'] = din("ident", [128, 128])
    P['own_idx'] = din("own_idx", [128, NQA], I32)
    for nm, shp in [("ffn1_wi", [D, 2 * FF]), ("ffn1_wo", [FF, D]), ("ffn2_wi", [D, 2 * FF]), ("ffn2_wo", [FF, D]),
                    ("w_in", [D, N_IN]), ("w_branch_a", [512, D]), ("w_branch_b", [512, D]), ("w_out", [D, D]),
                    ("ln1_g", [1, D]), ("ln1_b", [1, D]), ("ln2_g", [1, D]), ("ln2_b", [1, D]),
                    ("ln3_g", [1, D]), ("ln3_b", [1, D]),
                    ("c_ka", [PAST, 512]), ("c_va", [PAST, 512]), ("c_ki", [PAST, 64]),
                    ("c_kb", [BAND, 512]), ("c_vb", [BAND, 512]),
                    ("admneg", [2, 128, 256]), ("tdsa", [2, 128, 3 * 8 * 128]), ("tband", [2, 128, 6 * 8 * 128]),
                    ("c15", [1, 8]), ("pow2", [1, N_BISECT])]:
        P[nm] = din(nm, shp)
    P['h_all'] = dscr("h_all", [NTA * 128, D])
    for pre, n_a, n_b in [('p_', NT, NT), ('s_', 9, 5)]:
        P[pre + 'KaT'] = dscr(pre + "KaT", [n_a, 64, 1024], BF16)
        P[pre + 'Va'] = dscr(pre + "Va", [n_a, 128, 520], BF16)
        P[pre + 'kiT'] = dscr(pre + "kiT", [64, n_a * 128], BF16)
        P[pre + 'KbT'] = dscr(pre + "KbT", [n_b, 64, 1024], BF16)
        P[pre + 'Vb'] = dscr(pre + "Vb", [n_b, 128, 520], BF16)
    P['q_qaT'] = dscr("q_qaT", [NQA, 64, 1024], BF16)
    P['q_qiT'] = dscr("q_qiT", [NQA, 64, 512], BF16)
    P['q_wi'] = dscr("q_wi", [NQA, 128, 4])
    P['q_qbT'] = dscr("q_qbT", [NQA, 64, 1024], BF16)
    P['q_sga'] = dscr("q_sga", [NQA, 128, D])
    P['q_sgb'] = dscr("q_sgb", [NQA, 128, D])
    P['q_oaT'] = dscr("q_oaT", [NQA, 64, 1024], BF16)
    P['h2_own'] = dscr("h2_own", [NQA * 128, D])
    if 'dbg_score' in dbg:
        NMAX = max(NT * 128, 9 * 128)
        P['dbg_score'] = dscr("dbg_score", [NQA, 128, NMAX])
        P['dbg_junk'] = dscr("dbg_junk", [NQA, 128, NMAX], BF16)
    P['y_own'] = dout("y_own", [NQA * 128, D])
    P['kA_out'] = dout("kA_out", [S, 512]); P['vA_out'] = dout("vA_out", [S, 512])
    P['kidx_out'] = dout("kidx_out", [S, 64])
    P['kB_out'] = dout("kB_out", [512, 512]); P['vB_out'] = dout("vB_out", [512, 512])
    P['skA'] = dout("skA", [128, 512]); P['svA'] = dout("svA", [128, 512]); P['skidx'] = dout("skidx", [128, 64])
    P['skB'] = dout("skB", [512, 512]); P['svB'] = dout("svB", [512, 512])

    with ExitStack() as es:
        g = G(nc, es)
        ffn_phase(g, nc, es, P['x_all'], P['h_all'], P['ffn1_wi'], P['ffn1_wo'], P['ln1_g'], P['ln1_b'],
                  P['ident'], NTA)
        if nphases >= 2:
            phase_b(g, es, P)
        if nphases >= 3:
            phase_c1(g, es, P)
        if nphases >= 4:
            phase_c2(g, es, P)
        if nphases >= 5:
            ffn_phase(g, nc, es, P['h2_own'], P['y_own'], P['ffn2_wi'], P['ffn2_wo'], P['ln3_g'], P['ln3_b'],
                      P['ident'], NQA)
    return nc


def _t5_bucket_np(rel):
    import math
    rel = np.asarray(rel, np.int64)
    half, exact = 16, 8
    n = np.abs(rel)
    lr = np.log(np.maximum(n, 1).astype(np.float32) / np.float32(exact)) / np.float32(math.log(128 / exact))
    large = np.minimum(exact + (lr.astype(np.float32) * np.float32(half - exact)).astype(np.int32), half - 1)
    return (rel > 0).astype(np.int32) * half + np.where(n < exact, n, large)


def _qpos(r, i):
    p = np.arange(128)
    return np.where(p < 64, (4 * i + r) * 64 + p, (4 * i + 2 + r) * 64 + (p - 64))


def _tables(r, t5_bias, rel_bias):
    kk = np.arange(128)
    i = 2
    qp = _qpos(r, i)
    adm = np.zeros((2, 128, 256), np.float32)
    kc = (2 * i * 128 + np.arange(256)) // 64
    adm[0] = np.where(kc[None, :] <= (qp // 64)[:, None], 0.0, NEG)
    tdsa = np.zeros((2, 128, 3, 8, 128), np.float32)
    for j in range(3):
        kpos = (2 * i - 1 + j) * 128 + kk
        bk = _t5_bucket_np(kpos[:, None] - qp[None, :])
        tdsa[0, :, j] = np.transpose(t5_bias[:, bk], (1, 0, 2))
    tband = np.full((2, 128, 6, 8, 128), MASKNEG, np.float32)
    for j in range(6):
        kpos = (2 * i - 4 + j) * 128 + kk
        rel = kpos[:, None] - qp[None, :]
        vis = ((kpos // 64)[:, None] <= (qp // 64)[None, :]) & ((kpos // 64)[:, None] >= (qp // 64)[None, :] - 8)
        ridx = np.clip(rel, -128, 63) + 128
        vals = np.transpose(rel_bias[:, ridx], (1, 0, 2))
        tband[0, :, j] = np.where(vis[:, None, :], vals, MASKNEG)
    qs = PAST + np.minimum(np.arange(128), 63)
    cols = 7 * 128 + np.arange(256)
    adm[1] = np.where(cols[None, :] < PAST + 64, 0.0, NEG) + np.zeros((128, 1), np.float32)
    for j in range(3):
        kpos = (6 + j) * 128 + kk
        bk = _t5_bucket_np(kpos[:, None] - qs[None, :])
        tdsa[1, :, j] = np.transpose(t5_bias[:, bk], (1, 0, 2))
    for j in range(5):
        kpos = (PAST - BAND) + j * 128 + kk
        rel = kpos[:, None] - qs[None, :]
        vis = (kpos < PAST + 64)[:, None] & np.ones((1, 128), bool)
        ridx = np.clip(rel, -128, 63) + 128
        vals = np.transpose(rel_bias[:, ridx], (1, 0, 2))
        tband[1, :, j] = np.where(vis[:, None, :], vals, MASKNEG)
    return adm, tdsa.reshape(2, 128, -1), tband.reshape(2, 128, -1)


_PROG = {}


def _prep_inputs(inputs, S, n_cores=8):
    NT, NQ = S // 128, S // 256
    f = lambda a: np.ascontiguousarray(a, dtype=np.float32)
    common = {
        "ident": np.eye(128, dtype=np.float32),
        "ffn1_wi": f(inputs['ffn1_wi'][0]), "ffn1_wo": f(inputs['ffn1_wo'][0]),
        "ffn2_wi": f(inputs['ffn2_wi'][0]), "ffn2_wo": f(inputs['ffn2_wo'][0]),
        "w_in": f(inputs['w_in'][0]), "w_branch_a": f(inputs['w_branch_a'][0]),
        "w_branch_b": f(inputs['w_branch_b'][0]), "w_out": f(inputs['w_out'][0]),
        "ln1_g": f(inputs['ln1_g']), "ln1_b": f(inputs['ln1_b']), "ln2_g": f(inputs['ln2_g']),
        "ln2_b": f(inputs['ln2_b']), "ln3_g": f(inputs['ln3_g']), "ln3_b": f(inputs['ln3_b']),
        "c15": f(np.asarray(inputs['t5_bias'])[:, 15][None, :]),
        "pow2": (0.5 ** np.arange(1, N_BISECT + 1, dtype=np.float64)).astype(np.float32)[None, :],
    }
    tabs = [_tables(r, np.asarray(inputs['t5_bias'], np.float32), np.asarray(inputs['rel_bias_b'][0], np.float32))
            for r in range(2)]
    maps = []
    for c in range(n_cores):
        b, r = c // 2, c % 2
        xs = np.zeros((128, D), np.float32)
        xs[:64] = inputs['x_sample'][c]
        own = np.zeros((128, NQ + 1), np.int32)
        for i in range(NQ):
            own[:, i] = _qpos(r, i)
        own[:, NQ] = S + np.arange(128)
        m = dict(common)
        m.update({
            "x_all": np.concatenate([f(inputs['x_prompt'][b, :S]), xs], 0),
            "own_idx": own,
            "c_ka": f(inputs['cache_k_a'][0, c]).reshape(PAST, 512),
            "c_va": f(inputs['cache_v_a'][0, c]).reshape(PAST, 512),
            "c_ki": f(inputs['cache_kidx_a'][0, c]),
            "c_kb": f(inputs['cache_k_b'][0, c]).reshape(BAND, 512),
            "c_vb": f(inputs['cache_v_b'][0, c]).reshape(BAND, 512),
            "admneg": tabs[r][0], "tdsa": tabs[r][1], "tband": tabs[r][2],
        })
        maps.append(m)
    return maps


def _assemble(results, S, B=4):
    NQ = S // 256
    keep = min(BAND, S)
    y_prompt = np.zeros((B, S, D), np.float32)
    y_sample = np.zeros((8, 64, D), np.float32)
    nka = np.zeros((1, B, S, 8, 64), np.float32); nva = np.zeros_like(nka)
    nki = np.zeros((1, B, S, 64), np.float32)
    nkb = np.zeros((1, B, keep, 8, 64), np.float32); nvb = np.zeros_like(nkb)
    ska = np.zeros((1, 8, 64, 8, 64), np.float32); sva = np.zeros_like(ska)
    ski = np.zeros((1, 8, 64, 64), np.float32)
    skb = np.zeros((1, 8, BAND, 8, 64), np.float32); svb = np.zeros_like(skb)
    for c, res in enumerate(results):
        b, r = c // 2, c % 2
        yo = res['y_own']
        for i in range(NQ):
            qp = _qpos(r, i)
            y_prompt[b, qp] = yo[i * 128:(i + 1) * 128]
        y_sample[c] = yo[NQ * 128:NQ * 128 + 64]
        if r == 0:
            nka[0, b] = res['kA_out'].reshape(S, 8, 64)
            nva[0, b] = res['vA_out'].reshape(S, 8, 64)
            nki[0, b] = res['kidx_out']
            nkb[0, b] = res['kB_out'].reshape(BAND, 8, 64)[-keep:]
            nvb[0, b] = res['vB_out'].reshape(BAND, 8, 64)[-keep:]
        ska[0, c] = res['skA'][:64].reshape(64, 8, 64)
        sva[0, c] = res['svA'][:64].reshape(64, 8, 64)
        ski[0, c] = res['skidx'][:64]
        skb[0, c] = res['skB'].reshape(BAND, 8, 64)
        svb[0, c] = res['svB'].reshape(BAND, 8, 64)
    return (y_prompt, y_sample, nka, nva, nki, nkb, nvb, ska, sva, ski, skb, svb)


def kernel(**inputs):
    S = int(np.asarray(inputs['x_prompt']).shape[1])
    if S not in _PROG:
        _PROG[S] = build_program(S)
    nc = _PROG[S]
    maps = _prep_inputs(inputs, S)
    res = run_bass_kernel_spmd(nc, maps, core_ids=list(range(8)))
    return _assemble(res.results, S)
```

```python
import numpy as np
import concourse.bass as bass
import concourse.mybir as mybir
from concourse.bass_utils import run_bass_kernel_spmd
from contextlib import ExitStack

F32 = mybir.dt.float32
BF16 = mybir.dt.bfloat16
I32 = mybir.dt.int32
AF = mybir.ActivationFunctionType
ALU = mybir.AluOpType
AX = mybir.AxisListType
ENGS = ['pe', 'act', 'dve', 'pool', 'sp']

D = 1024
FF = 2816
NFF = 22
HD = 64
NH = 8
CHUNK = 64
PAST = 1024
BAND = 512
TOPK = 256
LN_EPS = 1e-5
ALPHA = 2.0 ** 0.25
NEG = -1e30
MASKNEG = -30000.0
N_BISECT = 22
C_QA, C_KA, C_VA, C_QI, C_KI, C_WI, C_QB, C_KB, C_VB, C_GA, C_GB = (
    0, 512, 1024, 1536, 1792, 1856, 1860, 2372, 2884, 3396, 4420)
N_IN = 5444


class G:
    NDS = {'sp': 40, 'pool': 24, 'act': 4, 'pe': 0, 'dve': 0}

    def __init__(self, nc, es):
        self.nc = nc
        self.es = es
        self.ops = {e: [] for e in ENGS}
        self.last_w = {}
        self.readers = {}
        self.nsb = 0
        self.sems = {}
        for e in ENGS:
            self.sems[('c', e)] = es.enter_context(nc.semaphore(f"c_{e}"))
            for j in range(self.NDS[e]):
                self.sems[('d', e, j)] = es.enter_context(nc.semaphore(f"d_{e}_{j}"))
        self.ccount = {e: 0 for e in ENGS}
        self.dnum = {e: 0 for e in ENGS}
        self.rr = 0

    def sb(self, shape, dt, name=None):
        self.nsb += 1
        return self.es.enter_context(self.nc.sbuf_tensor(name or f"sb{self.nsb}", list(shape), dt))

    def ps(self, shape, dt=F32, name=None):
        self.nsb += 1
        return self.es.enter_context(self.nc.psum_tensor(name or f"ps{self.nsb}", list(shape), dt))

    def op(self, eng, fn, reads=(), writes=(), dma=False):
        idx = len(self.ops[eng])
        deps = set()
        for k in reads:
            w = self.last_w.get(k)
            if w is not None:
                deps.add(w)
        for k in writes:
            w = self.last_w.get(k)
            if w is not None:
                deps.add(w)
            for r in self.readers.get(k, ()):
                deps.add(r)
        deps.discard((eng, idx))
        for k in writes:
            self.last_w[k] = (eng, idx)
            self.readers[k] = []
        for k in reads:
            self.readers.setdefault(k, []).append((eng, idx))
        self.ops[eng].append(dict(fn=fn, deps=deps, dma=dma, signal=False, hval=None, pre=None))
        return (eng, idx)

    def emit(self):
        nc = self.nc
        sems = self.sems
        ops = self.ops
        for e in ENGS:
            for o in ops[e]:
                for (de, di) in o['deps']:
                    d = ops[de][di]
                    if de == 'pe' and e == 'pe' and not d['dma']:
                        continue
                    d['signal'] = True
            for o in reversed(ops[e]):
                if not o['dma']:
                    o['signal'] = True
                    break
        for e in ENGS:
            for o in ops[e]:
                if o['dma']:
                    m = self.dnum[e]
                    self.dnum[e] += 1
                    j, rnd = m % self.NDS[e], m // self.NDS[e]
                    o['hval'] = (('d', e, j), 16 * (rnd + 1))
                    o['pre'] = (('d', e, j), 16 * rnd) if rnd > 0 else None
                elif o['signal']:
                    self.ccount[e] += 1
                    o['hval'] = (('c', e), self.ccount[e])
        ccount = dict(self.ccount)
        dfinal = {}
        for e in ENGS:
            for j in range(self.NDS[e]):
                n_j = (self.dnum[e] - j + self.NDS[e] - 1) // self.NDS[e] if self.dnum[e] > j else 0
                if n_j > 0:
                    dfinal[('d', e, j)] = 16 * n_j
        with nc.Block() as block:
            def mk(e):
                def body(engine):
                    seen = {}

                    def wait(sk, v):
                        if v <= 0 or seen.get(sk, 0) >= v:
                            return
                        engine.wait_ge(sems[sk], v)
                        seen[sk] = v
                    for o in ops[e]:
                        for (de, di) in sorted(o['deps']):
                            d = ops[de][di]
                            if de == 'pe' and e == 'pe' and not d['dma']:
                                continue
                            wait(*d['hval'])
                        if o['pre'] is not None:
                            wait(*o['pre'])
                        ins = o['fn'](engine)
                        if o['dma']:
                            ins.then_inc(sems[o['hval'][0]], 16)
                        elif o['signal']:
                            ins.then_inc(sems[o['hval'][0]], 1)
                    for e2 in ENGS:
                        if ccount[e2] > 0:
                            wait(('c', e2), ccount[e2])
                    for sk, v in dfinal.items():
                        wait(sk, v)
                return body
            block.tensor(mk('pe'))
            block.scalar(mk('act'))
            block.vector(mk('dve'))
            block.gpsimd(mk('pool'))
            block.sync(mk('sp'))
        self.ops = {e: [] for e in ENGS}
        self.last_w = {}
        self.readers = {}

    def dma(self, q, out, in_, r, w):
        return self.op(q, lambda e: e.dma_start(out=out, in_=in_), reads=r, writes=w, dma=True)

    def mm(self, out, lhsT, rhs, start, stop, r, w, skip=False):
        return self.op('pe', lambda e: e.matmul(out=out, lhsT=lhsT, rhs=rhs, start=start, stop=stop,
                                                skip_group_check=skip), reads=r, writes=w)

    def tr(self, out, in_, ident, r, w):
        return self.op('pe', lambda e: e.transpose(out=out, in_=in_, identity=ident), reads=r, writes=w)

    def act(self, out, in_, func, r, w, scale=None, bias=None, accum_out=None):
        kw = {}
        if scale is not None:
            kw['scale'] = scale
        if bias is not None:
            kw['bias'] = bias
        if accum_out is not None:
            kw['accum_out'] = accum_out
        return self.op('act', lambda e: e.activation(out=out, in_=in_, func=func, **kw), reads=r, writes=w)

    def ts(self, eng, out, in0, s1, s2, op0, op1, r, w, accum_out=None):
        kw = {}
        if op1 is not None:
            kw['op1'] = op1
        if accum_out is not None:
            kw['accum_out'] = accum_out
        return self.op(eng, lambda e: e.tensor_scalar(out=out, in0=in0, scalar1=s1, scalar2=s2, op0=op0, **kw),
                       reads=r, writes=w)

    def tt(self, eng, out, in0, in1, op, r, w):
        return self.op(eng, lambda e: e.tensor_tensor(out=out, in0=in0, in1=in1, op=op), reads=r, writes=w)

    def stt(self, out, in0, scalar, in1, op0, op1, r, w):
        return self.op('dve', lambda e: e.scalar_tensor_tensor(out=out, in0=in0, scalar=scalar, in1=in1,
                                                               op0=op0, op1=op1), reads=r, writes=w)

    def cp(self, eng, out, in_, r, w):
        if eng == 'act':
            return self.op('act', lambda e: e.activation(out=out, in_=in_, func=AF.Copy), reads=r, writes=w)
        return self.op(eng, lambda e: e.tensor_copy(out=out, in_=in_), reads=r, writes=w)

    def cast_eng(self):
        self.rr += 1
        return ['dve', 'pool', 'act'][self.rr % 3]


def load_weight_bf16(g, dst, src, kchunks, ncols, stage, stage_keys, dst_key, piece=2048):
    i = 0
    for kc in range(kchunks):
        for c0 in range(0, ncols, piece):
            n = min(piece, ncols - c0)
            st, sk = stage[i % len(stage)], stage_keys[i % len(stage)]
            i += 1
            g.dma('sp', st[:, 0:n], src[kc * 128:(kc + 1) * 128, c0:c0 + n], [], [sk])
            g.cp(g.cast_eng(), dst[:, kc, c0:c0 + n], st[:, 0:n], [sk], [dst_key + f"_{kc}_{c0}"])


def rsqrt_newton(g, var_ap, rs, small, rkeys, tag):
    v, ti, ui, b, t = small['v'], small['ti'], small['ui'], small['b'], small['t']
    g.ts('dve', v[:], var_ap, LN_EPS, None, ALU.add, None, rkeys, [tag + 'v'])
    g.ts('dve', ti[:], v[:].bitcast(I32), 1, None, ALU.arith_shift_right, None, [tag + 'v'], [tag + 'ti'])
    g.ts('dve', ui[:], ti[:], -1.0, 1597463007.0, ALU.mult, ALU.add, [tag + 'ti'], [tag + 'y'])
    y = ui[:].bitcast(F32)
    for it in range(3):
        g.stt(b[:], y, v[:], y, ALU.mult, ALU.mult, [tag + 'y', tag + 'v'], [tag + 'b'])
        g.stt(t[:], b[:], -0.5, y, ALU.mult, ALU.mult, [tag + 'b', tag + 'y'], [tag + 't'])
        dst = rs[:] if it == 2 else y
        g.stt(dst, y, 1.5, t[:], ALU.mult, ALU.add, [tag + 'y', tag + 't'], [tag + ('rs' if it == 2 else 'y')])


def make_small(g):
    return dict(st=g.sb([128, 2, 6], F32), mv=g.sb([128, 2], F32), rs=g.sb([128, 1], F32),
                nb=g.sb([128, 1], F32), v=g.sb([128, 1], F32), ti=g.sb([128, 1], I32), ui=g.sb([128, 1], I32),
                b=g.sb([128, 1], F32), t=g.sb([128, 1], F32))


def layer_norm_tile(g, y, ykey, gbc, bbc, small, tag):
    st, mv, rs, nb = small['st'], small['mv'], small['rs'], small['nb']
    for hh in range(2):
        g.op('dve', lambda e, hh=hh: e.bn_stats(out=st[:, hh, :], in_=y[:, hh * 512:(hh + 1) * 512]),
             reads=[ykey], writes=[tag + 'st'])
    g.op('dve', lambda e: e.bn_aggr(out=mv[:], in_=st[:].rearrange('p a b -> p (a b)')), reads=[tag + 'st'], writes=[tag + 'mv'])
    rsqrt_newton(g, mv[:, 1:2], rs, small, [tag + 'mv'], tag)
    g.stt(nb[:], mv[:, 0:1], -1.0, rs[:], ALU.mult, ALU.mult, [tag + 'mv', tag + 'rs'], [tag + 'nb'])
    g.act(y, y, AF.Identity, [ykey, tag + 'rs', tag + 'nb'], [ykey], scale=rs[:], bias=nb[:])
    g.tt('dve', y, y, gbc[:], ALU.mult, [ykey, 'lng'], [ykey])
    g.tt('pool', y, y, bbc[:], ALU.add, [ykey, 'lnb'], [ykey])


def ffn_phase(g, nc, es_outer, src, dst, wi, wo, lng, lnb, ident_d, ntiles, T=4):
    TN = T * 128
    with ExitStack() as es:
        g.es = es
        wi_b = g.sb([128, 8, 2 * FF], BF16)
        wo_b = g.sb([128, NFF, D], BF16)
        xin = g.sb([128, T, D], F32)
        xb = g.sb([128, D], BF16)
        xT = g.sb([128, 8, TN], BF16)
        gT = g.sb([128, NFF, TN], BF16)
        sil = g.sb([128, TN], F32)
        gbc = g.sb([128, D], F32)
        bbc = g.sb([128, D], F32)
        idf = g.sb([128, 128], F32)
        idb = g.sb([128, 128], BF16)
        small = [make_small(g) for _ in range(2)]
        up = [g.ps([128, 512], F32) for _ in range(4)]
        dn = [g.ps([128, 512], F32) for _ in range(2)]
        tp = g.ps([128, 8, 128], BF16)

        g.dma('sp', idf[:], ident_d, [], ['idf'])
        g.cp('dve', idb[:], idf[:], ['idf'], ['idb'])
        g.dma('sp', gbc[:], lng.to_broadcast([128, D]), [], ['lng'])
        g.dma('sp', bbc[:], lnb.to_broadcast([128, D]), [], ['lnb'])
        xflat = xin[:].rearrange("p a b -> p (a b)")
        nst_ = max(2, T // 2)
        stages = [xflat[:, q * 2048:(q + 1) * 2048] for q in range(T // 2)] if T >= 4 else [xflat[:, 0:2048]]
        skeys = [f"xin_t{2 * q}" for q in range(len(stages))]
        load_weight_bf16(g, wi_b, wi, 8, 2 * FF, stages, skeys, 'wi')
        load_weight_bf16(g, wo_b, wo, NFF, D, stages, skeys, 'wo')
        wi_keys = [f"wi_{kc}_{c0}" for kc in range(8) for c0 in range(0, 2 * FF, 2048)]
        wo_keys = [f"wo_{kc}_0" for kc in range(NFF)]
        g.op('pool', lambda e: e.memset(sil[:, 0:1], 0.0), reads=wi_keys + wo_keys,
             writes=[f"xin_t{j}" for j in range(T)] + ['sil'])

        nst = (ntiles + T - 1) // T
        for st_i in range(nst):
            t0 = st_i * T
            Tc = min(T, ntiles - t0)
            Tn = Tc * 128
            for j in range(Tc):
                xk = f"xin_t{j}"
                g.dma('sp', xin[:, j, :], src[(t0 + j) * 128:(t0 + j + 1) * 128, :], [], [xk])
                if T >= 4 and j % 2 == 0 and j + 1 < T:
                    pass
            for j in range(Tc):
                xk = f"xin_t{j}"
                g.cp('pool', xb[:], xin[:, j, :], [xk], ['xb'])
                g.act(xin[:, j, :], xin[:, j, :], AF.Copy, [xk, 'xb'], [xk], scale=ALPHA)
                for k in range(8):
                    g.tr(tp[:, k, :], xb[:, k * 128:(k + 1) * 128], idb[:], ['xb', 'idb'], ['tp'])
                g.cp('dve', xT[:, :, j * 128:(j + 1) * 128], tp[:], ['tp'], [f"xT{j}"])
            xTk = [f"xT{j}" for j in range(Tc)]
            for c in range(NFF):
                pa, pu = up[2 * (c % 2)], up[2 * (c % 2) + 1]
                ka, ku = f"up{2 * (c % 2)}", f"up{2 * (c % 2) + 1}"
                for k in range(8):
                    g.mm(pa[:, 0:Tn], wi_b[:, k, c * 128:(c + 1) * 128], xT[:, k, 0:Tn], k == 0, k == 7,
                         xTk + (wi_keys if st_i == 0 else []), [ka])
                for k in range(8):
                    g.mm(pu[:, 0:Tn], wi_b[:, k, FF + c * 128:FF + (c + 1) * 128], xT[:, k, 0:Tn],
                         k == 0, k == 7, xTk, [ku])
                g.act(sil[:, 0:Tn], pa[:, 0:Tn], AF.Tanh, [ka], ['sil'], scale=0.5)
                g.stt(sil[:, 0:Tn], sil[:, 0:Tn], 1.0, pa[:, 0:Tn], ALU.add, ALU.mult, ['sil', ka], ['sil'])
                g.stt(gT[:, c, 0:Tn], sil[:, 0:Tn], 0.5, pu[:, 0:Tn], ALU.mult, ALU.mult, ['sil', ku], [f"gT{c}"])
            gk = [f"gT{c}" for c in range(NFF)]
            for j in range(Tc):
                yk = f"xin_t{j}"
                for hh in range(2):
                    for c in range(NFF):
                        g.mm(dn[hh][:], gT[:, c, j * 128:(j + 1) * 128], wo_b[:, c, hh * 512:(hh + 1) * 512],
                             c == 0, c == NFF - 1, gk + (wo_keys if st_i == 0 else []), [f"dn{hh}"])
                    ysl = xin[:, j, hh * 512:(hh + 1) * 512]
                    g.stt(ysl, dn[hh][:], 0.5, ysl, ALU.mult, ALU.add, [f"dn{hh}", yk], [yk])
                layer_norm_tile(g, xin[:, j, :], yk, gbc, bbc, small[j % 2], f"ln{j % 2}")
                g.dma('pool', dst[(t0 + j) * 128:(t0 + j + 1) * 128, :], xin[:, j, :], [yk], [yk])
        g.es = es_outer
        g.emit()


def proj_tile(g, b, bufs, w_b, groups):
    hf, hb, hT, tp, idb = bufs['hf'][b], bufs['hb'][b], bufs['hT'][b], bufs['tp'], bufs['idb']
    g.cp('pool', hb[:], hf[:], [f"hf{b}"], [f"hb{b}"])
    for k in range(8):
        g.tr(tp[:, k * 128:(k + 1) * 128], hb[:, k * 128:(k + 1) * 128], idb[:], [f"hb{b}", 'idb'], ['tp'])
    g.cp('dve', hT[:].rearrange("p a b -> p (a b)"), tp[:, 0:1024], ['tp'], [f"hT{b}"])
    for (pap, key, c0, n) in groups:
        for k in range(8):
            g.mm(pap, hT[:, k, :], w_b[:, k, c0:c0 + n], k == 0, k == 7, [f"hT{b}", 'w_in'], [key])


def headT(g, src_bf, src_key, nheads, tp, idb, dstT, dst_key):
    for h in range(nheads):
        g.tr(tp[0:64, h * 128:(h + 1) * 128], src_bf[:, h * 64:(h + 1) * 64], idb[:], [src_key, 'idb'], ['tp'])
    g.cp('act', dstT[:].rearrange("p a b -> p (a b)"), tp[0:64, 0:nheads * 128], ['tp'], [dst_key])


def phase_b(g, es_outer, P):
    S, NT, NQ = P['S'], P['NT'], P['NQ']
    with ExitStack() as es:
        g.es = es
        w_b = g.sb([128, 8, N_IN], BF16)
        stage = [g.sb([128, 2048], F32) for _ in range(2)]
        idf = g.sb([128, 128], F32)
        idb = g.sb([128, 128], BF16)
        idx_sb = g.sb([128, NQ + 1], I32)
        bufs = dict(hf=[g.sb([128, D], F32) for _ in range(2)], hb=[g.sb([128, D], BF16) for _ in range(2)],
                    hT=[g.sb([128, 8, 128], BF16) for _ in range(2)], idb=idb)
        kaf = g.sb([128, 512], F32); vaf = g.sb([128, 512], F32); kif = g.sb([128, 64], F32)
        kbf = g.sb([128, 512], F32); vbf = g.sb([128, 512], F32)
        kab = g.sb([128, 512], BF16); kib = g.sb([128, 64], BF16); kbb = g.sb([128, 512], BF16)
        vaug = g.sb([128, 8, 65], BF16); vbug = g.sb([128, 8, 65], BF16)
        kaT = g.sb([64, 8, 128], BF16); kbT = g.sb([64, 8, 128], BF16); kiT = g.sb([64, 1, 128], BF16)
        qab = g.sb([128, 512], BF16); qib = g.sb([128, 256], BF16); qbb = g.sb([128, 512], BF16)
        qaT = g.sb([64, 8, 128], BF16); qiT = g.sb([64, 4, 128], BF16); qbT = g.sb([64, 8, 128], BF16)
        wif = g.sb([128, 4], F32)
        sga = g.sb([128, D], F32); sgb = g.sb([128, D], F32)
        pb = [g.ps([128, 512], F32) for _ in range(7)]
        tpf = g.ps([128, 512], F32)
        tp = tpf[:].bitcast(BF16)
        bufs['tp'] = tp

        g.dma('sp', idf[:], P['ident'], [], ['idf'])
        g.cp('dve', idb[:], idf[:], ['idf'], ['idb'])
        g.dma('sp', idx_sb[:], P['own_idx'], [], ['idx'])
        load_weight_bf16(g, w_b, P['w_in'], 8, N_IN, [st[:] for st in stage], ['stg0', 'stg1'], 'w_in_p')
        wkeys = [f"w_in_p_{kc}_{c0}" for kc in range(8) for c0 in range(0, N_IN, 2048)]
        g.op('pe', lambda e: e.nop(), reads=wkeys, writes=['w_in'])
        g.op('pool', lambda e: e.memset(vaug[:], 1.0), [], ['vaug'])
        g.op('pool', lambda e: e.memset(vbug[:], 1.0), [], ['vbug'])

        def kside_ingest(kind, t, src):
            pre = kind + '_'
            import os
            KS = os.environ.get('KS', 'ka,va,ki,kb,vb,samp')
            if kind == 's' and 'samp' not in KS:
                return
            if src.get('ka') is not None:
                ap, key = src['ka']
                g.cp('act', kaf[:], ap, [key], ['kaf'])
                g.cp('dve', kab[:], kaf[:], ['kaf'], ['kab'])
                headT(g, kab, 'kab', 8, tp, idb, kaT, 'kaT')
                g.dma('sp', P[pre + 'KaT'][t], kaT[:].rearrange("p a b -> p (a b)"), ['kaT'], ['kaT'])
                ap, key = src['va']
                g.cp('act', vaf[:], ap, [key], ['vaf'])
                if 'va' in KS:
                    g.cp('dve', vaug[:, :, 0:64], vaf[:].rearrange("p (h d) -> p h d", h=8), ['vaf'], ['vaug'])
                    g.dma('sp', P[pre + 'Va'][t], vaug[:].rearrange("p a b -> p (a b)"), ['vaug'], ['vaug'])
                ap, key = src['ki']
                g.cp('act', kif[:], ap, [key], ['kif'])
                if 'ki' in KS:
                    g.cp('dve', kib[:], kif[:], ['kif'], ['kib'])
                    g.tr(tp[0:64, 0:128], kib[:], idb[:], ['kib', 'idb'], ['tp'])
                    g.cp('act', kiT[:, 0, :], tp[0:64, 0:128], ['tp'], ['kiT'])
                    g.dma('sp', P[pre + 'kiT'][:, t * 128:(t + 1) * 128], kiT[:, 0, :], ['kiT'], ['kiT'])
            if src.get('kb') is not None and 'kb' in KS:
                tb = src['tb']
                ap, key = src['kb']
                g.cp('act', kbf[:], ap, [key], ['kbf'])
                g.cp('dve', kbb[:], kbf[:], ['kbf'], ['kbb'])
                headT(g, kbb, 'kbb', 8, tp, idb, kbT, 'kbT')
                g.dma('sp', P[pre + 'KbT'][tb], kbT[:].rearrange("p a b -> p (a b)"), ['kbT'], ['kbT'])
                ap, key = src['vb']
                g.cp('act', vbf[:], ap, [key], ['vbf'])
                g.cp('dve', vbug[:, :, 0:64], vbf[:].rearrange("p (h d) -> p h d", h=8), ['vbf'], ['vbug'])
                g.dma('sp', P[pre + 'Vb'][tb], vbug[:].rearrange("p a b -> p (a b)"), ['vbug'], ['vbug'])

        def ld(t):
            g.dma('sp', bufs['hf'][t % 2][:], P['h_all'][t * 128:(t + 1) * 128, :], [], [f"hf{t % 2}"])
        ld(0)
        for t in range(NT + 1):
            groups = [(pb[0][:], 'pb0', C_KA, 512), (pb[1][:], 'pb1', C_VA, 512), (pb[2][:, 0:64], 'pb2', C_KI, 64),
                      (pb[3][:], 'pb3', C_KB, 512), (pb[4][:], 'pb4', C_VB, 512)]
            proj_tile(g, t % 2, bufs, w_b, groups)
            if t + 1 <= NT:
                ld(t + 1)
            src = dict(ka=(pb[0][:], 'pb0'), va=(pb[1][:], 'pb1'), ki=(pb[2][:, 0:64], 'pb2'),
                       kb=(pb[3][:], 'pb3'), vb=(pb[4][:], 'pb4'))
            import os
            B1 = os.environ.get('B1', 'ingest,out')
            if 'ingest' not in B1:
                for kk_ in ['pb0', 'pb1', 'pb2', 'pb3', 'pb4']:
                    g.cp('act', kaf[:, 0:64], pb[int(kk_[2])][:, 0:64], [kk_], ['kaf'])
                continue
            if t < NT:
                src['tb'] = t
                kside_ingest('p', t, src)
                if 'out' not in B1:
                    continue
                g.dma('pool', P['kA_out'][t * 128:(t + 1) * 128, :], kaf[:], ['kaf'], ['kaf'])
                g.dma('pool', P['vA_out'][t * 128:(t + 1) * 128, :], vaf[:], ['vaf'], ['vaf'])
                g.dma('pool', P['kidx_out'][t * 128:(t + 1) * 128, :], kif[:], ['kif'], ['kif'])
                if t >= NT - 4:
                    o = (t - (NT - 4)) * 128
                    g.dma('pool', P['kB_out'][o:o + 128, :], kbf[:], ['kbf'], ['kbf'])
                    g.dma('pool', P['vB_out'][o:o + 128, :], vbf[:], ['vbf'], ['vbf'])
            else:
                src['tb'] = 4
                kside_ingest('s', 8, src)
                g.dma('pool', P['skA'], kaf[:], ['kaf'], ['kaf'])
                g.dma('pool', P['svA'], vaf[:], ['vaf'], ['vaf'])
                g.dma('pool', P['skidx'], kif[:], ['kif'], ['kif'])
                g.dma('pool', P['skB'][448:512, :], kbf[0:64, :], ['kbf'], ['kbf'])
                g.dma('pool', P['svB'][448:512, :], vbf[0:64, :], ['vbf'], ['vbf'])
        if 'roll' in P['bparts']:
            g.dma('pool', P['skB'][0:448, :], P['c_kb'][64:512, :], [], ['skB_roll'])
            g.dma('pool', P['svB'][0:448, :], P['c_vb'][64:512, :], [], ['svB_roll'])
        for t in range(8 if 'cache' in P['bparts'] else 0):
            g.dma('sp', kaf[:], P['c_ka'][t * 128:(t + 1) * 128, :], [], ['kaf'])
            g.dma('sp', vaf[:], P['c_va'][t * 128:(t + 1) * 128, :], [], ['vaf'])
            g.dma('sp', kif[:], P['c_ki'][t * 128:(t + 1) * 128, :], [], ['kif'])
            g.cp('dve', kab[:], kaf[:], ['kaf'], ['kab'])
            headT(g, kab, 'kab', 8, tp, idb, kaT, 'kaT')
            g.dma('sp', P['s_KaT'][t], kaT[:].rearrange("p a b -> p (a b)"), ['kaT'], ['kaT'])
            g.cp('pool', vaug[:, :, 0:64], vaf[:].rearrange("p (h d) -> p h d", h=8), ['vaf'], ['vaug'])
            g.dma('sp', P['s_Va'][t], vaug[:].rearrange("p a b -> p (a b)"), ['vaug'], ['vaug'])
            g.cp('dve', kib[:], kif[:], ['kif'], ['kib'])
            g.tr(tp[0:64, 0:128], kib[:], idb[:], ['kib', 'idb'], ['tp'])
            g.cp('act', kiT[:, 0, :], tp[0:64, 0:128], ['tp'], ['kiT'])
            g.dma('sp', P['s_kiT'][:, t * 128:(t + 1) * 128], kiT[:, 0, :], ['kiT'], ['kiT'])
        for t in range(4 if 'cache' in P['bparts'] else 0):
            g.dma('sp', kbf[:], P['c_kb'][t * 128:(t + 1) * 128, :], [], ['kbf'])
            g.dma('sp', vbf[:], P['c_vb'][t * 128:(t + 1) * 128, :], [], ['vbf'])
            g.cp('dve', kbb[:], kbf[:], ['kbf'], ['kbb'])
            headT(g, kbb, 'kbb', 8, tp, idb, kbT, 'kbT')
            g.dma('sp', P['s_KbT'][t], kbT[:].rearrange("p a b -> p (a b)"), ['kbT'], ['kbT'])
            g.cp('pool', vbug[:, :, 0:64], vbf[:].rearrange("p (h d) -> p h d", h=8), ['vbf'], ['vbug'])
            g.dma('sp', P['s_Vb'][t], vbug[:].rearrange("p a b -> p (a b)"), ['vbug'], ['vbug'])

        def ldq(i):
            hf = bufs['hf'][i % 2]
            g.op('pool', lambda e: e.indirect_dma_start(
                out=hf[:], out_offset=None, in_=P['h_all'],
                in_offset=bass.IndirectOffsetOnAxis(ap=idx_sb[:, i:i + 1], axis=0)),
                reads=['idx'], writes=[f"hf{i % 2}"], dma=True)
        nb2 = NQ + 1 if 'b2' in P['bparts'] else 0
        if nb2:
            ldq(0)
        for i in range(nb2):
            groups = [(pb[0][:], 'pb0', C_QA, 512), (pb[1][:, 0:256], 'pb1', C_QI, 256),
                      (pb[1][:, 256:260], 'pb1', C_WI, 4), (pb[2][:], 'pb2', C_QB, 512),
                      (pb[3][:], 'pb3', C_GA, 512), (pb[4][:], 'pb4', C_GA + 512, 512),
                      (pb[5][:], 'pb5', C_GB, 512), (pb[6][:], 'pb6', C_GB + 512, 512)]
            proj_tile(g, i % 2, bufs, w_b, groups)
            if i + 1 < nb2:
                ldq(i + 1)
            g.cp('dve', qab[:], pb[0][:], ['pb0'], ['qab'])
            headT(g, qab, 'qab', 8, tp, idb, qaT, 'qaT')
            g.dma('sp', P['q_qaT'][i], qaT[:].rearrange("p a b -> p (a b)"), ['qaT'], ['qaT'])
            g.cp('dve', qib[:], pb[1][:, 0:256], ['pb1'], ['qib'])
            g.cp('dve', wif[:], pb[1][:, 256:260], ['pb1'], ['wif'])
            headT(g, qib, 'qib', 4, tp, idb, qiT, 'qiT')
            g.dma('sp', P['q_qiT'][i], qiT[:].rearrange("p a b -> p (a b)"), ['qiT'], ['qiT'])
            g.dma('sp', P['q_wi'][i], wif[:], ['wif'], ['wif'])
            g.cp('dve', qbb[:], pb[2][:], ['pb2'], ['qbb'])
            headT(g, qbb, 'qbb', 8, tp, idb, qbT, 'qbT')
            g.dma('sp', P['q_qbT'][i], qbT[:].rearrange("p a b -> p (a b)"), ['qbT'], ['qbT'])
            for hh in range(2):
                g.act(sga[:, hh * 512:(hh + 1) * 512], pb[3 + hh][:], AF.Tanh, [f"pb{3 + hh}"], ['sga'], scale=0.5)
                g.act(sgb[:, hh * 512:(hh + 1) * 512], pb[5 + hh][:], AF.Tanh, [f"pb{5 + hh}"], ['sgb'], scale=0.5)
            g.ts('pool', sga[:], sga[:], 1.0, 0.5, ALU.add, ALU.mult, ['sga'], ['sga'])
            g.ts('pool', sgb[:], sgb[:], 1.0, 0.5, ALU.add, ALU.mult, ['sgb'], ['sgb'])
            g.dma('sp', P['q_sga'][i], sga[:], ['sga'], ['sga'])
            g.dma('sp', P['q_sgb'][i], sgb[:], ['sgb'], ['sgb'])
        g.es = es_outer
        g.emit()


def attn_core(g, bufs, kts, kt_src, vt_src, qT, qkey, table, tslot_fn, mask_fn, outT, out_key, mid_at=None,
              mid_fn=None, defer_norm=False):
    S_ps, O_ps, bc_ps = bufs['S_ps'], bufs['O_ps'], bufs['bc_ps']
    kt_t, vt_t, Sb, Pb = bufs['kt_t'], bufs['vt_t'], bufs['Sb'], bufs['Pb']
    oT, rden, ones, irep = bufs['oT'], bufs['rden'], bufs['ones'], bufs['irep']
    NBUF = len(kt_t)
    NS = len(S_ps)
    n = len(kts)
    items = [(p, hg) for p in range(n) for hg in range(2)]

    def load(p):
        b = p % NBUF
        g.dma('sp', kt_t[b][:].rearrange("p a b -> p (a b)"), kt_src(kts[p]), [], [f"kt{b}"])
        g.dma('sp', vt_t[b][:], vt_src(kts[p]), [], [f"vt{b}"])

    def stage_s(j):
        p, hg = items[j]
        kt = kts[p]
        b = p % NBUF
        sp_ = S_ps[j % NS]
        sk = f"S{j % NS}"
        mk = mask_fn(kt) if mask_fn is not None else None
        if mk is not None:
            g.mm(sp_[:, 0:512], mk[0], irep[:], True, False, [mk[1], 'irep'], [sk], skip=True)
        for hh in range(4):
            h = hg * 4 + hh
            g.mm(sp_[:, hh * 128:(hh + 1) * 128], kt_t[b][:, h, :], qT[:, h, :], mk is None and hh == 0, True,
                 [f"kt{b}", qkey], [sk], skip=True)

    def stage_e(j):
        p, hg = items[j]
        slot = tslot_fn(kts[p])
        sp_ = S_ps[j % NS]
        sk = f"S{j % NS}"
        pk = f"P{j % NS}"
        if slot is not None:
            sb_ = Sb[j % 2]
            g.stt(sb_[:], sp_[:].rearrange("p (a b) -> p a b", a=4), 0.125,
                  table[:, slot, hg * 4:(hg + 1) * 4, :], ALU.mult, ALU.add, [sk, 'table'], [f"Sb{j % 2}"])
            g.act(Pb[j % NS][:], sb_[:], AF.Exp, [f"Sb{j % 2}"], [pk])
        else:
            g.act(Pb[j % NS][:], sp_[:].rearrange("p (a b) -> p a b", a=4), AF.Exp, [sk], [pk], scale=0.125)

    def stage_v(j):
        p, hg = items[j]
        b = p % NBUF
        for hh in range(4):
            h = hg * 4 + hh
            g.mm(O_ps[hg][0:65, hh * 128:(hh + 1) * 128], vt_t[b][:, h * 65:(h + 1) * 65], Pb[j % NS][:, hh, :],
                 p == 0 and hh == 0, p == n - 1, [f"vt{b}", f"P{j % NS}"], [f"O{hg}"], skip=True)

    for p in range(min(NBUF - 1, n)):
        load(p)
    stage_s(0)
    if len(items) > 1:
        stage_s(1)
    mid_done = False
    for j in range(len(items)):
        p, hg = items[j]
        if hg == 0 and p + NBUF - 1 < n:
            load(p + NBUF - 1)
        stage_e(j)
        if j + 2 < len(items):
            stage_s(j + 2)
        stage_v(j)
        if mid_fn is not None and not mid_done and j + 1 >= min(mid_at, len(items)):
            mid_fn()
            mid_done = True
    for hg in range(2):
        g.cp('act', oT[0:65, hg * 512:(hg + 1) * 512], O_ps[hg][0:65, :], [f"O{hg}"], [f"oT{hg}"])

    def norm():
        g.op('dve', lambda e: e.reciprocal(out=rden[64:65, :], in_=oT[64:65, :]), ['oT0', 'oT1'], ['rden'])
        for hg in range(2):
            g.mm(bc_ps[hg][0:64, :], ones[64:65, 0:64], rden[64:65, hg * 512:(hg + 1) * 512], True, True,
                 ['ones', 'rden'], [f"bc{hg}"])
            g.tt('dve', outT[:, hg * 4:(hg + 1) * 4, :],
                 oT[0:64, hg * 512:(hg + 1) * 512].rearrange("p (a b) -> p a b", a=4),
                 bc_ps[hg][0:64, :].rearrange("p (a b) -> p a b", a=4), ALU.mult, [f"oT{hg}", f"bc{hg}"], [out_key])
    if defer_norm:
        return norm
    norm()
    return None


def attn_bufs(g, idb, n_s=4):
    bufs = dict(
        S_ps=[g.ps([128, 512], F32) for _ in range(n_s)], O_ps=[g.ps([128, 512], F32) for _ in range(2)],
        bc_ps=[g.ps([128, 512], F32) for _ in range(2)],
        kt_t=[g.sb([64, 8, 128], BF16) for _ in range(4)], vt_t=[g.sb([128, 520], BF16) for _ in range(4)],
        Sb=[g.sb([128, 4, 128], F32) for _ in range(2)], Pb=[g.sb([128, 4, 128], BF16) for _ in range(4)],
        oT=g.sb([128, 1024], F32), rden=g.sb([128, 1024], F32), ones=g.sb([128, 64], F32),
        irep=g.sb([128, 512], BF16))
    return bufs


def attn_consts(g, bufs, idb):
    g.op('pool', lambda e: e.memset(bufs['ones'][:], 1.0), [], ['ones'])
    for r4 in range(4):
        g.cp('pool', bufs['irep'][:, r4 * 128:(r4 + 1) * 128], idb[:], ['idb'], ['irep'])


def phase_c1(g, es_outer, P):
    S, NT, NQ, NB, KSEL = P['S'], P['NT'], P['NQ'], P['NB'], P['KSEL']
    NMAX = max(NT * 128, 9 * 128)
    with ExitStack() as es:
        g.es = es
        idf = g.sb([128, 128], F32); idb = g.sb([128, 128], BF16)
        bufs = attn_bufs(g, idb)
        score = [g.sb([128, NMAX], F32) for _ in range(2)]
        junk = [g.sb([128, NMAX], BF16) for _ in range(2)]
        table = g.sb([128, 3, 8, 128], F32)
        c15 = g.sb([128, 8], F32)
        admneg = g.sb([128, 256], F32)
        pow2 = g.sb([128, NB], F32)
        Dh = g.sb([128, 4, 128], BF16)
        qaT = g.sb([64, 8, 128], BF16); qiT = g.sb([64, 4, 128], BF16); wif = g.sb([128, 4], F32)
        kiT = [g.sb([64, 512], BF16) for _ in range(2)]
        R = g.sb([128, 4, 512], BF16)
        oaT = g.sb([64, 8, 128], BF16)
        sm = {k: g.sb([128, 1], F32) for k in ['mn', 'mx', 'lo', 'w0', 't', 'cnt', 'inc']}
        tz = {k: g.sb([128, 1], F32) for k in ['cpos', 'cnn', 'a', 'b', 'tie', 'r', 'nt', 'thr', 'carry']}
        CH = 2048
        zc = g.sb([128, CH], BF16)
        cum = g.sb([128, CH], F32)
        onesb = g.sb([128, CH], BF16)
        hw = g.sb([128, NB], F32)
        sc_ps = bufs['O_ps'][0]
        dots = [bufs['S_ps'][0], bufs['S_ps'][1], bufs['S_ps'][2], bufs['S_ps'][3]]
        dkeys = ['S0', 'S1', 'S2', 'S3']

        g.dma('sp', idf[:], P['ident'], [], ['idf'])
        g.cp('dve', idb[:], idf[:], ['idf'], ['idb'])
        g.dma('sp', pow2[:], P['pow2'].to_broadcast([128, NB]), [], ['pow2'])
        g.dma('sp', c15[:], P['c15'].to_broadcast([128, 8]), [], ['c15'])
        attn_consts(g, bufs, idb)
        g.op('pool', lambda e: e.memset(onesb[:], 1.0), [], ['onesb'])

        def geom(i):
            samp = (i == NQ)
            nkt = 9 if samp else 2 * i + 2
            return samp, nkt, nkt * 128

        def load_tables(ti, which):
            if which == 'table':
                g.dma('sp', table[:].rearrange("p a b c -> p (a b c)"), P['tdsa'][ti], [], ['table'])
                for j in range(3):
                    g.tt('pool', table[:, j, :, :], table[:, j, :, :], c15[:].unsqueeze(2).to_broadcast([128, 8, 128]),
                         ALU.subtract, ['table', 'c15'], ['table'])
            else:
                g.dma('sp', admneg[:], P['admneg'][ti], [], ['admneg'])

        def IDX(i):
            samp, nkt, N = geom(i)
            sc, sck = score[i % 2], f"score{i % 2}"
            kis = P['s_kiT'] if samp else P['p_kiT']
            g.dma('sp', qiT[:].rearrange("p a b -> p (a b)"), P['q_qiT'][i], [], ['qiT'])
            g.dma('sp', wif[:], P['q_wi'][i], [], ['wif'])
            for h in range(4):
                g.ts('pool', Dh[:, h, :], idf[:], wif[:, h:h + 1], 1.0 / 16.0, ALU.mult, ALU.mult, ['idf', 'wif'], ['Dh'])
            ngr = (nkt + 3) // 4
            for gi in range(ngr):
                n = min(512, N - gi * 512)
                b = gi % 2
                g.dma('sp', kiT[b][:, 0:n], kis[:, gi * 512:gi * 512 + n], [], [f"kiT{b}"])
                for h in range(4):
                    g.mm(dots[h][:, 0:n], qiT[:, h, :], kiT[b][:, 0:n], True, True, [f"kiT{b}", 'qiT'], [dkeys[h]])
                for h in range(4):
                    g.act(R[:, h, 0:n], dots[h][:, 0:n], AF.Relu, [dkeys[h]], [f"R{h}"])
                for h in range(4):
                    g.mm(sc_ps[:, 0:n], Dh[:, h, :], R[:, h, 0:n], h == 0, h == 3, ['Dh', f"R{h}"], ['O0'])
                g.cp('act', sc[:, gi * 512:gi * 512 + n], sc_ps[:, 0:n], ['O0'], [sck])

        def BIS(i):
            samp, nkt, N = geom(i)
            sc, sck = score[i % 2], f"score{i % 2}"
            jk, jkk = junk[i % 2], f"junk{i % 2}"
            if i == 0 or samp:
                load_tables(1 if samp else 0, 'admneg')
            g.op('dve', lambda e: e.tensor_reduce(out=sm['mn'][:], in_=sc[:, 0:N], axis=AX.X, op=ALU.min),
                 [sck], ['mn'])
            g.op('dve', lambda e: e.tensor_reduce(out=sm['mx'][:], in_=sc[:, 0:N], axis=AX.X, op=ALU.max),
                 [sck], ['mx'])
            g.tt('dve', sc[:, N - 256:N], sc[:, N - 256:N], admneg[:], ALU.add, [sck, 'admneg', 'mn', 'mx'], [sck])
            g.ts('dve', sm['lo'][:], sm['mn'][:], -1.0, None, ALU.add, None, ['mn'], ['lo'])
            g.tt('dve', sm['w0'][:], sm['mx'][:], sm['lo'][:], ALU.subtract, ['mx', 'lo'], ['w0'])
            g.ts('dve', hw[:], pow2[:], sm['w0'][:], None, ALU.mult, None, ['pow2', 'w0'], ['hw'])
            ksel = (min(TOPK, (PAST + 64) // 4) if samp else KSEL)
            for k in range(NB):
                g.tt('dve', sm['t'][:], sm['lo'][:], hw[:, k:k + 1], ALU.add, ['lo', 'hw'], ['t'])
                g.ts('dve', jk[:, 0:N], sc[:, 0:N], sm['t'][:], 0.0, ALU.is_gt, ALU.add, [sck, 't'],
                     [jkk, 'cnt'], accum_out=sm['cnt'][:])
                g.ts('dve', sm['inc'][:], sm['cnt'][:], ksel - 0.5, None, ALU.is_gt, None, ['cnt'], ['inc'])
                g.stt(sm['lo'][:], sm['inc'][:], hw[:, k:k + 1], sm['lo'][:], ALU.mult, ALU.add,
                      ['inc', 'hw', 'lo'], ['lo'])
            kf = float(ksel)
            g.ts('dve', jk[:, 0:N], sc[:, 0:N], 0.0, 0.0, ALU.is_gt, ALU.add, [sck], [jkk, 'cpos'],
                 accum_out=tz['cpos'][:])
            g.ts('dve', jk[:, 0:N], sc[:, 0:N], 0.0, 0.0, ALU.is_ge, ALU.add, [sck, 'cpos'], [jkk, 'cnn'],
                 accum_out=tz['cnn'][:])
            g.ts('dve', tz['a'][:], tz['cpos'][:], kf - 0.5, None, ALU.is_lt, None, ['cpos'], ['tz_a'])
            g.ts('dve', tz['b'][:], tz['cnn'][:], kf + 0.5, None, ALU.is_gt, None, ['cnn'], ['tz_b'])
            g.tt('dve', tz['tie'][:], tz['a'][:], tz['b'][:], ALU.mult, ['tz_a', 'tz_b'], ['tie'])
            g.ts('dve', tz['r'][:], tz['cpos'][:], -1.0, kf, ALU.mult, ALU.add, ['cpos'], ['tz_r0'])
            g.tt('dve', tz['r'][:], tz['r'][:], tz['tie'][:], ALU.mult, ['tz_r0', 'tie'], ['tz_r'])
            g.ts('dve', tz['nt'][:], tz['tie'][:], -1.0, 1.0, ALU.mult, ALU.add, ['tie'], ['tz_nt'])
            g.tt('dve', tz['thr'][:], sm['lo'][:], tz['nt'][:], ALU.mult, ['lo', 'tz_nt'], ['thr'])
            for c0 in range(0, N, CH):
                n = min(CH, N - c0)
                g.ts('dve', zc[:, 0:n], sc[:, c0:c0 + n], 0.0, None, ALU.is_equal, None, [sck], ['zc'])
                init = 0.0 if c0 == 0 else tz['carry'][:]
                g.op('dve', lambda e, n=n, init=init: e.tensor_tensor_scan(
                    out=cum[:, 0:n], data0=onesb[:, 0:n], data1=zc[:, 0:n], initial=init, op0=ALU.mult, op1=ALU.add),
                    ['zc', 'onesb', 'carry'], ['cum'])
                if c0 + n < N:
                    g.cp('dve', tz['carry'][:], cum[:, n - 1:n], ['cum'], ['carry'])
                g.stt(zc[:, 0:n], cum[:, 0:n], tz['r'][:], zc[:, 0:n], ALU.is_le, ALU.mult, ['cum', 'tz_r', 'zc'], ['zc'])
                g.stt(zc[:, 0:n], sc[:, c0:c0 + n], tz['thr'][:], zc[:, 0:n], ALU.is_gt, ALU.max, [sck, 'thr', 'zc'], ['zc'])
                g.ts('dve', jk[:, c0:c0 + n], zc[:, 0:n], -1.0, -MASKNEG, ALU.add, ALU.mult, ['zc'], [jkk])
            if 'dbg_score' in P:
                g.dma('sp', P['dbg_score'][i, :, 0:N], sc[:, 0:N], [sck], ['dbg'])
                g.dma('sp', P['dbg_junk'][i, :, 0:N], jk[:, 0:N], [jkk], ['dbg'])

        def ATT(i):
            samp, nkt, N = geom(i)
            jk, jkk = junk[i % 2], f"junk{i % 2}"
            if i == 0 or samp:
                load_tables(1 if samp else 0, 'table')
            g.dma('sp', qaT[:].rearrange("p a b -> p (a b)"), P['q_qaT'][i], [], ['qaT'])
            kas = P['s_KaT'] if samp else P['p_KaT']
            vas = P['s_Va'] if samp else P['p_Va']
            near = [kt for kt in range(nkt - 3, nkt) if kt >= 0]
            order = near + [kt for kt in range(nkt) if kt not in near]
            return attn_core(g, bufs, order, lambda kt: kas[kt], lambda kt: vas[kt], qaT, 'qaT', table,
                             lambda kt: (kt - (nkt - 3)) if kt >= nkt - 3 else None,
                             lambda kt: (jk[:, kt * 128:(kt + 1) * 128], jkk), oaT, 'oaT',
                             mid_at=2 * len(near), mid_fn=(lambda: BIS(i + 1)) if i + 1 <= NQ else None,
                             defer_norm=True)

        IDX(0)
        BIS(0)
        if NQ >= 1:
            IDX(1)
        for i in range(NQ + 1):
            norm = ATT(i)
            if i + 2 <= NQ:
                IDX(i + 2)
            norm()
            g.dma('sp', P['q_oaT'][i], oaT[:].rearrange("p a b -> p (a b)"), ['oaT'], ['oaT'])
        g.es = es_outer
        g.emit()


def phase_c2(g, es_outer, P):
    S, NT, NQ = P['S'], P['NT'], P['NQ']
    with ExitStack() as es:
        g.es = es
        idf = g.sb([128, 128], F32); idb = g.sb([128, 128], BF16)
        bufs = attn_bufs(g, idb, n_s=2)
        table = g.sb([128, 6, 8, 128], F32)
        idx_sb = g.sb([128, NQ + 1], I32)
        wba = g.sb([64, 8, D], BF16); wbb = g.sb([64, 8, D], BF16); wout = g.sb([128, 8, D], BF16)
        stage = [g.sb([128, 2048], F32) for _ in range(2)]
        gbc = g.sb([128, D], F32); bbc = g.sb([128, D], F32)
        qbT = [g.sb([64, 8, 128], BF16) for _ in range(2)]
        oaT = [g.sb([64, 8, 128], BF16) for _ in range(2)]
        obT = g.sb([64, 8, 128], BF16)
        sga = [g.sb([128, D], F32) for _ in range(2)]
        sgb = [g.sb([128, D], F32) for _ in range(2)]
        hown = [g.sb([128, D], F32) for _ in range(2)]
        mpb = g.sb([128, D], BF16); mixT = g.sb([128, 8, 128], BF16)
        small = [make_small(g) for _ in range(2)]
        y_ps = [g.ps([128, 512], F32) for _ in range(2)]
        tp = bufs['bc_ps'][1][:].bitcast(BF16)

        g.dma('sp', idf[:], P['ident'], [], ['idf'])
        g.cp('dve', idb[:], idf[:], ['idf'], ['idb'])
        g.dma('sp', idx_sb[:], P['own_idx'], [], ['idx'])
        g.dma('sp', gbc[:], P['ln2_g'].to_broadcast([128, D]), [], ['lng'])
        g.dma('sp', bbc[:], P['ln2_b'].to_broadcast([128, D]), [], ['lnb'])
        attn_consts(g, bufs, idb)
        for wsrc, wdst, key in [(P['w_branch_a'], wba, 'wba'), (P['w_branch_b'], wbb, 'wbb')]:
            for h in range(8):
                st = stage[h % 2]
                g.dma('sp', st[0:64, 0:D], wsrc[h * 64:(h + 1) * 64, :], [], [f"stg{h % 2}"])
                g.cp(g.cast_eng(), wdst[:, h, :], st[0:64, 0:D], [f"stg{h % 2}"], [key])
        load_weight_bf16(g, wout, P['w_out'], 8, D, [st[:] for st in stage], ['stg0', 'stg1'], 'wout_p')
        g.op('pe', lambda e: e.nop(), reads=[f"wout_p_{kc}_0" for kc in range(8)], writes=['wout'])

        def loads(i):
            b = i % 2
            g.dma('sp', qbT[b][:].rearrange("p a b -> p (a b)"), P['q_qbT'][i], [], [f"qbT{b}"])
            g.dma('sp', oaT[b][:].rearrange("p a b -> p (a b)"), P['q_oaT'][i], [], [f"oaT{b}"])
            g.dma('sp', sga[b][:], P['q_sga'][i], [], [f"sga{b}"])
            g.dma('sp', sgb[b][:], P['q_sgb'][i], [], [f"sgb{b}"])
            g.op('pool', lambda e: e.indirect_dma_start(
                out=hown[b][:], out_offset=None, in_=P['h_all'],
                in_offset=bass.IndirectOffsetOnAxis(ap=idx_sb[:, i:i + 1], axis=0)),
                reads=['idx'], writes=[f"hown{b}"], dma=True)

        loads(0)
        for i in range(NQ + 1):
            samp = (i == NQ)
            b = i % 2
            if i == 0 or samp:
                g.dma('sp', table[:].rearrange("p a b c -> p (a b c)"), P['tband'][1 if samp else 0], [], ['table'])
            if samp:
                kts = list(range(5)); base = 0
                kbs, vbs = P['s_KbT'], P['s_Vb']
            else:
                base = 2 * i - 4
                kts = [kt for kt in range(base, 2 * i + 2) if kt >= 0]
                kbs, vbs = P['p_KbT'], P['p_Vb']
            attn_core(g, bufs, kts, lambda kt: kbs[kt], lambda kt: vbs[kt], qbT[b], f"qbT{b}", table,
                      lambda kt, base=base: kt - base, None, obT, 'obT')
            if i + 1 <= NQ:
                loads(i + 1)
            for (oT_, okey, w_, wkey, sg, sgk) in [(oaT[b], f"oaT{b}", wba, 'wba', sga[b], f"sga{b}"),
                                                   (obT, 'obT', wbb, 'wbb', sgb[b], f"sgb{b}")]:
                for hh in range(2):
                    for h in range(8):
                        g.mm(y_ps[hh][:], oT_[:, h, :], w_[:, h, hh * 512:(hh + 1) * 512], h == 0, h == 7,
                             [okey, wkey], [f"y{hh}"])
                    g.tt('dve', sg[:, hh * 512:(hh + 1) * 512], sg[:, hh * 512:(hh + 1) * 512], y_ps[hh][:], ALU.mult,
                         [sgk, f"y{hh}"], [sgk])
            g.tt('pool', mpb[:], sga[b][:], sgb[b][:], ALU.add, [f"sga{b}", f"sgb{b}"], ['mpb'])
            for k in range(8):
                g.tr(tp[:, k * 128:(k + 1) * 128], mpb[:, k * 128:(k + 1) * 128], idb[:], ['mpb', 'idb'], ['bc1'])
            g.cp('act', mixT[:].rearrange("p a b -> p (a b)"), tp[:, 0:1024], ['bc1'], ['mixT'])
            hk = f"hown{b}"
            for hh in range(2):
                for k in range(8):
                    g.mm(y_ps[hh][:], mixT[:, k, :], wout[:, k, hh * 512:(hh + 1) * 512], k == 0, k == 7,
                         ['mixT', 'wout'], [f"y{hh}"])
                hs = hown[b][:, hh * 512:(hh + 1) * 512]
                g.stt(hs, hs, ALPHA, y_ps[hh][:], ALU.mult, ALU.add, [hk, f"y{hh}"], [hk])
            layer_norm_tile(g, hown[b][:], hk, gbc, bbc, small[b], f"ln2{b}")
            g.dma('pool', P['h2_own'][i * 128:(i + 1) * 128, :], hown[b][:], [hk], [hk])
        g.es = es_outer
        g.emit()


def build_program(S, debug=None, nphases=5, bparts=('roll', 'cache', 'b2')):
    NT = S // 128
    NQ = S // 256
    NTA = NT + 1
    NQA = NQ + 1
    nc = bass.Bass("TRN2", target_bir_lowering=False)
    dbg = set(debug or [])

    def din(name, shape, dt=F32):
        return nc.dram_tensor(name, list(shape), dt, kind="ExternalInput").ap()

    def dout(name, shape, dt=F32):
        return nc.dram_tensor(name, list(shape), dt, kind="ExternalOutput").ap()

    def dscr(name, shape, dt=F32):
        if name in dbg:
            return dout(name, shape, dt)
        return nc.dram_tensor(name, list(shape), dt, kind="Internal").ap()

    P = dict(S=S, NT=NT, NQ=NQ, NB=N_BISECT, KSEL=min(TOPK, S // 4), bparts=set(bparts))
    P['x_all'] = din("x_all", [NTA * 128, D])
    P['ident'] = din("ident", [128, 128])
    P['own_idx'] = din("own_idx", [128, NQA], I32)
    for nm, shp in [("ffn1_wi", [D, 2 * FF]), ("ffn1_wo", [FF, D]), ("ffn2_wi", [D, 2 * FF]), ("ffn2_wo", [FF, D]),
                    ("w_in", [D, N_IN]), ("w_branch_a", [512, D]), ("w_branch_b", [512, D]), ("w_out", [D, D]),
                    ("ln1_g", [1, D]), ("ln1_b", [1, D]), ("ln2_g", [1, D]), ("ln2_b", [1, D]),
                    ("ln3_g", [1, D]), ("ln3_b", [1, D]),
                    ("c_ka", [PAST, 512]), ("c_va", [PAST, 512]), ("c_ki", [PAST, 64]),
                    ("c_kb", [BAND, 512]), ("c_vb", [BAND, 512]),
                    ("admneg", [2, 128, 256]), ("tdsa", [2, 128, 3 * 8 * 128]), ("tband", [2, 128, 6 * 8 * 128]),
                    ("c15", [1, 8]), ("pow2", [1, N_BISECT])]:
        P[nm] = din(nm, shp)
    P['h_all'] = dscr("h_all", [NTA * 128, D])
    for pre, n_a, n_b in [('p_', NT, NT), ('s_', 9, 5)]:
        P[pre + 'KaT'] = dscr(pre + "KaT", [n_a, 64, 1024], BF16)
        P[pre + 'Va'] = dscr(pre + "Va", [n_a, 128, 520], BF16)
        P[pre + 'kiT'] = dscr(pre + "kiT", [64, n_a * 128], BF16)
        P[pre + 'KbT'] = dscr(pre + "KbT", [n_b, 64, 1024], BF16)
        P[pre + 'Vb'] = dscr(pre + "Vb", [n_b, 128, 520], BF16)
    P['q_qaT'] = dscr("q_qaT", [NQA, 64, 1024], BF16)
    P['q_qiT'] = dscr("q_qiT", [NQA, 64, 512], BF16)
    P['q_wi'] = dscr("q_wi", [NQA, 128, 4])
    P['q_qbT'] = dscr("q_qbT", [NQA, 64, 1024], BF16)
    P['q_sga'] = dscr("q_sga", [NQA, 128, D])
    P['q_sgb'] = dscr("q_sgb", [NQA, 128, D])
    P['q_oaT'] = dscr("q_oaT", [NQA, 64, 1024], BF16)
    P['h2_own'] = dscr("h2_own", [NQA * 128, D])
    if 'dbg_score' in dbg:
        NMAX = max(NT * 128, 9 * 128)
        P['dbg_score'] = dscr("dbg_score", [NQA, 128, NMAX])
        P['dbg_junk'] = dscr("dbg_junk", [NQA, 128, NMAX], BF16)
    P['y_own'] = dout("y_own", [NQA * 128, D])
    P['kA_out'] = dout("kA_out", [S, 512]); P['vA_out'] = dout("vA_out", [S, 512])
    P['kidx_out'] = dout("kidx_out", [S, 64])
    P['kB_out'] = dout("kB_out", [512, 512]); P['vB_out'] = dout("vB_out", [512, 512])
    P['skA'] = dout("skA", [128, 512]); P['svA'] = dout("svA", [128, 512]); P['skidx'] = dout("skidx", [128, 64])
    P['skB'] = dout("skB", [512, 512]); P['svB'] = dout("svB", [512, 512])

    with ExitStack() as es:
        g = G(nc, es)
        ffn_phase(g, nc, es, P['x_all'], P['h_all'], P['ffn1_wi'], P['ffn1_wo'], P['ln1_g'], P['ln1_b'],
                  P['ident'], NTA)
        if nphases >= 2:
            phase_b(g, es, P)
        if nphases >= 3:
            phase_c1(g, es, P)
        if nphases >= 4:
            phase_c2(g, es, P)
        if nphases >= 5:
            ffn_phase(g, nc, es, P['h2_own'], P['y_own'], P['ffn2_wi'], P['ffn2_wo'], P['ln3_g'], P['ln3_b'],
                      P['ident'], NQA)
    return nc


def _t5_bucket_np(rel):
    import math
    rel = np.asarray(rel, np.int64)
    half, exact = 16, 8
    n = np.abs(rel)
    lr = np.log(np.maximum(n, 1).astype(np.float32) / np.float32(exact)) / np.float32(math.log(128 / exact))
    large = np.minimum(exact + (lr.astype(np.float32) * np.float32(half - exact)).astype(np.int32), half - 1)
    return (rel > 0).astype(np.int32) * half + np.where(n < exact, n, large)


def _qpos(r, i):
    p = np.arange(128)
    return np.where(p < 64, (4 * i + r) * 64 + p, (4 * i + 2 + r) * 64 + (p - 64))


def _tables(r, t5_bias, rel_bias):
    kk = np.arange(128)
    i = 2
    qp = _qpos(r, i)
    adm = np.zeros((2, 128, 256), np.float32)
    kc = (2 * i * 128 + np.arange(256)) // 64
    adm[0] = np.where(kc[None, :] <= (qp // 64)[:, None], 0.0, NEG)
    tdsa = np.zeros((2, 128, 3, 8, 128), np.float32)
    for j in range(3):
        kpos = (2 * i - 1 + j) * 128 + kk
        bk = _t5_bucket_np(kpos[:, None] - qp[None, :])
        tdsa[0, :, j] = np.transpose(t5_bias[:, bk], (1, 0, 2))
    tband = np.full((2, 128, 6, 8, 128), MASKNEG, np.float32)
    for j in range(6):
        kpos = (2 * i - 4 + j) * 128 + kk
        rel = kpos[:, None] - qp[None, :]
        vis = ((kpos // 64)[:, None] <= (qp // 64)[None, :]) & ((kpos // 64)[:, None] >= (qp // 64)[None, :] - 8)
        ridx = np.clip(rel, -128, 63) + 128
        vals = np.transpose(rel_bias[:, ridx], (1, 0, 2))
        tband[0, :, j] = np.where(vis[:, None, :], vals, MASKNEG)
    qs = PAST + np.minimum(np.arange(128), 63)
    cols = 7 * 128 + np.arange(256)
    adm[1] = np.where(cols[None, :] < PAST + 64, 0.0, NEG) + np.zeros((128, 1), np.float32)
    for j in range(3):
        kpos = (6 + j) * 128 + kk
        bk = _t5_bucket_np(kpos[:, None] - qs[None, :])
        tdsa[1, :, j] = np.transpose(t5_bias[:, bk], (1, 0, 2))
    for j in range(5):
        kpos = (PAST - BAND) + j * 128 + kk
        rel = kpos[:, None] - qs[None, :]
        vis = (kpos < PAST + 64)[:, None] & np.ones((1, 128), bool)
        ridx = np.clip(rel, -128, 63) + 128
        vals = np.transpose(rel_bias[:, ridx], (1, 0, 2))
        tband[1, :, j] = np.where(vis[:, None, :], vals, MASKNEG)
    return adm, tdsa.reshape(2, 128, -1), tband.reshape(2, 128, -1)


_PROG = {}


def _prep_inputs(inputs, S, n_cores=8):
    NT, NQ = S // 128, S // 256
    f = lambda a: np.ascontiguousarray(a, dtype=np.float32)
    common = {
        "ident": np.eye(128, dtype=np.float32),
        "ffn1_wi": f(inputs['ffn1_wi'][0]), "ffn1_wo": f(inputs['ffn1_wo'][0]),
        "ffn2_wi": f(inputs['ffn2_wi'][0]), "ffn2_wo": f(inputs['ffn2_wo'][0]),
        "w_in": f(inputs['w_in'][0]), "w_branch_a": f(inputs['w_branch_a'][0]),
        "w_branch_b": f(inputs['w_branch_b'][0]), "w_out": f(inputs['w_out'][0]),
        "ln1_g": f(inputs['ln1_g']), "ln1_b": f(inputs['ln1_b']), "ln2_g": f(inputs['ln2_g']),
        "ln2_b": f(inputs['ln2_b']), "ln3_g": f(inputs['ln3_g']), "ln3_b": f(inputs['ln3_b']),
        "c15": f(np.asarray(inputs['t5_bias'])[:, 15][None, :]),
        "pow2": (0.5 ** np.arange(1, N_BISECT + 1, dtype=np.float64)).astype(np.float32)[None, :],
    }
    tabs = [_tables(r, np.asarray(inputs['t5_bias'], np.float32), np.asarray(inputs['rel_bias_b'][0], np.float32))
            for r in range(2)]
    maps = []
    for c in range(n_cores):
        b, r = c // 2, c % 2
        xs = np.zeros((128, D), np.float32)
        xs[:64] = inputs['x_sample'][c]
        own = np.zeros((128, NQ + 1), np.int32)
        for i in range(NQ):
            own[:, i] = _qpos(r, i)
        own[:, NQ] = S + np.arange(128)
        m = dict(common)
        m.update({
            "x_all": np.concatenate([f(inputs['x_prompt'][b, :S]), xs], 0),
            "own_idx": own,
            "c_ka": f(inputs['cache_k_a'][0, c]).reshape(PAST, 512),
            "c_va": f(inputs['cache_v_a'][0, c]).reshape(PAST, 512),
            "c_ki": f(inputs['cache_kidx_a'][0, c]),
            "c_kb": f(inputs['cache_k_b'][0, c]).reshape(BAND, 512),
            "c_vb": f(inputs['cache_v_b'][0, c]).reshape(BAND, 512),
            "admneg": tabs[r][0], "tdsa": tabs[r][1], "tband": tabs[r][2],
        })
        maps.append(m)
    return maps


def _assemble(results, S, B=4):
    NQ = S // 256
    keep = min(BAND, S)
    y_prompt = np.zeros((B, S, D), np.float32)
    y_sample = np.zeros((8, 64, D), np.float32)
    nka = np.zeros((1, B, S, 8, 64), np.float32); nva = np.zeros_like(nka)
    nki = np.zeros((1, B, S, 64), np.float32)
    nkb = np.zeros((1, B, keep, 8, 64), np.float32); nvb = np.zeros_like(nkb)
    ska = np.zeros((1, 8, 64, 8, 64), np.float32); sva = np.zeros_like(ska)
    ski = np.zeros((1, 8, 64, 64), np.float32)
    skb = np.zeros((1, 8, BAND, 8, 64), np.float32); svb = np.zeros_like(skb)
    for c, res in enumerate(results):
        b, r = c // 2, c % 2
        yo = res['y_own']
        for i in range(NQ):
            qp = _qpos(r, i)
            y_prompt[b, qp] = yo[i * 128:(i + 1) * 128]
        y_sample[c] = yo[NQ * 128:NQ * 128 + 64]
        if r == 0:
            nka[0, b] = res['kA_out'].reshape(S, 8, 64)
            nva[0, b] = res['vA_out'].reshape(S, 8, 64)
            nki[0, b] = res['kidx_out']
            nkb[0, b] = res['kB_out'].reshape(BAND, 8, 64)[-keep:]
            nvb[0, b] = res['vB_out'].reshape(BAND, 8, 64)[-keep:]
        ska[0, c] = res['skA'][:64].reshape(64, 8, 64)
        sva[0, c] = res['svA'][:64].reshape(64, 8, 64)
        ski[0, c] = res['skidx'][:64]
        skb[0, c] = res['skB'].reshape(BAND, 8, 64)
        svb[0, c] = res['svB'].reshape(BAND, 8, 64)
    return (y_prompt, y_sample, nka, nva, nki, nkb, nvb, ska, sva, ski, skb, svb)


def kernel(**inputs):
    S = int(np.asarray(inputs['x_prompt']).shape[1])
    if S not in _PROG:
        _PROG[S] = build_program(S)
    nc = _PROG[S]
    maps = _prep_inputs(inputs, S)
    res = run_bass_kernel_spmd(nc, maps, core_ids=list(range(8)))
    return _assemble(res.results, S)
```

```python
import numpy as np
import concourse.bass as bass
import concourse.mybir as mybir
from concourse.bass_utils import run_bass_kernel_spmd
from contextlib import ExitStack

F32 = mybir.dt.float32
BF16 = mybir.dt.bfloat16
I32 = mybir.dt.int32
AF = mybir.ActivationFunctionType
ALU = mybir.AluOpType
AX = mybir.AxisListType
ENGS = ['pe', 'act', 'dve', 'pool', 'sp']

D = 1024
FF = 2816
NFF = 22
HD = 64
NH = 8
CHUNK = 64
PAST = 1024
BAND = 512
TOPK = 256
LN_EPS = 1e-5
ALPHA = 2.0 ** 0.25
NEG = -1e30
MASKNEG = -30000.0
N_BISECT = 22
C_QA, C_KA, C_VA, C_QI, C_KI, C_WI, C_QB, C_KB, C_VB, C_GA, C_GB = (
    0, 512, 1024, 1536, 1792, 1856, 1860, 2372, 2884, 3396, 4420)
N_IN = 5444


class G:
    NDS = {'sp': 40, 'pool': 24, 'act': 4, 'pe': 0, 'dve': 0}

    def __init__(self, nc, es):
        self.nc = nc
        self.es = es
        self.ops = {e: [] for e in ENGS}
        self.last_w = {}
        self.readers = {}
        self.nsb = 0
        self.sems = {}
        for e in ENGS:
            self.sems[('c', e)] = es.enter_context(nc.semaphore(f"c_{e}"))
            for j in range(self.NDS[e]):
                self.sems[('d', e, j)] = es.enter_context(nc.semaphore(f"d_{e}_{j}"))
        self.ccount = {e: 0 for e in ENGS}
        self.dnum = {e: 0 for e in ENGS}
        self.rr = 0

    def sb(self, shape, dt, name=None):
        self.nsb += 1
        return self.es.enter_context(self.nc.sbuf_tensor(name or f"sb{self.nsb}", list(shape), dt))

    def ps(self, shape, dt=F32, name=None):
        self.nsb += 1
        return self.es.enter_context(self.nc.psum_tensor(name or f"ps{self.nsb}", list(shape), dt))

    def op(self, eng, fn, reads=(), writes=(), dma=False):
        idx = len(self.ops[eng])
        deps = set()
        for k in reads:
            w = self.last_w.get(k)
            if w is not None:
                deps.add(w)
        for k in writes:
            w = self.last_w.get(k)
            if w is not None:
                deps.add(w)
            for r in self.readers.get(k, ()):
                deps.add(r)
        deps.discard((eng, idx))
        for k in writes:
            self.last_w[k] = (eng, idx)
            self.readers[k] = []
        for k in reads:
            self.readers.setdefault(k, []).append((eng, idx))
        self.ops[eng].append(dict(fn=fn, deps=deps, dma=dma, signal=False, hval=None, pre=None))
        return (eng, idx)

    def emit(self):
        nc = self.nc
        sems = self.sems
        ops = self.ops
        for e in ENGS:
            for o in ops[e]:
                for (de, di) in o['deps']:
                    d = ops[de][di]
                    if de == 'pe' and e == 'pe' and not d['dma']:
                        continue
                    d['signal'] = True
            for o in reversed(ops[e]):
                if not o['dma']:
                    o['signal'] = True
                    break
        for e in ENGS:
            for o in ops[e]:
                if o['dma']:
                    m = self.dnum[e]
                    self.dnum[e] += 1
                    j, rnd = m % self.NDS[e], m // self.NDS[e]
                    o['hval'] = (('d', e, j), 16 * (rnd + 1))
                    o['pre'] = (('d', e, j), 16 * rnd) if rnd > 0 else None
                elif o['signal']:
                    self.ccount[e] += 1
                    o['hval'] = (('c', e), self.ccount[e])
        ccount = dict(self.ccount)
        dfinal = {}
        for e in ENGS:
            for j in range(self.NDS[e]):
                n_j = (self.dnum[e] - j + self.NDS[e] - 1) // self.NDS[e] if self.dnum[e] > j else 0
                if n_j > 0:
                    dfinal[('d', e, j)] = 16 * n_j
        with nc.Block() as block:
            def mk(e):
                def body(engine):
                    seen = {}

                    def wait(sk, v):
                        if v <= 0 or seen.get(sk, 0) >= v:
                            return
                        engine.wait_ge(sems[sk], v)
                        seen[sk] = v
                    for o in ops[e]:
                        for (de, di) in sorted(o['deps']):
                            d = ops[de][di]
                            if de == 'pe' and e == 'pe' and not d['dma']:
                                continue
                            wait(*d['hval'])
                        if o['pre'] is not None:
                            wait(*o['pre'])
                        ins = o['fn'](engine)
                        if o['dma']:
                            ins.then_inc(sems[o['hval'][0]], 16)
                        elif o['signal']:
                            ins.then_inc(sems[o['hval'][0]], 1)
                    for e2 in ENGS:
                        if ccount[e2] > 0:
                            wait(('c', e2), ccount[e2])
                    for sk, v in dfinal.items():
                        wait(sk, v)
                return body
            block.tensor(mk('pe'))
            block.scalar(mk('act'))
            block.vector(mk('dve'))
            block.gpsimd(mk('pool'))
            block.sync(mk('sp'))
        self.ops = {e: [] for e in ENGS}
        self.last_w = {}
        self.readers = {}

    def dma(self, q, out, in_, r, w):
        return self.op(q, lambda e: e.dma_start(out=out, in_=in_), reads=r, writes=w, dma=True)

    def mm(self, out, lhsT, rhs, start, stop, r, w, skip=False):
        return self.op('pe', lambda e: e.matmul(out=out, lhsT=lhsT, rhs=rhs, start=start, stop=stop,
                                                skip_group_check=skip), reads=r, writes=w)

    def tr(self, out, in_, ident, r, w):
        return self.op('pe', lambda e: e.transpose(out=out, in_=in_, identity=ident), reads=r, writes=w)

    def act(self, out, in_, func, r, w, scale=None, bias=None, accum_out=None):
        kw = {}
        if scale is not None:
            kw['scale'] = scale
        if bias is not None:
            kw['bias'] = bias
        if accum_out is not None:
            kw['accum_out'] = accum_out
        return self.op('act', lambda e: e.activation(out=out, in_=in_, func=func, **kw), reads=r, writes=w)

    def ts(self, eng, out, in0, s1, s2, op0, op1, r, w, accum_out=None):
        kw = {}
        if op1 is not None:
            kw['op1'] = op1
        if accum_out is not None:
            kw['accum_out'] = accum_out
        return self.op(eng, lambda e: e.tensor_scalar(out=out, in0=in0, scalar1=s1, scalar2=s2, op0=op0, **kw),
                       reads=r, writes=w)

    def tt(self, eng, out, in0, in1, op, r, w):
        return self.op(eng, lambda e: e.tensor_tensor(out=out, in0=in0, in1=in1, op=op), reads=r, writes=w)

    def stt(self, out, in0, scalar, in1, op0, op1, r, w):
        return self.op('dve', lambda e: e.scalar_tensor_tensor(out=out, in0=in0, scalar=scalar, in1=in1,
                                                               op0=op0, op1=op1), reads=r, writes=w)

    def cp(self, eng, out, in_, r, w):
        if eng == 'act':
            return self.op('act', lambda e: e.activation(out=out, in_=in_, func=AF.Copy), reads=r, writes=w)
        return self.op(eng, lambda e: e.tensor_copy(out=out, in_=in_), reads=r, writes=w)

    def cast_eng(self):
        self.rr += 1
        return ['dve', 'act', 'dve'][self.rr % 3]


def load_weight_bf16(g, dst, src, kchunks, ncols, stage, stage_keys, dst_key, piece=2048):
    i = 0
    for kc in range(kchunks):
        for c0 in range(0, ncols, piece):
            n = min(piece, ncols - c0)
            st, sk = stage[i % len(stage)], stage_keys[i % len(stage)]
            i += 1
            g.dma('sp', st[:, 0:n], src[kc * 128:(kc + 1) * 128, c0:c0 + n], [], [sk])
            g.cp(g.cast_eng(), dst[:, kc, c0:c0 + n], st[:, 0:n], [sk], [dst_key + f"_{kc}_{c0}"])


def rsqrt_newton(g, var_ap, rs, small, rkeys, tag):
    v, ti, ui, b, t = small['v'], small['ti'], small['ui'], small['b'], small['t']
    g.ts('dve', v[:], var_ap, LN_EPS, None, ALU.add, None, rkeys, [tag + 'v'])
    g.ts('dve', ti[:], v[:].bitcast(I32), 1, None, ALU.arith_shift_right, None, [tag + 'v'], [tag + 'ti'])
    g.ts('dve', ui[:], ti[:], -1.0, 1597463007.0, ALU.mult, ALU.add, [tag + 'ti'], [tag + 'y'])
    y = ui[:].bitcast(F32)
    for it in range(3):
        g.stt(b[:], y, v[:], y, ALU.mult, ALU.mult, [tag + 'y', tag + 'v'], [tag + 'b'])
        g.stt(t[:], b[:], -0.5, y, ALU.mult, ALU.mult, [tag + 'b', tag + 'y'], [tag + 't'])
        dst = rs[:] if it == 2 else y
        g.stt(dst, y, 1.5, t[:], ALU.mult, ALU.add, [tag + 'y', tag + 't'], [tag + ('rs' if it == 2 else 'y')])


def make_small(g):
    return dict(st=g.sb([128, 2, 6], F32), mv=g.sb([128, 2], F32), rs=g.sb([128, 1], F32),
                nb=g.sb([128, 1], F32), v=g.sb([128, 1], F32), ti=g.sb([128, 1], I32), ui=g.sb([128, 1], I32),
                b=g.sb([128, 1], F32), t=g.sb([128, 1], F32))


def layer_norm_tile(g, y, ykey, gbc, bbc, small, tag):
    st, mv, rs, nb = small['st'], small['mv'], small['rs'], small['nb']
    for hh in range(2):
        g.op('dve', lambda e, hh=hh: e.bn_stats(out=st[:, hh, :], in_=y[:, hh * 512:(hh + 1) * 512]),
             reads=[ykey], writes=[tag + 'st'])
    g.op('dve', lambda e: e.bn_aggr(out=mv[:], in_=st[:].rearrange('p a b -> p (a b)')), reads=[tag + 'st'], writes=[tag + 'mv'])
    rsqrt_newton(g, mv[:, 1:2], rs, small, [tag + 'mv'], tag)
    g.stt(nb[:], mv[:, 0:1], -1.0, rs[:], ALU.mult, ALU.mult, [tag + 'mv', tag + 'rs'], [tag + 'nb'])
    g.act(y, y, AF.Identity, [ykey, tag + 'rs', tag + 'nb'], [ykey], scale=rs[:], bias=nb[:])
    g.tt('dve', y, y, gbc[:], ALU.mult, [ykey, 'lng'], [ykey])
    g.tt('pool', y, y, bbc[:], ALU.add, [ykey, 'lnb'], [ykey])


def ffn_phase(g, nc, es_outer, src, dst, wi, wo, lng, lnb, ident_d, ntiles, T=4):
    TN = T * 128
    with ExitStack() as es:
        g.es = es
        wi_b = g.sb([128, 8, 2 * FF], BF16)
        wo_b = g.sb([128, NFF, D], BF16)
        xin = g.sb([128, T, D], F32)
        xb = g.sb([128, D], BF16)
        xT = g.sb([128, 8, TN], BF16)
        gT = g.sb([128, NFF, TN], BF16)
        sil = g.sb([128, TN], F32)
        gbc = g.sb([128, D], F32)
        bbc = g.sb([128, D], F32)
        idf = g.sb([128, 128], F32)
        idb = g.sb([128, 128], BF16)
        small = [make_small(g) for _ in range(2)]
        up = [g.ps([128, 512], F32) for _ in range(4)]
        dn = [g.ps([128, 512], F32) for _ in range(2)]
        tp = g.ps([128, 8, 128], BF16)

        g.dma('sp', idf[:], ident_d, [], ['idf'])
        g.cp('dve', idb[:], idf[:], ['idf'], ['idb'])
        g.dma('sp', gbc[:], lng.to_broadcast([128, D]), [], ['lng'])
        g.dma('sp', bbc[:], lnb.to_broadcast([128, D]), [], ['lnb'])
        xflat = xin[:].rearrange("p a b -> p (a b)")
        stages = [xflat[:, q * 1024:(q + 1) * 1024] for q in range(T)]
        skeys = [f"xin_t{q}" for q in range(T)]
        load_weight_bf16(g, wi_b, wi, 8, 2 * FF, stages, skeys, 'wi', piece=1024)
        load_weight_bf16(g, wo_b, wo, NFF, D, stages, skeys, 'wo', piece=1024)
        wi_keys = [f"wi_{kc}_{c0}" for kc in range(8) for c0 in range(0, 2 * FF, 1024)]
        wo_keys = [f"wo_{kc}_0" for kc in range(NFF)]
        g.op('pool', lambda e: e.memset(sil[:, 0:1], 0.0), reads=wi_keys + wo_keys,
             writes=[f"xin_t{j}" for j in range(T)] + ['sil'])

        nst = (ntiles + T - 1) // T
        for st_i in range(nst):
            t0 = st_i * T
            Tc = min(T, ntiles - t0)
            Tn = Tc * 128
            for j in range(Tc):
                xk = f"xin_t{j}"
                g.dma('sp', xin[:, j, :], src[(t0 + j) * 128:(t0 + j + 1) * 128, :], [], [xk])
            for j in range(Tc):
                xk = f"xin_t{j}"
                g.cp('pool', xb[:], xin[:, j, :], [xk], ['xb'])
                g.act(xin[:, j, :], xin[:, j, :], AF.Copy, [xk, 'xb'], [xk], scale=ALPHA)
                for k in range(8):
                    g.tr(tp[:, k, :], xb[:, k * 128:(k + 1) * 128], idb[:], ['xb', 'idb'], ['tp'])
                g.cp('dve', xT[:, :, j * 128:(j + 1) * 128], tp[:], ['tp'], [f"xT{j}"])
            xTk = [f"xT{j}" for j in range(Tc)]
            for c in range(NFF):
                pa, pu = up[2 * (c % 2)], up[2 * (c % 2) + 1]
                ka, ku = f"up{2 * (c % 2)}", f"up{2 * (c % 2) + 1}"
                for k in range(8):
                    g.mm(pa[:, 0:Tn], wi_b[:, k, c * 128:(c + 1) * 128], xT[:, k, 0:Tn], k == 0, k == 7,
                         xTk + (wi_keys if st_i == 0 else []), [ka])
                for k in range(8):
                    g.mm(pu[:, 0:Tn], wi_b[:, k, FF + c * 128:FF + (c + 1) * 128], xT[:, k, 0:Tn],
                         k == 0, k == 7, xTk, [ku])
                g.act(sil[:, 0:Tn], pa[:, 0:Tn], AF.Tanh, [ka], ['sil'], scale=0.5)
                g.stt(sil[:, 0:Tn], sil[:, 0:Tn], 1.0, pa[:, 0:Tn], ALU.add, ALU.mult, ['sil', ka], ['sil'])
                g.stt(gT[:, c, 0:Tn], sil[:, 0:Tn], 0.5, pu[:, 0:Tn], ALU.mult, ALU.mult, ['sil', ku], [f"gT{c}"])
            gk = [f"gT{c}" for c in range(NFF)]
            for j in range(Tc):
                yk = f"xin_t{j}"
                for hh in range(2):
                    for c in range(NFF):
                        g.mm(dn[hh][:], gT[:, c, j * 128:(j + 1) * 128], wo_b[:, c, hh * 512:(hh + 1) * 512],
                             c == 0, c == NFF - 1, gk + (wo_keys if st_i == 0 else []), [f"dn{hh}"])
                    ysl = xin[:, j, hh * 512:(hh + 1) * 512]
                    g.stt(ysl, dn[hh][:], 0.5, ysl, ALU.mult, ALU.add, [f"dn{hh}", yk], [yk])
                layer_norm_tile(g, xin[:, j, :], yk, gbc, bbc, small[j % 2], f"ln{j % 2}")
                dap = dst(t0 + j) if callable(dst) else dst[(t0 + j) * 128:(t0 + j + 1) * 128, :]
                g.dma('pool', dap, xin[:, j, :], [yk], [yk])
        g.es = es_outer
        g.emit()


def proj_tile(g, b, bufs, w_b, groups):
    hf, hb, hT, tp, idb = bufs['hf'][b], bufs['hb'][b], bufs['hT'][b], bufs['tp'], bufs['idb']
    g.cp('pool', hb[:], hf[:], [f"hf{b}"], [f"hb{b}"])
    for k in range(8):
        g.tr(tp[:, k * 128:(k + 1) * 128], hb[:, k * 128:(k + 1) * 128], idb[:], [f"hb{b}", 'idb'], ['tp'])
    g.cp('dve', hT[:].rearrange("p a b -> p (a b)"), tp[:, 0:1024], ['tp'], [f"hT{b}"])
    for (pap, key, c0, n) in groups:
        for k in range(8):
            g.mm(pap, hT[:, k, :], w_b[:, k, c0:c0 + n], k == 0, k == 7, [f"hT{b}", 'w_in'], [key])


def headT(g, src_bf, src_key, nheads, tp, idb, dstT, dst_key):
    for h in range(nheads):
        g.tr(tp[0:64, h * 128:(h + 1) * 128], src_bf[:, h * 64:(h + 1) * 64], idb[:], [src_key, 'idb'], ['tp'])
    g.cp('act', dstT[:].rearrange("p a b -> p (a b)"), tp[0:64, 0:nheads * 128], ['tp'], [dst_key])


def phase_b(g, es_outer, P):
    S, NT, NQ = P['S'], P['NT'], P['NQ']
    with ExitStack() as es:
        g.es = es
        w_b = g.sb([128, 8, N_IN], BF16)
        stage = [g.sb([128, 2048], F32) for _ in range(2)]
        idf = g.sb([128, 128], F32)
        idb = g.sb([128, 128], BF16)
        idx_sb = g.sb([128, NQ + 1], I32)
        bufs = dict(hf=[g.sb([128, D], F32) for _ in range(2)], hb=[g.sb([128, D], BF16) for _ in range(2)],
                    hT=[g.sb([128, 8, 128], BF16) for _ in range(2)], idb=idb)
        kaf = g.sb([128, 512], F32); vaf = g.sb([128, 512], F32); kif = g.sb([128, 64], F32)
        kbf = g.sb([128, 512], F32); vbf = g.sb([128, 512], F32)
        kab = g.sb([128, 512], BF16); kib = g.sb([128, 64], BF16); kbb = g.sb([128, 512], BF16)
        vaug = g.sb([128, 8, 65], BF16); vbug = g.sb([128, 8, 65], BF16)
        kaT = g.sb([64, 8, 128], BF16); kbT = g.sb([64, 8, 128], BF16); kiT = g.sb([64, 1, 128], BF16)
        qab = g.sb([128, 512], BF16); qib = g.sb([128, 256], BF16); qbb = g.sb([128, 512], BF16)
        qaT = g.sb([64, 8, 128], BF16); qiT = g.sb([64, 4, 128], BF16); qbT = g.sb([64, 8, 128], BF16)
        wif = g.sb([128, 4], F32)
        sga = g.sb([128, D], F32); sgb = g.sb([128, D], F32)
        pb = [g.ps([128, 512], F32) for _ in range(7)]
        tpf = g.ps([128, 512], F32)
        tp = tpf[:].bitcast(BF16)
        bufs['tp'] = tp

        g.dma('sp', idf[:], P['ident'], [], ['idf'])
        g.cp('dve', idb[:], idf[:], ['idf'], ['idb'])
        g.dma('sp', idx_sb[:], P['own_idx'], [], ['idx'])
        st4 = [stage[q // 2][:, (q % 2) * 1024:(q % 2 + 1) * 1024] for q in range(4)]
        load_weight_bf16(g, w_b, P['w_in'], 8, N_IN, st4, ['stg0', 'stg1', 'stg2', 'stg3'], 'w_in_p', piece=1024)
        wkeys = [f"w_in_p_{kc}_{c0}" for kc in range(8) for c0 in range(0, N_IN, 1024)]
        g.op('pe', lambda e: e.nop(), reads=wkeys, writes=['w_in'])
        g.op('pool', lambda e: e.memset(vaug[:], 1.0), [], ['vaug'])
        g.op('pool', lambda e: e.memset(vbug[:], 1.0), [], ['vbug'])

        def kside_ingest(kind, t, src):
            pre = kind + '_'
            if src.get('ka') is not None:
                ap, key = src['ka']
                g.cp('act', kaf[:], ap, [key], ['kaf'])
                g.cp('dve', kab[:], kaf[:], ['kaf'], ['kab'])
                headT(g, kab, 'kab', 8, tp, idb, kaT, 'kaT')
                g.dma('sp', P[pre + 'KaT'][t], kaT[:].rearrange("p a b -> p (a b)"), ['kaT'], ['kaT'])
                ap, key = src['va']
                g.cp('act', vaf[:], ap, [key], ['vaf'])
                g.cp('dve', vaug[:, :, 0:64], vaf[:].rearrange("p (h d) -> p h d", h=8), ['vaf'], ['vaug'])
                g.dma('sp', P[pre + 'Va'][t], vaug[:].rearrange("p a b -> p (a b)"), ['vaug'], ['vaug'])
                ap, key = src['ki']
                g.cp('act', kif[:], ap, [key], ['kif'])
                g.cp('dve', kib[:], kif[:], ['kif'], ['kib'])
                g.tr(tp[0:64, 0:128], kib[:], idb[:], ['kib', 'idb'], ['tp'])
                g.cp('act', kiT[:, 0, :], tp[0:64, 0:128], ['tp'], ['kiT'])
                g.dma('sp', P[pre + 'kiT'][:, t * 128:(t + 1) * 128], kiT[:, 0, :], ['kiT'], ['kiT'])
            if src.get('kb') is not None:
                tb = src['tb']
                ap, key = src['kb']
                g.cp('act', kbf[:], ap, [key], ['kbf'])
                g.cp('dve', kbb[:], kbf[:], ['kbf'], ['kbb'])
                headT(g, kbb, 'kbb', 8, tp, idb, kbT, 'kbT')
                g.dma('sp', P[pre + 'KbT'][tb], kbT[:].rearrange("p a b -> p (a b)"), ['kbT'], ['kbT'])
                ap, key = src['vb']
                g.cp('act', vbf[:], ap, [key], ['vbf'])
                g.cp('dve', vbug[:, :, 0:64], vbf[:].rearrange("p (h d) -> p h d", h=8), ['vbf'], ['vbug'])
                g.dma('sp', P[pre + 'Vb'][tb], vbug[:].rearrange("p a b -> p (a b)"), ['vbug'], ['vbug'])

        def ld(t):
            r0 = _hrow(t * 128, S)
            srcap = P['h_all'][r0:r0 + 128, :] if t < NT else P['h_samp']
            g.dma('sp', bufs['hf'][t % 2][:], srcap, [], [f"hf{t % 2}"])
        ld(0)
        for t in range(NT + 1):
            groups = [(pb[0][:], 'pb0', C_KA, 512), (pb[1][:], 'pb1', C_VA, 512), (pb[2][:, 0:64], 'pb2', C_KI, 64),
                      (pb[3][:], 'pb3', C_KB, 512), (pb[4][:], 'pb4', C_VB, 512)]
            proj_tile(g, t % 2, bufs, w_b, groups)
            if t + 1 <= NT:
                ld(t + 1)
            src = dict(ka=(pb[0][:], 'pb0'), va=(pb[1][:], 'pb1'), ki=(pb[2][:, 0:64], 'pb2'),
                       kb=(pb[3][:], 'pb3'), vb=(pb[4][:], 'pb4'))
            if t < NT:
                src['tb'] = t
                kside_ingest('p', t, src)
                g.dma('pool', P['kA_out'][t * 128:(t + 1) * 128, :], kaf[:], ['kaf'], ['kaf'])
                g.dma('pool', P['vA_out'][t * 128:(t + 1) * 128, :], vaf[:], ['vaf'], ['vaf'])
                g.dma('pool', P['kidx_out'][t * 128:(t + 1) * 128, :], kif[:], ['kif'], ['kif'])
                if t >= NT - 4:
                    o = (t - (NT - 4)) * 128
                    g.dma('pool', P['kB_out'][o:o + 128, :], kbf[:], ['kbf'], ['kbf'])
                    g.dma('pool', P['vB_out'][o:o + 128, :], vbf[:], ['vbf'], ['vbf'])
            else:
                src['tb'] = 4
                kside_ingest('s', 8, src)
                g.dma('pool', P['skA'], kaf[:], ['kaf'], ['kaf'])
                g.dma('pool', P['svA'], vaf[:], ['vaf'], ['vaf'])
                g.dma('pool', P['skidx'], kif[:], ['kif'], ['kif'])
                g.dma('pool', P['skB'][448:512, :], kbf[0:64, :], ['kbf'], ['kbf'])
                g.dma('pool', P['svB'][448:512, :], vbf[0:64, :], ['vbf'], ['vbf'])
        if 'roll' in P['bparts']:
            g.dma('pool', P['skB'][0:448, :], P['c_kb'][64:512, :], [], ['skB_roll'])
            g.dma('pool', P['svB'][0:448, :], P['c_vb'][64:512, :], [], ['svB_roll'])
        for t in range(8 if 'cache' in P['bparts'] else 0):
            g.dma('sp', kaf[:], P['c_ka'][t * 128:(t + 1) * 128, :], [], ['kaf'])
            g.dma('sp', vaf[:], P['c_va'][t * 128:(t + 1) * 128, :], [], ['vaf'])
            g.dma('sp', kif[:], P['c_ki'][t * 128:(t + 1) * 128, :], [], ['kif'])
            g.cp('dve', kab[:], kaf[:], ['kaf'], ['kab'])
            headT(g, kab, 'kab', 8, tp, idb, kaT, 'kaT')
            g.dma('sp', P['s_KaT'][t], kaT[:].rearrange("p a b -> p (a b)"), ['kaT'], ['kaT'])
            g.cp('pool', vaug[:, :, 0:64], vaf[:].rearrange("p (h d) -> p h d", h=8), ['vaf'], ['vaug'])
            g.dma('sp', P['s_Va'][t], vaug[:].rearrange("p a b -> p (a b)"), ['vaug'], ['vaug'])
            g.cp('dve', kib[:], kif[:], ['kif'], ['kib'])
            g.tr(tp[0:64, 0:128], kib[:], idb[:], ['kib', 'idb'], ['tp'])
            g.cp('act', kiT[:, 0, :], tp[0:64, 0:128], ['tp'], ['kiT'])
            g.dma('sp', P['s_kiT'][:, t * 128:(t + 1) * 128], kiT[:, 0, :], ['kiT'], ['kiT'])
        for t in range(4 if 'cache' in P['bparts'] else 0):
            g.dma('sp', kbf[:], P['c_kb'][t * 128:(t + 1) * 128, :], [], ['kbf'])
            g.dma('sp', vbf[:], P['c_vb'][t * 128:(t + 1) * 128, :], [], ['vbf'])
            g.cp('dve', kbb[:], kbf[:], ['kbf'], ['kbb'])
            headT(g, kbb, 'kbb', 8, tp, idb, kbT, 'kbT')
            g.dma('sp', P['s_KbT'][t], kbT[:].rearrange("p a b -> p (a b)"), ['kbT'], ['kbT'])
            g.cp('pool', vbug[:, :, 0:64], vbf[:].rearrange("p (h d) -> p h d", h=8), ['vbf'], ['vbug'])
            g.dma('sp', P['s_Vb'][t], vbug[:].rearrange("p a b -> p (a b)"), ['vbug'], ['vbug'])

        def ldq(i):
            hf = bufs['hf'][i % 2]
            if i == NQ:
                g.dma('pool', hf[:], P['h_samp'], [], [f"hf{i % 2}"])
                return
            g.op('pool', lambda e: e.indirect_dma_start(
                out=hf[:], out_offset=None, in_=P['h_all'],
                in_offset=bass.IndirectOffsetOnAxis(ap=idx_sb[:, i:i + 1], axis=0)),
                reads=['idx'], writes=[f"hf{i % 2}"], dma=True)
        nb2 = NQ + 1 if 'b2' in P['bparts'] else 0
        if nb2:
            ldq(0)
        for i in range(nb2):
            groups = [(pb[0][:], 'pb0', C_QA, 512), (pb[1][:, 0:256], 'pb1', C_QI, 256),
                      (pb[1][:, 256:260], 'pb1', C_WI, 4), (pb[2][:], 'pb2', C_QB, 512),
                      (pb[3][:], 'pb3', C_GA, 512), (pb[4][:], 'pb4', C_GA + 512, 512),
                      (pb[5][:], 'pb5', C_GB, 512), (pb[6][:], 'pb6', C_GB + 512, 512)]
            proj_tile(g, i % 2, bufs, w_b, groups)
            if i + 1 < nb2:
                ldq(i + 1)
            g.cp('dve', qab[:], pb[0][:], ['pb0'], ['qab'])
            headT(g, qab, 'qab', 8, tp, idb, qaT, 'qaT')
            g.dma('sp', P['q_qaT'][i], qaT[:].rearrange("p a b -> p (a b)"), ['qaT'], ['qaT'])
            g.cp('dve', qib[:], pb[1][:, 0:256], ['pb1'], ['qib'])
            g.cp('dve', wif[:], pb[1][:, 256:260], ['pb1'], ['wif'])
            headT(g, qib, 'qib', 4, tp, idb, qiT, 'qiT')
            g.dma('sp', P['q_qiT'][i], qiT[:].rearrange("p a b -> p (a b)"), ['qiT'], ['qiT'])
            g.dma('sp', P['q_wi'][i], wif[:], ['wif'], ['wif'])
            g.cp('dve', qbb[:], pb[2][:], ['pb2'], ['qbb'])
            headT(g, qbb, 'qbb', 8, tp, idb, qbT, 'qbT')
            g.dma('sp', P['q_qbT'][i], qbT[:].rearrange("p a b -> p (a b)"), ['qbT'], ['qbT'])
            for hh in range(2):
                g.act(sga[:, hh * 512:(hh + 1) * 512], pb[3 + hh][:], AF.Tanh, [f"pb{3 + hh}"], ['sga'], scale=0.5)
                g.act(sgb[:, hh * 512:(hh + 1) * 512], pb[5 + hh][:], AF.Tanh, [f"pb{5 + hh}"], ['sgb'], scale=0.5)
            g.ts('pool', sga[:], sga[:], 1.0, 0.5, ALU.add, ALU.mult, ['sga'], ['sga'])
            g.ts('pool', sgb[:], sgb[:], 1.0, 0.5, ALU.add, ALU.mult, ['sgb'], ['sgb'])
            g.dma('sp', P['q_sga'][i], sga[:], ['sga'], ['sga'])
            g.dma('sp', P['q_sgb'][i], sgb[:], ['sgb'], ['sgb'])
        g.es = es_outer
        g.emit()


def attn_core(g, bufs, kts, kt_src, vt_src, qT, qkey, table, tslot_fn, mask_fn, outT, out_key, mid_at=None,
              mid_fn=None, defer_norm=False):
    S_ps, O_ps, bc_ps = bufs['S_ps'], bufs['O_ps'], bufs['bc_ps']
    kt_t, vt_t, Sb, Pb = bufs['kt_t'], bufs['vt_t'], bufs['Sb'], bufs['Pb']
    oT, rden, ones, irep = bufs['oT'], bufs['rden'], bufs['ones'], bufs['irep']
    NBUF = len(kt_t)
    NS = len(S_ps)
    n = len(kts)
    items = [(p, hg) for p in range(n) for hg in range(2)]

    def load(p):
        b = p % NBUF
        g.dma('sp', kt_t[b][:].rearrange("p a b -> p (a b)"), kt_src(kts[p]), [], [f"kt{b}"])
        g.dma('sp', vt_t[b][:], vt_src(kts[p]), [], [f"vt{b}"])

    def stage_s(j):
        p, hg = items[j]
        kt = kts[p]
        b = p % NBUF
        sp_ = S_ps[j % NS]
        sk = f"S{j % NS}"
        mk = mask_fn(kt) if mask_fn is not None else None
        if mk is not None:
            g.mm(sp_[:, 0:512], mk[0], irep[:], True, False, [mk[1], 'irep'], [sk], skip=True)
        for hh in range(4):
            h = hg * 4 + hh
            g.mm(sp_[:, hh * 128:(hh + 1) * 128], kt_t[b][:, h, :], qT[:, h, :], mk is None and hh == 0, True,
                 [f"kt{b}", qkey], [sk], skip=True)

    def stage_e(j):
        p, hg = items[j]
        slot = tslot_fn(kts[p])
        sp_ = S_ps[j % NS]
        sk = f"S{j % NS}"
        pk = f"P{j % NS}"
        if slot is not None:
            sb_ = Sb[j % 2]
            g.stt(sb_[:], sp_[:].rearrange("p (a b) -> p a b", a=4), 0.125,
                  table[:, slot, hg * 4:(hg + 1) * 4, :], ALU.mult, ALU.add, [sk, 'table'], [f"Sb{j % 2}"])
            g.act(Pb[j % NS][:], sb_[:], AF.Exp, [f"Sb{j % 2}"], [pk])
        else:
            g.act(Pb[j % NS][:], sp_[:].rearrange("p (a b) -> p a b", a=4), AF.Exp, [sk], [pk], scale=0.125)

    def stage_v(j):
        p, hg = items[j]
        b = p % NBUF
        for hh in range(4):
            h = hg * 4 + hh
            g.mm(O_ps[hg][0:65, hh * 128:(hh + 1) * 128], vt_t[b][:, h * 65:(h + 1) * 65], Pb[j % NS][:, hh, :],
                 p == 0 and hh == 0, p == n - 1, [f"vt{b}", f"P{j % NS}"], [f"O{hg}"], skip=True)

    for p in range(min(NBUF - 1, n)):
        load(p)
    stage_s(0)
    if len(items) > 1:
        stage_s(1)
    mid_done = False
    for j in range(len(items)):
        p, hg = items[j]
        if hg == 0 and p + NBUF - 1 < n:
            load(p + NBUF - 1)
        stage_e(j)
        if j + 2 < len(items):
            stage_s(j + 2)
        stage_v(j)
        if mid_fn is not None and not mid_done and j + 1 >= min(mid_at, len(items)):
            mid_fn()
            mid_done = True
    for hg in range(2):
        g.cp('act', oT[0:65, hg * 512:(hg + 1) * 512], O_ps[hg][0:65, :], [f"O{hg}"], [f"oT{hg}"])

    def norm():
        g.op('dve', lambda e: e.reciprocal(out=rden[64:65, :], in_=oT[64:65, :]), ['oT0', 'oT1'], ['rden'])
        for hg in range(2):
            g.mm(bc_ps[hg][0:64, :], ones[64:65, 0:64], rden[64:65, hg * 512:(hg + 1) * 512], True, True,
                 ['ones', 'rden'], [f"bc{hg}"])
            g.tt('dve', outT[:, hg * 4:(hg + 1) * 4, :],
                 oT[0:64, hg * 512:(hg + 1) * 512].rearrange("p (a b) -> p a b", a=4),
                 bc_ps[hg][0:64, :].rearrange("p (a b) -> p a b", a=4), ALU.mult, [f"oT{hg}", f"bc{hg}"], [out_key])
    if defer_norm:
        return norm
    norm()
    return None


def attn_bufs(g, idb, n_s=4):
    bufs = dict(
        S_ps=[g.ps([128, 512], F32) for _ in range(n_s)], O_ps=[g.ps([128, 512], F32) for _ in range(2)],
        bc_ps=[g.ps([128, 512], F32) for _ in range(2)],
        kt_t=[g.sb([64, 8, 128], BF16) for _ in range(4)], vt_t=[g.sb([128, 520], BF16) for _ in range(4)],
        Sb=[g.sb([128, 4, 128], F32) for _ in range(2)], Pb=[g.sb([128, 4, 128], BF16) for _ in range(4)],
        oT=g.sb([128, 1024], F32), rden=g.sb([128, 1024], F32), ones=g.sb([128, 64], F32),
        irep=g.sb([128, 512], BF16))
    return bufs


def attn_consts(g, bufs, idb):
    g.op('pool', lambda e: e.memset(bufs['ones'][:], 1.0), [], ['ones'])
    for r4 in range(4):
        g.cp('pool', bufs['irep'][:, r4 * 128:(r4 + 1) * 128], idb[:], ['idb'], ['irep'])


def phase_c1(g, es_outer, P):
    S, NT, NQ, NB, KSEL = P['S'], P['NT'], P['NQ'], P['NB'], P['KSEL']
    NMAX = max(NT * 128, 9 * 128)
    with ExitStack() as es:
        g.es = es
        idf = g.sb([128, 128], F32); idb = g.sb([128, 128], BF16)
        bufs = attn_bufs(g, idb)
        score = [g.sb([128, NMAX], F32) for _ in range(2)]
        junk = [g.sb([128, NMAX], BF16) for _ in range(2)]
        table = g.sb([128, 3, 8, 128], F32)
        c15 = g.sb([128, 8], F32)
        admneg = g.sb([128, 256], F32)
        pow2 = g.sb([128, NB], F32)
        Dh = g.sb([128, 4, 128], BF16)
        qaT = g.sb([64, 8, 128], BF16); qiT = g.sb([64, 4, 128], BF16); wif = g.sb([128, 4], F32)
        kiT = [g.sb([64, 512], BF16) for _ in range(2)]
        R = g.sb([128, 4, 512], BF16)
        oaT = g.sb([64, 8, 128], BF16)
        sm = {k: g.sb([128, 1], F32) for k in ['mn', 'mx', 'lo', 'w0', 't', 'cnt', 'inc']}
        tz = {k: g.sb([128, 1], F32) for k in ['cpos', 'cnn', 'a', 'b', 'tie', 'r', 'nt', 'thr', 'carry']}
        CH = 2048
        zc = g.sb([128, CH], BF16)
        cum = g.sb([128, CH], F32)
        onesb = g.sb([128, CH], BF16)
        hw = g.sb([128, NB], F32)
        sc_ps = bufs['O_ps'][0]
        dots = [bufs['S_ps'][0], bufs['S_ps'][1], bufs['S_ps'][2], bufs['S_ps'][3]]
        dkeys = ['S0', 'S1', 'S2', 'S3']

        g.dma('sp', idf[:], P['ident'], [], ['idf'])
        g.cp('dve', idb[:], idf[:], ['idf'], ['idb'])
        g.dma('sp', pow2[:], P['pow2'].to_broadcast([128, NB]), [], ['pow2'])
        g.dma('sp', c15[:], P['c15'].to_broadcast([128, 8]), [], ['c15'])
        attn_consts(g, bufs, idb)
        g.op('pool', lambda e: e.memset(onesb[:], 1.0), [], ['onesb'])

        def geom(i):
            samp = (i == NQ)
            nkt = 9 if samp else 2 * i + 2
            return samp, nkt, nkt * 128

        def load_tables(ti, which):
            if which == 'table':
                g.dma('sp', table[:].rearrange("p a b c -> p (a b c)"), P['tdsa'][ti], [], ['table'])
                for j in range(3):
                    g.tt('pool', table[:, j, :, :], table[:, j, :, :], c15[:].unsqueeze(2).to_broadcast([128, 8, 128]),
                         ALU.subtract, ['table', 'c15'], ['table'])
            else:
                g.dma('sp', admneg[:], P['admneg'][ti], [], ['admneg'])

        def IDX(i):
            samp, nkt, N = geom(i)
            sc, sck = score[i % 2], f"score{i % 2}"
            kis = P['s_kiT'] if samp else P['p_kiT']
            g.dma('sp', qiT[:].rearrange("p a b -> p (a b)"), P['q_qiT'][i], [], ['qiT'])
            g.dma('sp', wif[:], P['q_wi'][i], [], ['wif'])
            for h in range(4):
                g.ts('pool', Dh[:, h, :], idf[:], wif[:, h:h + 1], 1.0 / 16.0, ALU.mult, ALU.mult, ['idf', 'wif'], ['Dh'])
            ngr = (nkt + 3) // 4
            for gi in range(ngr):
                n = min(512, N - gi * 512)
                b = gi % 2
                g.dma('sp', kiT[b][:, 0:n], kis[:, gi * 512:gi * 512 + n], [], [f"kiT{b}"])
                for h in range(4):
                    g.mm(dots[h][:, 0:n], qiT[:, h, :], kiT[b][:, 0:n], True, True, [f"kiT{b}", 'qiT'], [dkeys[h]])
                for h in range(4):
                    g.act(R[:, h, 0:n], dots[h][:, 0:n], AF.Relu, [dkeys[h]], [f"R{h}"])
                for h in range(4):
                    g.mm(sc_ps[:, 0:n], Dh[:, h, :], R[:, h, 0:n], h == 0, h == 3, ['Dh', f"R{h}"], ['O0'])
                g.cp('act', sc[:, gi * 512:gi * 512 + n], sc_ps[:, 0:n], ['O0'], [sck])

        def BIS(i):
            samp, nkt, N = geom(i)
            sc, sck = score[i % 2], f"score{i % 2}"
            jk, jkk = junk[i % 2], f"junk{i % 2}"
            if i == 0 or samp:
                load_tables(1 if samp else 0, 'admneg')
            g.op('dve', lambda e: e.tensor_reduce(out=sm['mn'][:], in_=sc[:, 0:N], axis=AX.X, op=ALU.min),
                 [sck], ['mn'])
            g.op('dve', lambda e: e.tensor_reduce(out=sm['mx'][:], in_=sc[:, 0:N], axis=AX.X, op=ALU.max),
                 [sck], ['mx'])
            g.tt('dve', sc[:, N - 256:N], sc[:, N - 256:N], admneg[:], ALU.add, [sck, 'admneg', 'mn', 'mx'], [sck])
            g.ts('dve', sm['lo'][:], sm['mn'][:], -1.0, None, ALU.add, None, ['mn'], ['lo'])
            g.tt('dve', sm['w0'][:], sm['mx'][:], sm['lo'][:], ALU.subtract, ['mx', 'lo'], ['w0'])
            g.ts('dve', hw[:], pow2[:], sm['w0'][:], None, ALU.mult, None, ['pow2', 'w0'], ['hw'])
            ksel = (min(TOPK, (PAST + 64) // 4) if samp else KSEL)
            for k in range(NB):
                g.tt('dve', sm['t'][:], sm['lo'][:], hw[:, k:k + 1], ALU.add, ['lo', 'hw'], ['t'])
                g.ts('dve', jk[:, 0:N], sc[:, 0:N], sm['t'][:], 0.0, ALU.is_gt, ALU.add, [sck, 't'],
                     [jkk, 'cnt'], accum_out=sm['cnt'][:])
                g.ts('dve', sm['inc'][:], sm['cnt'][:], ksel - 0.5, None, ALU.is_gt, None, ['cnt'], ['inc'])
                g.stt(sm['lo'][:], sm['inc'][:], hw[:, k:k + 1], sm['lo'][:], ALU.mult, ALU.add,
                      ['inc', 'hw', 'lo'], ['lo'])
            kf = float(ksel)
            g.ts('dve', jk[:, 0:N], sc[:, 0:N], 0.0, 0.0, ALU.is_gt, ALU.add, [sck], [jkk, 'cpos'],
                 accum_out=tz['cpos'][:])
            g.ts('dve', jk[:, 0:N], sc[:, 0:N], 0.0, 0.0, ALU.is_ge, ALU.add, [sck, 'cpos'], [jkk, 'cnn'],
                 accum_out=tz['cnn'][:])
            g.ts('dve', tz['a'][:], tz['cpos'][:], kf - 0.5, None, ALU.is_lt, None, ['cpos'], ['tz_a'])
            g.ts('dve', tz['b'][:], tz['cnn'][:], kf + 0.5, None, ALU.is_gt, None, ['cnn'], ['tz_b'])
            g.tt('dve', tz['tie'][:], tz['a'][:], tz['b'][:], ALU.mult, ['tz_a', 'tz_b'], ['tie'])
            g.ts('dve', tz['r'][:], tz['cpos'][:], -1.0, kf, ALU.mult, ALU.add, ['cpos'], ['tz_r0'])
            g.tt('dve', tz['r'][:], tz['r'][:], tz['tie'][:], ALU.mult, ['tz_r0', 'tie'], ['tz_r'])
            g.ts('dve', tz['nt'][:], tz['tie'][:], -1.0, 1.0, ALU.mult, ALU.add, ['tie'], ['tz_nt'])
            g.tt('dve', tz['thr'][:], sm['lo'][:], tz['nt'][:], ALU.mult, ['lo', 'tz_nt'], ['thr'])
            for c0 in range(0, N, CH):
                n = min(CH, N - c0)
                g.ts('dve', zc[:, 0:n], sc[:, c0:c0 + n], 0.0, None, ALU.is_equal, None, [sck], ['zc'])
                init = 0.0 if c0 == 0 else tz['carry'][:]
                g.op('dve', lambda e, n=n, init=init: e.tensor_tensor_scan(
                    out=cum[:, 0:n], data0=onesb[:, 0:n], data1=zc[:, 0:n], initial=init, op0=ALU.mult, op1=ALU.add),
                    ['zc', 'onesb', 'carry'], ['cum'])
                if c0 + n < N:
                    g.cp('dve', tz['carry'][:], cum[:, n - 1:n], ['cum'], ['carry'])
                g.stt(zc[:, 0:n], cum[:, 0:n], tz['r'][:], zc[:, 0:n], ALU.is_le, ALU.mult, ['cum', 'tz_r', 'zc'], ['zc'])
                g.stt(zc[:, 0:n], sc[:, c0:c0 + n], tz['thr'][:], zc[:, 0:n], ALU.is_gt, ALU.max, [sck, 'thr', 'zc'], ['zc'])
                g.ts('dve', jk[:, c0:c0 + n], zc[:, 0:n], -1.0, -MASKNEG, ALU.add, ALU.mult, ['zc'], [jkk])
            if 'dbg_score' in P:
                g.dma('sp', P['dbg_score'][i, :, 0:N], sc[:, 0:N], [sck], ['dbg'])
                g.dma('sp', P['dbg_junk'][i, :, 0:N], jk[:, 0:N], [jkk], ['dbg'])

        def ATT(i):
            samp, nkt, N = geom(i)
            jk, jkk = junk[i % 2], f"junk{i % 2}"
            if i == 0 or samp:
                load_tables(1 if samp else 0, 'table')
            g.dma('sp', qaT[:].rearrange("p a b -> p (a b)"), P['q_qaT'][i], [], ['qaT'])
            kas = P['s_KaT'] if samp else P['p_KaT']
            vas = P['s_Va'] if samp else P['p_Va']
            near = [kt for kt in range(nkt - 3, nkt) if kt >= 0]
            order = near + [kt for kt in range(nkt) if kt not in near]
            return attn_core(g, bufs, order, lambda kt: kas[kt], lambda kt: vas[kt], qaT, 'qaT', table,
                             lambda kt: (kt - (nkt - 3)) if kt >= nkt - 3 else None,
                             lambda kt: (jk[:, kt * 128:(kt + 1) * 128], jkk), oaT, 'oaT',
                             mid_at=2 * len(near), mid_fn=(lambda: BIS(i + 1)) if i + 1 <= NQ else None,
                             defer_norm=True)

        IDX(0)
        BIS(0)
        if NQ >= 1:
            IDX(1)
        for i in range(NQ + 1):
            norm = ATT(i)
            if i + 2 <= NQ:
                IDX(i + 2)
            norm()
            g.dma('sp', P['q_oaT'][i], oaT[:].rearrange("p a b -> p (a b)"), ['oaT'], ['oaT'])
        g.es = es_outer
        g.emit()


def phase_c2(g, es_outer, P):
    S, NT, NQ = P['S'], P['NT'], P['NQ']
    with ExitStack() as es:
        g.es = es
        idf = g.sb([128, 128], F32); idb = g.sb([128, 128], BF16)
        bufs = attn_bufs(g, idb, n_s=2)
        table = g.sb([128, 6, 8, 128], F32)
        idx_sb = g.sb([128, NQ + 1], I32)
        wba = g.sb([64, 8, D], BF16); wbb = g.sb([64, 8, D], BF16); wout = g.sb([128, 8, D], BF16)
        stage = [g.sb([128, 2048], F32) for _ in range(2)]
        gbc = g.sb([128, D], F32); bbc = g.sb([128, D], F32)
        qbT = [g.sb([64, 8, 128], BF16) for _ in range(2)]
        oaT = [g.sb([64, 8, 128], BF16) for _ in range(2)]
        obT = g.sb([64, 8, 128], BF16)
        sga = [g.sb([128, D], F32) for _ in range(2)]
        sgb = [g.sb([128, D], F32) for _ in range(2)]
        hown = [g.sb([128, D], F32) for _ in range(2)]
        mpb = g.sb([128, D], BF16); mixT = g.sb([128, 8, 128], BF16)
        small = [make_small(g) for _ in range(2)]
        y_ps = [g.ps([128, 512], F32) for _ in range(2)]
        tp = bufs['bc_ps'][1][:].bitcast(BF16)

        g.dma('sp', idf[:], P['ident'], [], ['idf'])
        g.cp('dve', idb[:], idf[:], ['idf'], ['idb'])
        g.dma('sp', idx_sb[:], P['own_idx'], [], ['idx'])
        g.dma('sp', gbc[:], P['ln2_g'].to_broadcast([128, D]), [], ['lng'])
        g.dma('sp', bbc[:], P['ln2_b'].to_broadcast([128, D]), [], ['lnb'])
        attn_consts(g, bufs, idb)
        for wsrc, wdst, key in [(P['w_branch_a'], wba, 'wba'), (P['w_branch_b'], wbb, 'wbb')]:
            for h in range(8):
                st = stage[h % 2]
                g.dma('sp', st[0:64, 0:D], wsrc[h * 64:(h + 1) * 64, :], [], [f"stg{h % 2}"])
                g.cp(g.cast_eng(), wdst[:, h, :], st[0:64, 0:D], [f"stg{h % 2}"], [key])
        load_weight_bf16(g, wout, P['w_out'], 8, D, [st[:] for st in stage], ['stg0', 'stg1'], 'wout_p')
        g.op('pe', lambda e: e.nop(), reads=[f"wout_p_{kc}_0" for kc in range(8)], writes=['wout'])

        def loads(i):
            b = i % 2
            g.dma('sp', qbT[b][:].rearrange("p a b -> p (a b)"), P['q_qbT'][i], [], [f"qbT{b}"])
            g.dma('sp', oaT[b][:].rearrange("p a b -> p (a b)"), P['q_oaT'][i], [], [f"oaT{b}"])
            g.dma('sp', sga[b][:], P['q_sga'][i], [], [f"sga{b}"])
            g.dma('sp', sgb[b][:], P['q_sgb'][i], [], [f"sgb{b}"])
            if i == NQ:
                g.dma('pool', hown[b][:], P['h_samp'], [], [f"hown{b}"])
                return
            g.op('pool', lambda e: e.indirect_dma_start(
                out=hown[b][:], out_offset=None, in_=P['h_all'],
                in_offset=bass.IndirectOffsetOnAxis(ap=idx_sb[:, i:i + 1], axis=0)),
                reads=['idx'], writes=[f"hown{b}"], dma=True)

        loads(0)
        for i in range(NQ + 1):
            samp = (i == NQ)
            b = i % 2
            if i == 0 or samp:
                g.dma('sp', table[:].rearrange("p a b c -> p (a b c)"), P['tband'][1 if samp else 0], [], ['table'])
            if samp:
                kts = list(range(5)); base = 0
                kbs, vbs = P['s_KbT'], P['s_Vb']
            else:
                base = 2 * i - 4
                kts = [kt for kt in range(base, 2 * i + 2) if kt >= 0]
                kbs, vbs = P['p_KbT'], P['p_Vb']
            attn_core(g, bufs, kts, lambda kt: kbs[kt], lambda kt: vbs[kt], qbT[b], f"qbT{b}", table,
                      lambda kt, base=base: kt - base, None, obT, 'obT')
            if i + 1 <= NQ:
                loads(i + 1)
            for (oT_, okey, w_, wkey, sg, sgk) in [(oaT[b], f"oaT{b}", wba, 'wba', sga[b], f"sga{b}"),
                                                   (obT, 'obT', wbb, 'wbb', sgb[b], f"sgb{b}")]:
                for hh in range(2):
                    for h in range(8):
                        g.mm(y_ps[hh][:], oT_[:, h, :], w_[:, h, hh * 512:(hh + 1) * 512], h == 0, h == 7,
                             [okey, wkey], [f"y{hh}"])
                    g.tt('dve', sg[:, hh * 512:(hh + 1) * 512], sg[:, hh * 512:(hh + 1) * 512], y_ps[hh][:], ALU.mult,
                         [sgk, f"y{hh}"], [sgk])
            g.tt('pool', mpb[:], sga[b][:], sgb[b][:], ALU.add, [f"sga{b}", f"sgb{b}"], ['mpb'])
            for k in range(8):
                g.tr(tp[:, k * 128:(k + 1) * 128], mpb[:, k * 128:(k + 1) * 128], idb[:], ['mpb', 'idb'], ['bc1'])
            g.cp('act', mixT[:].rearrange("p a b -> p (a b)"), tp[:, 0:1024], ['bc1'], ['mixT'])
            hk = f"hown{b}"
            for hh in range(2):
                for k in range(8):
                    g.mm(y_ps[hh][:], mixT[:, k, :], wout[:, k, hh * 512:(hh + 1) * 512], k == 0, k == 7,
                         ['mixT', 'wout'], [f"y{hh}"])
                hs = hown[b][:, hh * 512:(hh + 1) * 512]
                g.stt(hs, hs, ALPHA, y_ps[hh][:], ALU.mult, ALU.add, [hk, f"y{hh}"], [hk])
            layer_norm_tile(g, hown[b][:], hk, gbc, bbc, small[b], f"ln2{b}")
            g.dma('pool', P['h2_own'][i * 128:(i + 1) * 128, :], hown[b][:], [hk], [hk])
        g.es = es_outer
        g.emit()


def build_program(S, debug=None, nphases=5, bparts=('roll', 'cache', 'b2'), npairs=4, cc_inc=16):
    NT = S // 128
    NQ = S // 256
    NTA = NT + 1
    NQA = NQ + 1
    nc = bass.Bass("TRN2", target_bir_lowering=False)
    dbg = set(debug or [])

    def din(name, shape, dt=F32):
        return nc.dram_tensor(name, list(shape), dt, kind="ExternalInput").ap()

    def dout(name, shape, dt=F32):
        return nc.dram_tensor(name, list(shape), dt, kind="ExternalOutput").ap()

    def dscr(name, shape, dt=F32):
        if name in dbg:
            return dout(name, shape, dt)
        return nc.dram_tensor(name, list(shape), dt, kind="Internal").ap()

    P = dict(S=S, NT=NT, NQ=NQ, NB=N_BISECT, KSEL=min(TOPK, S // 4), bparts=set(bparts), npairs=npairs,
             CHR=min(512, (NT // 2) * 128))
    NH = NT // 2
    P['x_all'] = din("x_all", [(NH + 1) * 128, D])
    P['ident'] = din("ident", [128, 128])
    P['own_idx'] = din("own_idx", [128, NQA], I32)
    for nm, shp in [("ffn1_wi", [D, 2 * FF]), ("ffn1_wo", [FF, D]), ("ffn2_wi", [D, 2 * FF]), ("ffn2_wo", [FF, D]),
                    ("w_in", [D, N_IN]), ("w_branch_a", [512, D]), ("w_branch_b", [512, D]), ("w_out", [D, D]),
                    ("ln1_g", [1, D]), ("ln1_b", [1, D]), ("ln2_g", [1, D]), ("ln2_b", [1, D]),
                    ("ln3_g", [1, D]), ("ln3_b", [1, D]),
                    ("c_ka", [PAST, 512]), ("c_va", [PAST, 512]), ("c_ki", [PAST, 64]),
                    ("c_kb", [BAND, 512]), ("c_vb", [BAND, 512]),
                    ("admneg", [2, 128, 256]), ("tdsa", [2, 128, 3 * 8 * 128]), ("tband", [2, 128, 6 * 8 * 128]),
                    ("c15", [1, 8]), ("pow2", [1, N_BISECT])]:
        P[nm] = din(nm, shp)
    P['h_all'] = dscr("h_all", [NT * 128, D])
    P['h_half'] = dscr("h_half", [NH * 128, D])
    P['h_samp'] = dscr("h_samp", [128, D])
    for pre, n_a, n_b in [('p_', NT, NT), ('s_', 9, 5)]:
        P[pre + 'KaT'] = dscr(pre + "KaT", [n_a, 64, 1024], BF16)
        P[pre + 'Va'] = dscr(pre + "Va", [n_a, 128, 520], BF16)
        P[pre + 'kiT'] = dscr(pre + "kiT", [64, n_a * 128], BF16)
        P[pre + 'KbT'] = dscr(pre + "KbT", [n_b, 64, 1024], BF16)
        P[pre + 'Vb'] = dscr(pre + "Vb", [n_b, 128, 520], BF16)
    P['q_qaT'] = dscr("q_qaT", [NQA, 64, 1024], BF16)
    P['q_qiT'] = dscr("q_qiT", [NQA, 64, 512], BF16)
    P['q_wi'] = dscr("q_wi", [NQA, 128, 4])
    P['q_qbT'] = dscr("q_qbT", [NQA, 64, 1024], BF16)
    P['q_sga'] = dscr("q_sga", [NQA, 128, D])
    P['q_sgb'] = dscr("q_sgb", [NQA, 128, D])
    P['q_oaT'] = dscr("q_oaT", [NQA, 64, 1024], BF16)
    P['h2_own'] = dscr("h2_own", [NQA * 128, D])
    if 'dbg_score' in dbg:
        NMAX = max(NT * 128, 9 * 128)
        P['dbg_score'] = dscr("dbg_score", [NQA, 128, NMAX])
        P['dbg_junk'] = dscr("dbg_junk", [NQA, 128, NMAX], BF16)
    P['y_own'] = dout("y_own", [NQA * 128, D])
    P['kA_out'] = dout("kA_out", [S, 512]); P['vA_out'] = dout("vA_out", [S, 512])
    P['kidx_out'] = dout("kidx_out", [S, 64])
    P['kB_out'] = dout("kB_out", [512, 512]); P['vB_out'] = dout("vB_out", [512, 512])
    P['skA'] = dout("skA", [128, 512]); P['svA'] = dout("svA", [128, 512]); P['skidx'] = dout("skidx", [128, 64])
    P['skB'] = dout("skB", [512, 512]); P['svB'] = dout("svB", [512, 512])

    with ExitStack() as es:
        g = G(nc, es)
        ffn_phase(g, nc, es, P['x_all'],
                  lambda t: P['h_half'][t * 128:(t + 1) * 128, :] if t < NH else P['h_samp'],
                  P['ffn1_wi'], P['ffn1_wo'], P['ln1_g'], P['ln1_b'], P['ident'], NH + 1)
        cc_sem = es.enter_context(nc.semaphore("cc_sem"))
        CHR = P['CHR']
        nch = (NH * 128) // CHR
        with nc.Block() as blk:
            def cc_body(engine):
                for k in range(nch):
                    engine.collective_compute(
                        "AllGather", ALU.bypass, replica_groups=[[0, 1], [2, 3], [4, 5], [6, 7]][:P['npairs']],
                        ins=[P['h_half'][k * CHR:(k + 1) * CHR, :]],
                        outs=[P['h_all'][2 * k * CHR:2 * (k + 1) * CHR, :]]).then_inc(cc_sem)
                engine.wait_ge(cc_sem, nch)

            def wait_body(engine):
                engine.wait_ge(cc_sem, nch)
            blk.gpsimd(cc_body)
            blk.tensor(wait_body)
            blk.scalar(wait_body)
            blk.vector(wait_body)
            blk.sync(wait_body)
        if nphases >= 2:
            phase_b(g, es, P)
        if nphases >= 3:
            phase_c1(g, es, P)
        if nphases >= 4:
            phase_c2(g, es, P)
        if nphases >= 5:
            ffn_phase(g, nc, es, P['h2_own'], P['y_own'], P['ffn2_wi'], P['ffn2_wo'], P['ln3_g'], P['ln3_b'],
                      P['ident'], NQA)
    return nc


def _t5_bucket_np(rel):
    import math
    rel = np.asarray(rel, np.int64)
    half, exact = 16, 8
    n = np.abs(rel)
    lr = np.log(np.maximum(n, 1).astype(np.float32) / np.float32(exact)) / np.float32(math.log(128 / exact))
    large = np.minimum(exact + (lr.astype(np.float32) * np.float32(half - exact)).astype(np.int32), half - 1)
    return (rel > 0).astype(np.int32) * half + np.where(n < exact, n, large)


def _hrow(tok, S):
    half = S // 2
    chr_ = min(512, half)
    tok = np.asarray(tok)
    r, w = tok // half, tok % half
    return (2 * (w // chr_) + r) * chr_ + (w % chr_)


def _qpos(r, i):
    p = np.arange(128)
    return np.where(p < 64, (4 * i + r) * 64 + p, (4 * i + 2 + r) * 64 + (p - 64))


def _tables(r, t5_bias, rel_bias):
    kk = np.arange(128)
    i = 2
    qp = _qpos(r, i)
    adm = np.zeros((2, 128, 256), np.float32)
    kc = (2 * i * 128 + np.arange(256)) // 64
    adm[0] = np.where(kc[None, :] <= (qp // 64)[:, None], 0.0, NEG)
    tdsa = np.zeros((2, 128, 3, 8, 128), np.float32)
    for j in range(3):
        kpos = (2 * i - 1 + j) * 128 + kk
        bk = _t5_bucket_np(kpos[:, None] - qp[None, :])
        tdsa[0, :, j] = np.transpose(t5_bias[:, bk], (1, 0, 2))
    tband = np.full((2, 128, 6, 8, 128), MASKNEG, np.float32)
    for j in range(6):
        kpos = (2 * i - 4 + j) * 128 + kk
        rel = kpos[:, None] - qp[None, :]
        vis = ((kpos // 64)[:, None] <= (qp // 64)[None, :]) & ((kpos // 64)[:, None] >= (qp // 64)[None, :] - 8)
        ridx = np.clip(rel, -128, 63) + 128
        vals = np.transpose(rel_bias[:, ridx], (1, 0, 2))
        tband[0, :, j] = np.where(vis[:, None, :], vals, MASKNEG)
    qs = PAST + np.minimum(np.arange(128), 63)
    cols = 7 * 128 + np.arange(256)
    adm[1] = np.where(cols[None, :] < PAST + 64, 0.0, NEG) + np.zeros((128, 1), np.float32)
    for j in range(3):
        kpos = (6 + j) * 128 + kk
        bk = _t5_bucket_np(kpos[:, None] - qs[None, :])
        tdsa[1, :, j] = np.transpose(t5_bias[:, bk], (1, 0, 2))
    for j in range(5):
        kpos = (PAST - BAND) + j * 128 + kk
        rel = kpos[:, None] - qs[None, :]
        vis = (kpos < PAST + 64)[:, None] & np.ones((1, 128), bool)
        ridx = np.clip(rel, -128, 63) + 128
        vals = np.transpose(rel_bias[:, ridx], (1, 0, 2))
        tband[1, :, j] = np.where(vis[:, None, :], vals, MASKNEG)
    return adm, tdsa.reshape(2, 128, -1), tband.reshape(2, 128, -1)


_PROG = {}


def _prep_inputs(inputs, S, n_cores=8):
    NT, NQ = S // 128, S // 256
    f = lambda a: np.ascontiguousarray(a, dtype=np.float32)
    common = {
        "ident": np.eye(128, dtype=np.float32),
        "ffn1_wi": f(inputs['ffn1_wi'][0]), "ffn1_wo": f(inputs['ffn1_wo'][0]),
        "ffn2_wi": f(inputs['ffn2_wi'][0]), "ffn2_wo": f(inputs['ffn2_wo'][0]),
        "w_in": f(inputs['w_in'][0]), "w_branch_a": f(inputs['w_branch_a'][0]),
        "w_branch_b": f(inputs['w_branch_b'][0]), "w_out": f(inputs['w_out'][0]),
        "ln1_g": f(inputs['ln1_g']), "ln1_b": f(inputs['ln1_b']), "ln2_g": f(inputs['ln2_g']),
        "ln2_b": f(inputs['ln2_b']), "ln3_g": f(inputs['ln3_g']), "ln3_b": f(inputs['ln3_b']),
        "c15": f(np.asarray(inputs['t5_bias'])[:, 15][None, :]),
        "pow2": (0.5 ** np.arange(1, N_BISECT + 1, dtype=np.float64)).astype(np.float32)[None, :],
    }
    tabs = [_tables(r, np.asarray(inputs['t5_bias'], np.float32), np.asarray(inputs['rel_bias_b'][0], np.float32))
            for r in range(2)]
    maps = []
    for c in range(n_cores):
        b, r = c // 2, c % 2
        xs = np.zeros((128, D), np.float32)
        xs[:64] = inputs['x_sample'][c]
        own = np.zeros((128, NQ + 1), np.int32)
        for i in range(NQ):
            own[:, i] = _hrow(_qpos(r, i), S)
        own[:, NQ] = 0
        m = dict(common)
        m.update({
            "x_all": np.concatenate([f(inputs['x_prompt'][b, r * (S // 2):(r + 1) * (S // 2)]), xs], 0),
            "own_idx": own,
            "c_ka": f(inputs['cache_k_a'][0, c]).reshape(PAST, 512),
            "c_va": f(inputs['cache_v_a'][0, c]).reshape(PAST, 512),
            "c_ki": f(inputs['cache_kidx_a'][0, c]),
            "c_kb": f(inputs['cache_k_b'][0, c]).reshape(BAND, 512),
            "c_vb": f(inputs['cache_v_b'][0, c]).reshape(BAND, 512),
            "admneg": tabs[r][0], "tdsa": tabs[r][1], "tband": tabs[r][2],
        })
        maps.append(m)
    return maps


def _assemble(results, S, B=4):
    NQ = S // 256
    keep = min(BAND, S)
    y_prompt = np.zeros((B, S, D), np.float32)
    y_sample = np.zeros((8, 64, D), np.float32)
    nka = np.zeros((1, B, S, 8, 64), np.float32); nva = np.zeros_like(nka)
    nki = np.zeros((1, B, S, 64), np.float32)
    nkb = np.zeros((1, B, keep, 8, 64), np.float32); nvb = np.zeros_like(nkb)
    ska = np.zeros((1, 8, 64, 8, 64), np.float32); sva = np.zeros_like(ska)
    ski = np.zeros((1, 8, 64, 64), np.float32)
    skb = np.zeros((1, 8, BAND, 8, 64), np.float32); svb = np.zeros_like(skb)
    for c, res in enumerate(results):
        b, r = c // 2, c % 2
        yo = res['y_own']
        for i in range(NQ):
            qp = _qpos(r, i)
            y_prompt[b, qp] = yo[i * 128:(i + 1) * 128]
        y_sample[c] = yo[NQ * 128:NQ * 128 + 64]
        if r == 0:
            nka[0, b] = res['kA_out'].reshape(S, 8, 64)
            nva[0, b] = res['vA_out'].reshape(S, 8, 64)
            nki[0, b] = res['kidx_out']
            nkb[0, b] = res['kB_out'].reshape(BAND, 8, 64)[-keep:]
            nvb[0, b] = res['vB_out'].reshape(BAND, 8, 64)[-keep:]
        ska[0, c] = res['skA'][:64].reshape(64, 8, 64)
        sva[0, c] = res['svA'][:64].reshape(64, 8, 64)
        ski[0, c] = res['skidx'][:64]
        skb[0, c] = res['skB'].reshape(BAND, 8, 64)
        svb[0, c] = res['svB'].reshape(BAND, 8, 64)
    return (y_prompt, y_sample, nka, nva, nki, nkb, nvb, ska, sva, ski, skb, svb)


def kernel(**inputs):
    S = int(np.asarray(inputs['x_prompt']).shape[1])
    if S not in _PROG:
        _PROG[S] = build_program(S)
    nc = _PROG[S]
    maps = _prep_inputs(inputs, S)
    res = run_bass_kernel_spmd(nc, maps, core_ids=list(range(8)))
    return _assemble(res.results, S)
```

```python
import numpy as np
import concourse.bass as bass
import concourse.mybir as mybir
from concourse.bass_utils import run_bass_kernel_spmd
from contextlib import ExitStack

F32 = mybir.dt.float32
BF16 = mybir.dt.bfloat16
I32 = mybir.dt.int32
AF = mybir.ActivationFunctionType
ALU = mybir.AluOpType
AX = mybir.AxisListType
ENGS = ['pe', 'act', 'dve', 'pool', 'sp']

D = 1024
FF = 2816
NFF = 22
HD = 64
NH = 8
CHUNK = 64
PAST = 1024
BAND = 512
TOPK = 256
LN_EPS = 1e-5
ALPHA = 2.0 ** 0.25
NEG = -1e30
MASKNEG = -30000.0
N_BISECT = 22
C_QA, C_KA, C_VA, C_QI, C_KI, C_WI, C_QB, C_KB, C_VB, C_GA, C_GB = (
    0, 512, 1024, 1536, 1792, 1856, 1860, 2372, 2884, 3396, 4420)
N_IN = 5444


class G:
    NDS = {'sp': 40, 'pool': 24, 'act': 4, 'pe': 0, 'dve': 0}

    def __init__(self, nc, es):
        self.nc = nc
        self.es = es
        self.ops = {e: [] for e in ENGS}
        self.last_w = {}
        self.readers = {}
        self.nsb = 0
        self.sems = {}
        for e in ENGS:
            self.sems[('c', e)] = es.enter_context(nc.semaphore(f"c_{e}"))
            for j in range(self.NDS[e]):
                self.sems[('d', e, j)] = es.enter_context(nc.semaphore(f"d_{e}_{j}"))
        self.sems[('cc',)] = es.enter_context(nc.semaphore("cc_sem"))
        self.ncc = 0
        self.ccount = {e: 0 for e in ENGS}
        self.dnum = {e: 0 for e in ENGS}
        self.rr = 0

    def sb(self, shape, dt, name=None):
        self.nsb += 1
        return self.es.enter_context(self.nc.sbuf_tensor(name or f"sb{self.nsb}", list(shape), dt))

    def ps(self, shape, dt=F32, name=None):
        self.nsb += 1
        return self.es.enter_context(self.nc.psum_tensor(name or f"ps{self.nsb}", list(shape), dt))

    def op(self, eng, fn, reads=(), writes=(), dma=False):
        idx = len(self.ops[eng])
        deps = set()
        for k in reads:
            w = self.last_w.get(k)
            if w is not None:
                deps.add(w)
        for k in writes:
            w = self.last_w.get(k)
            if w is not None:
                deps.add(w)
            for r in self.readers.get(k, ()):
                deps.add(r)
        deps.discard((eng, idx))
        for k in writes:
            self.last_w[k] = (eng, idx)
            self.readers[k] = []
        for k in reads:
            self.readers.setdefault(k, []).append((eng, idx))
        self.ops[eng].append(dict(fn=fn, deps=deps, dma=dma, signal=False, hval=None, pre=None))
        return (eng, idx)

    def emit(self):
        nc = self.nc
        sems = self.sems
        ops = self.ops
        for e in ENGS:
            for o in ops[e]:
                for (de, di) in o['deps']:
                    d = ops[de][di]
                    if de == 'pe' and e == 'pe' and not d['dma']:
                        continue
                    d['signal'] = True
            for o in reversed(ops[e]):
                if not o['dma']:
                    o['signal'] = True
                    break
        for e in ENGS:
            for o in ops[e]:
                if o['dma'] == 'cc':
                    self.ncc += 1
                    o['hval'] = (('cc',), self.ncc)
                    o['pre'] = None
                elif o['dma']:
                    m = self.dnum[e]
                    self.dnum[e] += 1
                    j, rnd = m % self.NDS[e], m // self.NDS[e]
                    o['hval'] = (('d', e, j), 16 * (rnd + 1))
                    o['pre'] = (('d', e, j), 16 * rnd) if rnd > 0 else None
                elif o['signal']:
                    self.ccount[e] += 1
                    o['hval'] = (('c', e), self.ccount[e])
        ccount = dict(self.ccount)
        dfinal = {}
        for e in ENGS:
            for j in range(self.NDS[e]):
                n_j = (self.dnum[e] - j + self.NDS[e] - 1) // self.NDS[e] if self.dnum[e] > j else 0
                if n_j > 0:
                    dfinal[('d', e, j)] = 16 * n_j
        if self.ncc > 0:
            dfinal[('cc',)] = self.ncc
        with nc.Block() as block:
            def mk(e):
                def body(engine):
                    seen = {}

                    def wait(sk, v):
                        if v <= 0 or seen.get(sk, 0) >= v:
                            return
                        engine.wait_ge(sems[sk], v)
                        seen[sk] = v
                    for o in ops[e]:
                        for (de, di) in sorted(o['deps']):
                            d = ops[de][di]
                            if de == 'pe' and e == 'pe' and not d['dma']:
                                continue
                            wait(*d['hval'])
                        if o['pre'] is not None:
                            wait(*o['pre'])
                        ins = o['fn'](engine)
                        if o['dma'] == 'cc':
                            ins.then_inc(sems[o['hval'][0]])
                        elif o['dma']:
                            ins.then_inc(sems[o['hval'][0]], 16)
                        elif o['signal']:
                            ins.then_inc(sems[o['hval'][0]], 1)
                    for e2 in ENGS:
                        if ccount[e2] > 0:
                            wait(('c', e2), ccount[e2])
                    for sk, v in dfinal.items():
                        wait(sk, v)
                return body
            block.tensor(mk('pe'))
            block.scalar(mk('act'))
            block.vector(mk('dve'))
            block.gpsimd(mk('pool'))
            block.sync(mk('sp'))
        self.ops = {e: [] for e in ENGS}
        self.last_w = {}
        self.readers = {}

    def dma(self, q, out, in_, r, w):
        return self.op(q, lambda e: e.dma_start(out=out, in_=in_), reads=r, writes=w, dma=True)

    def mm(self, out, lhsT, rhs, start, stop, r, w, skip=False):
        return self.op('pe', lambda e: e.matmul(out=out, lhsT=lhsT, rhs=rhs, start=start, stop=stop,
                                                skip_group_check=skip), reads=r, writes=w)

    def tr(self, out, in_, ident, r, w):
        return self.op('pe', lambda e: e.transpose(out=out, in_=in_, identity=ident), reads=r, writes=w)

    def act(self, out, in_, func, r, w, scale=None, bias=None, accum_out=None):
        kw = {}
        if scale is not None:
            kw['scale'] = scale
        if bias is not None:
            kw['bias'] = bias
        if accum_out is not None:
            kw['accum_out'] = accum_out
        return self.op('act', lambda e: e.activation(out=out, in_=in_, func=func, **kw), reads=r, writes=w)

    def ts(self, eng, out, in0, s1, s2, op0, op1, r, w, accum_out=None):
        kw = {}
        if op1 is not None:
            kw['op1'] = op1
        if accum_out is not None:
            kw['accum_out'] = accum_out
        return self.op(eng, lambda e: e.tensor_scalar(out=out, in0=in0, scalar1=s1, scalar2=s2, op0=op0, **kw),
                       reads=r, writes=w)

    def tt(self, eng, out, in0, in1, op, r, w):
        return self.op(eng, lambda e: e.tensor_tensor(out=out, in0=in0, in1=in1, op=op), reads=r, writes=w)

    def stt(self, out, in0, scalar, in1, op0, op1, r, w):
        return self.op('dve', lambda e: e.scalar_tensor_tensor(out=out, in0=in0, scalar=scalar, in1=in1,
                                                               op0=op0, op1=op1), reads=r, writes=w)

    def cp(self, eng, out, in_, r, w):
        if eng == 'act':
            return self.op('act', lambda e: e.activation(out=out, in_=in_, func=AF.Copy), reads=r, writes=w)
        return self.op(eng, lambda e: e.tensor_copy(out=out, in_=in_), reads=r, writes=w)

    def cast_eng(self):
        self.rr += 1
        return ['dve', 'act', 'dve'][self.rr % 3]


def load_weight_bf16(g, dst, src, kchunks, ncols, stage, stage_keys, dst_key, piece=2048):
    i = 0
    for kc in range(kchunks):
        for c0 in range(0, ncols, piece):
            n = min(piece, ncols - c0)
            st, sk = stage[i % len(stage)], stage_keys[i % len(stage)]
            i += 1
            g.dma('sp', st[:, 0:n], src[kc * 128:(kc + 1) * 128, c0:c0 + n], [], [sk])
            g.cp(g.cast_eng(), dst[:, kc, c0:c0 + n], st[:, 0:n], [sk], [dst_key + f"_{kc}_{c0}"])


def rsqrt_newton(g, var_ap, rs, small, rkeys, tag):
    v, ti, ui, b, t = small['v'], small['ti'], small['ui'], small['b'], small['t']
    g.ts('dve', v[:], var_ap, LN_EPS, None, ALU.add, None, rkeys, [tag + 'v'])
    g.ts('dve', ti[:], v[:].bitcast(I32), 1, None, ALU.arith_shift_right, None, [tag + 'v'], [tag + 'ti'])
    g.ts('dve', ui[:], ti[:], -1.0, 1597463007.0, ALU.mult, ALU.add, [tag + 'ti'], [tag + 'y'])
    y = ui[:].bitcast(F32)
    for it in range(3):
        g.stt(b[:], y, v[:], y, ALU.mult, ALU.mult, [tag + 'y', tag + 'v'], [tag + 'b'])
        g.stt(t[:], b[:], -0.5, y, ALU.mult, ALU.mult, [tag + 'b', tag + 'y'], [tag + 't'])
        dst = rs[:] if it == 2 else y
        g.stt(dst, y, 1.5, t[:], ALU.mult, ALU.add, [tag + 'y', tag + 't'], [tag + ('rs' if it == 2 else 'y')])


def make_small(g):
    return dict(st=g.sb([128, 2, 6], F32), mv=g.sb([128, 2], F32), rs=g.sb([128, 1], F32),
                nb=g.sb([128, 1], F32), v=g.sb([128, 1], F32), ti=g.sb([128, 1], I32), ui=g.sb([128, 1], I32),
                b=g.sb([128, 1], F32), t=g.sb([128, 1], F32))


def layer_norm_tile(g, y, ykey, gbc, bbc, small, tag):
    st, mv, rs, nb = small['st'], small['mv'], small['rs'], small['nb']
    for hh in range(2):
        g.op('dve', lambda e, hh=hh: e.bn_stats(out=st[:, hh, :], in_=y[:, hh * 512:(hh + 1) * 512]),
             reads=[ykey], writes=[tag + 'st'])
    g.op('dve', lambda e: e.bn_aggr(out=mv[:], in_=st[:].rearrange('p a b -> p (a b)')), reads=[tag + 'st'], writes=[tag + 'mv'])
    rsqrt_newton(g, mv[:, 1:2], rs, small, [tag + 'mv'], tag)
    g.stt(nb[:], mv[:, 0:1], -1.0, rs[:], ALU.mult, ALU.mult, [tag + 'mv', tag + 'rs'], [tag + 'nb'])
    g.act(y, y, AF.Identity, [ykey, tag + 'rs', tag + 'nb'], [ykey], scale=rs[:], bias=nb[:])
    g.tt('dve', y, y, gbc[:], ALU.mult, [ykey, 'lng'], [ykey])
    g.tt('pool', y, y, bbc[:], ALU.add, [ykey, 'lnb'], [ykey])


def ffn_phase(g, nc, es_outer, src, dst, wi, wo, lng, lnb, ident_d, ntiles, T=4, after_tile=None):
    TN = T * 128
    with ExitStack() as es:
        g.es = es
        wi_b = g.sb([128, 8, 2 * FF], BF16)
        wo_b = g.sb([128, NFF, D], BF16)
        xin = g.sb([128, T, D], F32)
        xb = g.sb([128, D], BF16)
        xT = g.sb([128, 8, TN], BF16)
        gT = g.sb([128, NFF, TN], BF16)
        sil = g.sb([128, TN], F32)
        gbc = g.sb([128, D], F32)
        bbc = g.sb([128, D], F32)
        idf = g.sb([128, 128], F32)
        idb = g.sb([128, 128], BF16)
        small = [make_small(g) for _ in range(2)]
        up = [g.ps([128, 512], F32) for _ in range(4)]
        dn = [g.ps([128, 512], F32) for _ in range(2)]
        tp = g.ps([128, 8, 128], BF16)

        g.dma('sp', idf[:], ident_d, [], ['idf'])
        g.cp('dve', idb[:], idf[:], ['idf'], ['idb'])
        g.dma('sp', gbc[:], lng.to_broadcast([128, D]), [], ['lng'])
        g.dma('sp', bbc[:], lnb.to_broadcast([128, D]), [], ['lnb'])
        xflat = xin[:].rearrange("p a b -> p (a b)")
        stages = [xflat[:, q * 1024:(q + 1) * 1024] for q in range(T)]
        skeys = [f"xin_t{q}" for q in range(T)]
        load_weight_bf16(g, wi_b, wi, 8, 2 * FF, stages, skeys, 'wi', piece=1024)
        load_weight_bf16(g, wo_b, wo, NFF, D, stages, skeys, 'wo', piece=1024)
        wi_keys = [f"wi_{kc}_{c0}" for kc in range(8) for c0 in range(0, 2 * FF, 1024)]
        wo_keys = [f"wo_{kc}_0" for kc in range(NFF)]
        g.op('pool', lambda e: e.memset(sil[:, 0:1], 0.0), reads=wi_keys + wo_keys,
             writes=[f"xin_t{j}" for j in range(T)] + ['sil'])

        nst = (ntiles + T - 1) // T
        for st_i in range(nst):
            t0 = st_i * T
            Tc = min(T, ntiles - t0)
            Tn = Tc * 128
            for j in range(Tc):
                xk = f"xin_t{j}"
                g.dma('sp', xin[:, j, :], src[(t0 + j) * 128:(t0 + j + 1) * 128, :], [], [xk])
            for j in range(Tc):
                xk = f"xin_t{j}"
                g.cp('pool', xb[:], xin[:, j, :], [xk], ['xb'])
                g.act(xin[:, j, :], xin[:, j, :], AF.Copy, [xk, 'xb'], [xk], scale=ALPHA)
                for k in range(8):
                    g.tr(tp[:, k, :], xb[:, k * 128:(k + 1) * 128], idb[:], ['xb', 'idb'], ['tp'])
                g.cp('dve', xT[:, :, j * 128:(j + 1) * 128], tp[:], ['tp'], [f"xT{j}"])
            xTk = [f"xT{j}" for j in range(Tc)]
            for c in range(NFF):
                pa, pu = up[2 * (c % 2)], up[2 * (c % 2) + 1]
                ka, ku = f"up{2 * (c % 2)}", f"up{2 * (c % 2) + 1}"
                for k in range(8):
                    g.mm(pa[:, 0:Tn], wi_b[:, k, c * 128:(c + 1) * 128], xT[:, k, 0:Tn], k == 0, k == 7,
                         xTk + (wi_keys if st_i == 0 else []), [ka])
                for k in range(8):
                    g.mm(pu[:, 0:Tn], wi_b[:, k, FF + c * 128:FF + (c + 1) * 128], xT[:, k, 0:Tn],
                         k == 0, k == 7, xTk, [ku])
                g.act(sil[:, 0:Tn], pa[:, 0:Tn], AF.Tanh, [ka], ['sil'], scale=0.5)
                g.stt(sil[:, 0:Tn], sil[:, 0:Tn], 1.0, pa[:, 0:Tn], ALU.add, ALU.mult, ['sil', ka], ['sil'])
                g.stt(gT[:, c, 0:Tn], sil[:, 0:Tn], 0.5, pu[:, 0:Tn], ALU.mult, ALU.mult, ['sil', ku], [f"gT{c}"])
            gk = [f"gT{c}" for c in range(NFF)]
            for j in range(Tc):
                yk = f"xin_t{j}"
                for hh in range(2):
                    for c in range(NFF):
                        g.mm(dn[hh][:], gT[:, c, j * 128:(j + 1) * 128], wo_b[:, c, hh * 512:(hh + 1) * 512],
                             c == 0, c == NFF - 1, gk + (wo_keys if st_i == 0 else []), [f"dn{hh}"])
                    ysl = xin[:, j, hh * 512:(hh + 1) * 512]
                    g.stt(ysl, dn[hh][:], 0.5, ysl, ALU.mult, ALU.add, [f"dn{hh}", yk], [yk])
                layer_norm_tile(g, xin[:, j, :], yk, gbc, bbc, small[j % 2], f"ln{j % 2}")
                dap = dst(t0 + j) if callable(dst) else dst[(t0 + j) * 128:(t0 + j + 1) * 128, :]
                g.dma('pool', dap, xin[:, j, :], [yk], [yk, f"dst{t0 + j}"])
                if after_tile is not None:
                    after_tile(t0 + j)
        g.es = es_outer
        g.emit()


def proj_tile(g, b, bufs, w_b, groups):
    hf, hb, hT, tp, idb = bufs['hf'][b], bufs['hb'][b], bufs['hT'][b], bufs['tp'], bufs['idb']
    g.cp('pool', hb[:], hf[:], [f"hf{b}"], [f"hb{b}"])
    for k in range(8):
        g.tr(tp[:, k * 128:(k + 1) * 128], hb[:, k * 128:(k + 1) * 128], idb[:], [f"hb{b}", 'idb'], ['tp'])
    g.cp('dve', hT[:].rearrange("p a b -> p (a b)"), tp[:, 0:1024], ['tp'], [f"hT{b}"])
    for (pap, key, c0, n) in groups:
        for k in range(8):
            g.mm(pap, hT[:, k, :], w_b[:, k, c0:c0 + n], k == 0, k == 7, [f"hT{b}", 'w_in'], [key])


def headT(g, src_bf, src_key, nheads, tp, idb, dstT, dst_key):
    for h in range(nheads):
        g.tr(tp[0:64, h * 128:(h + 1) * 128], src_bf[:, h * 64:(h + 1) * 64], idb[:], [src_key, 'idb'], ['tp'])
    g.cp('act', dstT[:].rearrange("p a b -> p (a b)"), tp[0:64, 0:nheads * 128], ['tp'], [dst_key])


def phase_b(g, es_outer, P):
    S, NT, NQ = P['S'], P['NT'], P['NQ']
    with ExitStack() as es:
        g.es = es
        w_b = g.sb([128, 8, N_IN], BF16)
        stage = [g.sb([128, 2048], F32) for _ in range(2)]
        idf = g.sb([128, 128], F32)
        idb = g.sb([128, 128], BF16)
        idx_sb = g.sb([128, NQ + 1], I32)
        bufs = dict(hf=[g.sb([128, D], F32) for _ in range(2)], hb=[g.sb([128, D], BF16) for _ in range(2)],
                    hT=[g.sb([128, 8, 128], BF16) for _ in range(2)], idb=idb)
        kaf = g.sb([128, 512], F32); vaf = g.sb([128, 512], F32); kif = g.sb([128, 64], F32)
        kbf = g.sb([128, 512], F32); vbf = g.sb([128, 512], F32)
        kab = g.sb([128, 512], BF16); kib = g.sb([128, 64], BF16); kbb = g.sb([128, 512], BF16)
        vaug = g.sb([128, 8, 65], BF16); vbug = g.sb([128, 8, 65], BF16)
        kaT = g.sb([64, 8, 128], BF16); kbT = g.sb([64, 8, 128], BF16); kiT = g.sb([64, 1, 128], BF16)
        qab = g.sb([128, 512], BF16); qib = g.sb([128, 256], BF16); qbb = g.sb([128, 512], BF16)
        qaT = g.sb([64, 8, 128], BF16); qiT = g.sb([64, 4, 128], BF16); qbT = g.sb([64, 8, 128], BF16)
        wif = g.sb([128, 4], F32)
        sga = g.sb([128, D], F32); sgb = g.sb([128, D], F32)
        pb = [g.ps([128, 512], F32) for _ in range(7)]
        tpf = g.ps([128, 512], F32)
        tp = tpf[:].bitcast(BF16)
        bufs['tp'] = tp

        g.dma('sp', idf[:], P['ident'], [], ['idf'])
        g.cp('dve', idb[:], idf[:], ['idf'], ['idb'])
        g.dma('sp', idx_sb[:], P['own_idx'], [], ['idx'])
        st4 = [stage[q // 2][:, (q % 2) * 1024:(q % 2 + 1) * 1024] for q in range(4)]
        load_weight_bf16(g, w_b, P['w_in'], 8, N_IN, st4, ['stg0', 'stg1', 'stg2', 'stg3'], 'w_in_p', piece=1024)
        wkeys = [f"w_in_p_{kc}_{c0}" for kc in range(8) for c0 in range(0, N_IN, 1024)]
        g.op('pe', lambda e: e.nop(), reads=wkeys, writes=['w_in'])
        g.op('pool', lambda e: e.memset(vaug[:], 1.0), [], ['vaug'])
        g.op('pool', lambda e: e.memset(vbug[:], 1.0), [], ['vbug'])

        def kside_ingest(kind, t, src):
            pre = kind + '_'
            if src.get('ka') is not None:
                ap, key = src['ka']
                g.cp('act', kaf[:], ap, [key], ['kaf'])
                g.cp('dve', kab[:], kaf[:], ['kaf'], ['kab'])
                headT(g, kab, 'kab', 8, tp, idb, kaT, 'kaT')
                g.dma('sp', P[pre + 'KaT'][t], kaT[:].rearrange("p a b -> p (a b)"), ['kaT'], ['kaT'])
                ap, key = src['va']
                g.cp('act', vaf[:], ap, [key], ['vaf'])
                g.cp('dve', vaug[:, :, 0:64], vaf[:].rearrange("p (h d) -> p h d", h=8), ['vaf'], ['vaug'])
                g.dma('sp', P[pre + 'Va'][t], vaug[:].rearrange("p a b -> p (a b)"), ['vaug'], ['vaug'])
                ap, key = src['ki']
                g.cp('act', kif[:], ap, [key], ['kif'])
                g.cp('dve', kib[:], kif[:], ['kif'], ['kib'])
                g.tr(tp[0:64, 0:128], kib[:], idb[:], ['kib', 'idb'], ['tp'])
                g.cp('act', kiT[:, 0, :], tp[0:64, 0:128], ['tp'], ['kiT'])
                g.dma('sp', P[pre + 'kiT'][:, t * 128:(t + 1) * 128], kiT[:, 0, :], ['kiT'], ['kiT'])
            if src.get('kb') is not None:
                tb = src['tb']
                ap, key = src['kb']
                g.cp('act', kbf[:], ap, [key], ['kbf'])
                g.cp('dve', kbb[:], kbf[:], ['kbf'], ['kbb'])
                headT(g, kbb, 'kbb', 8, tp, idb, kbT, 'kbT')
                g.dma('sp', P[pre + 'KbT'][tb], kbT[:].rearrange("p a b -> p (a b)"), ['kbT'], ['kbT'])
                ap, key = src['vb']
                g.cp('act', vbf[:], ap, [key], ['vbf'])
                g.cp('dve', vbug[:, :, 0:64], vbf[:].rearrange("p (h d) -> p h d", h=8), ['vbf'], ['vbug'])
                g.dma('sp', P[pre + 'Vb'][tb], vbug[:].rearrange("p a b -> p (a b)"), ['vbug'], ['vbug'])

        def ld(t):
            r0 = _hrow(t * 128, S)
            srcap = P['h_all'][r0:r0 + 128, :] if t < NT else P['h_samp']
            g.dma('sp', bufs['hf'][t % 2][:], srcap, [], [f"hf{t % 2}"])
        ld(0)
        for t in range(NT + 1):
            groups = [(pb[0][:], 'pb0', C_KA, 512), (pb[1][:], 'pb1', C_VA, 512), (pb[2][:, 0:64], 'pb2', C_KI, 64),
                      (pb[3][:], 'pb3', C_KB, 512), (pb[4][:], 'pb4', C_VB, 512)]
            proj_tile(g, t % 2, bufs, w_b, groups)
            if t + 1 <= NT:
                ld(t + 1)
            src = dict(ka=(pb[0][:], 'pb0'), va=(pb[1][:], 'pb1'), ki=(pb[2][:, 0:64], 'pb2'),
                       kb=(pb[3][:], 'pb3'), vb=(pb[4][:], 'pb4'))
            if t < NT:
                src['tb'] = t
                kside_ingest('p', t, src)
                g.dma('pool', P['kA_out'][t * 128:(t + 1) * 128, :], kaf[:], ['kaf'], ['kaf'])
                g.dma('pool', P['vA_out'][t * 128:(t + 1) * 128, :], vaf[:], ['vaf'], ['vaf'])
                g.dma('pool', P['kidx_out'][t * 128:(t + 1) * 128, :], kif[:], ['kif'], ['kif'])
                if t >= NT - 4:
                    o = (t - (NT - 4)) * 128
                    g.dma('pool', P['kB_out'][o:o + 128, :], kbf[:], ['kbf'], ['kbf'])
                    g.dma('pool', P['vB_out'][o:o + 128, :], vbf[:], ['vbf'], ['vbf'])
            else:
                src['tb'] = 4
                kside_ingest('s', 8, src)
                g.dma('pool', P['skA'], kaf[:], ['kaf'], ['kaf'])
                g.dma('pool', P['svA'], vaf[:], ['vaf'], ['vaf'])
                g.dma('pool', P['skidx'], kif[:], ['kif'], ['kif'])
                g.dma('pool', P['skB'][448:512, :], kbf[0:64, :], ['kbf'], ['kbf'])
                g.dma('pool', P['svB'][448:512, :], vbf[0:64, :], ['vbf'], ['vbf'])
        if 'roll' in P['bparts']:
            g.dma('pool', P['skB'][0:448, :], P['c_kb'][64:512, :], [], ['skB_roll'])
            g.dma('pool', P['svB'][0:448, :], P['c_vb'][64:512, :], [], ['svB_roll'])
        for t in range(8 if 'cache' in P['bparts'] else 0):
            g.dma('sp', kaf[:], P['c_ka'][t * 128:(t + 1) * 128, :], [], ['kaf'])
            g.dma('sp', vaf[:], P['c_va'][t * 128:(t + 1) * 128, :], [], ['vaf'])
            g.dma('sp', kif[:], P['c_ki'][t * 128:(t + 1) * 128, :], [], ['kif'])
            g.cp('dve', kab[:], kaf[:], ['kaf'], ['kab'])
            headT(g, kab, 'kab', 8, tp, idb, kaT, 'kaT')
            g.dma('sp', P['s_KaT'][t], kaT[:].rearrange("p a b -> p (a b)"), ['kaT'], ['kaT'])
            g.cp('pool', vaug[:, :, 0:64], vaf[:].rearrange("p (h d) -> p h d", h=8), ['vaf'], ['vaug'])
            g.dma('sp', P['s_Va'][t], vaug[:].rearrange("p a b -> p (a b)"), ['vaug'], ['vaug'])
            g.cp('dve', kib[:], kif[:], ['kif'], ['kib'])
            g.tr(tp[0:64, 0:128], kib[:], idb[:], ['kib', 'idb'], ['tp'])
            g.cp('act', kiT[:, 0, :], tp[0:64, 0:128], ['tp'], ['kiT'])
            g.dma('sp', P['s_kiT'][:, t * 128:(t + 1) * 128], kiT[:, 0, :], ['kiT'], ['kiT'])
        for t in range(4 if 'cache' in P['bparts'] else 0):
            g.dma('sp', kbf[:], P['c_kb'][t * 128:(t + 1) * 128, :], [], ['kbf'])
            g.dma('sp', vbf[:], P['c_vb'][t * 128:(t + 1) * 128, :], [], ['vbf'])
            g.cp('dve', kbb[:], kbf[:], ['kbf'], ['kbb'])
            headT(g, kbb, 'kbb', 8, tp, idb, kbT, 'kbT')
            g.dma('sp', P['s_KbT'][t], kbT[:].rearrange("p a b -> p (a b)"), ['kbT'], ['kbT'])
            g.cp('pool', vbug[:, :, 0:64], vbf[:].rearrange("p (h d) -> p h d", h=8), ['vbf'], ['vbug'])
            g.dma('sp', P['s_Vb'][t], vbug[:].rearrange("p a b -> p (a b)"), ['vbug'], ['vbug'])

        def ldq(i):
            hf = bufs['hf'][i % 2]
            if i == NQ:
                g.dma('pool', hf[:], P['h_samp'], [], [f"hf{i % 2}"])
                return
            g.op('pool', lambda e: e.indirect_dma_start(
                out=hf[:], out_offset=None, in_=P['h_all'],
                in_offset=bass.IndirectOffsetOnAxis(ap=idx_sb[:, i:i + 1], axis=0)),
                reads=['idx'], writes=[f"hf{i % 2}"], dma=True)
        nb2 = NQ + 1 if 'b2' in P['bparts'] else 0
        if nb2:
            ldq(0)
        for i in range(nb2):
            groups = [(pb[0][:], 'pb0', C_QA, 512), (pb[1][:, 0:256], 'pb1', C_QI, 256),
                      (pb[1][:, 256:260], 'pb1', C_WI, 4), (pb[2][:], 'pb2', C_QB, 512),
                      (pb[3][:], 'pb3', C_GA, 512), (pb[4][:], 'pb4', C_GA + 512, 512),
                      (pb[5][:], 'pb5', C_GB, 512), (pb[6][:], 'pb6', C_GB + 512, 512)]
            proj_tile(g, i % 2, bufs, w_b, groups)
            if i + 1 < nb2:
                ldq(i + 1)
            g.cp('dve', qab[:], pb[0][:], ['pb0'], ['qab'])
            headT(g, qab, 'qab', 8, tp, idb, qaT, 'qaT')
            g.dma('sp', P['q_qaT'][i], qaT[:].rearrange("p a b -> p (a b)"), ['qaT'], ['qaT'])
            g.cp('dve', qib[:], pb[1][:, 0:256], ['pb1'], ['qib'])
            g.cp('dve', wif[:], pb[1][:, 256:260], ['pb1'], ['wif'])
            headT(g, qib, 'qib', 4, tp, idb, qiT, 'qiT')
            g.dma('sp', P['q_qiT'][i], qiT[:].rearrange("p a b -> p (a b)"), ['qiT'], ['qiT'])
            g.dma('sp', P['q_wi'][i], wif[:], ['wif'], ['wif'])
            g.cp('dve', qbb[:], pb[2][:], ['pb2'], ['qbb'])
            headT(g, qbb, 'qbb', 8, tp, idb, qbT, 'qbT')
            g.dma('sp', P['q_qbT'][i], qbT[:].rearrange("p a b -> p (a b)"), ['qbT'], ['qbT'])
            for hh in range(2):
                g.act(sga[:, hh * 512:(hh + 1) * 512], pb[3 + hh][:], AF.Tanh, [f"pb{3 + hh}"], ['sga'], scale=0.5)
                g.act(sgb[:, hh * 512:(hh + 1) * 512], pb[5 + hh][:], AF.Tanh, [f"pb{5 + hh}"], ['sgb'], scale=0.5)
            g.ts('pool', sga[:], sga[:], 1.0, 0.5, ALU.add, ALU.mult, ['sga'], ['sga'])
            g.ts('pool', sgb[:], sgb[:], 1.0, 0.5, ALU.add, ALU.mult, ['sgb'], ['sgb'])
            g.dma('sp', P['q_sga'][i], sga[:], ['sga'], ['sga'])
            g.dma('sp', P['q_sgb'][i], sgb[:], ['sgb'], ['sgb'])
        g.es = es_outer
        g.emit()


def attn_core(g, bufs, kts, kt_src, vt_src, qT, qkey, table, tslot_fn, mask_fn, outT, out_key, mid_at=None,
              mid_fn=None, defer_norm=False):
    S_ps, O_ps, bc_ps = bufs['S_ps'], bufs['O_ps'], bufs['bc_ps']
    kt_t, vt_t, Sb, Pb = bufs['kt_t'], bufs['vt_t'], bufs['Sb'], bufs['Pb']
    oT, rden, ones, irep = bufs['oT'], bufs['rden'], bufs['ones'], bufs['irep']
    NBUF = len(kt_t)
    NS = len(S_ps)
    n = len(kts)
    items = [(p, hg) for p in range(n) for hg in range(2)]

    def load(p):
        b = p % NBUF
        g.dma('sp', kt_t[b][:].rearrange("p a b -> p (a b)"), kt_src(kts[p]), [], [f"kt{b}"])
        g.dma('sp', vt_t[b][:], vt_src(kts[p]), [], [f"vt{b}"])

    def stage_s(j):
        p, hg = items[j]
        kt = kts[p]
        b = p % NBUF
        sp_ = S_ps[j % NS]
        sk = f"S{j % NS}"
        mk = mask_fn(kt) if mask_fn is not None else None
        if mk is not None:
            g.mm(sp_[:, 0:512], mk[0], irep[:], True, False, [mk[1], 'irep'], [sk], skip=True)
        for hh in range(4):
            h = hg * 4 + hh
            g.mm(sp_[:, hh * 128:(hh + 1) * 128], kt_t[b][:, h, :], qT[:, h, :], mk is None and hh == 0, True,
                 [f"kt{b}", qkey], [sk], skip=True)

    def stage_e(j):
        p, hg = items[j]
        slot = tslot_fn(kts[p])
        sp_ = S_ps[j % NS]
        sk = f"S{j % NS}"
        pk = f"P{j % NS}"
        if slot is not None:
            sb_ = Sb[j % 2]
            g.stt(sb_[:], sp_[:].rearrange("p (a b) -> p a b", a=4), 0.125,
                  table[:, slot, hg * 4:(hg + 1) * 4, :], ALU.mult, ALU.add, [sk, 'table'], [f"Sb{j % 2}"])
            g.act(Pb[j % NS][:], sb_[:], AF.Exp, [f"Sb{j % 2}"], [pk])
        else:
            g.act(Pb[j % NS][:], sp_[:].rearrange("p (a b) -> p a b", a=4), AF.Exp, [sk], [pk], scale=0.125)

    def stage_v(j):
        p, hg = items[j]
        b = p % NBUF
        for hh in range(4):
            h = hg * 4 + hh
            g.mm(O_ps[hg][0:65, hh * 128:(hh + 1) * 128], vt_t[b][:, h * 65:(h + 1) * 65], Pb[j % NS][:, hh, :],
                 p == 0 and hh == 0, p == n - 1, [f"vt{b}", f"P{j % NS}"], [f"O{hg}"], skip=True)

    for p in range(min(NBUF - 1, n)):
        load(p)
    stage_s(0)
    if len(items) > 1:
        stage_s(1)
    mid_done = False
    for j in range(len(items)):
        p, hg = items[j]
        if hg == 0 and p + NBUF - 1 < n:
            load(p + NBUF - 1)
        stage_e(j)
        if j + 2 < len(items):
            stage_s(j + 2)
        stage_v(j)
        if mid_fn is not None and not mid_done and j + 1 >= min(mid_at, len(items)):
            mid_fn()
            mid_done = True
    for hg in range(2):
        g.cp('act', oT[0:65, hg * 512:(hg + 1) * 512], O_ps[hg][0:65, :], [f"O{hg}"], [f"oT{hg}"])

    def norm():
        g.op('dve', lambda e: e.reciprocal(out=rden[64:65, :], in_=oT[64:65, :]), ['oT0', 'oT1'], ['rden'])
        for hg in range(2):
            g.mm(bc_ps[hg][0:64, :], ones[64:65, 0:64], rden[64:65, hg * 512:(hg + 1) * 512], True, True,
                 ['ones', 'rden'], [f"bc{hg}"])
            g.tt('dve', outT[:, hg * 4:(hg + 1) * 4, :],
                 oT[0:64, hg * 512:(hg + 1) * 512].rearrange("p (a b) -> p a b", a=4),
                 bc_ps[hg][0:64, :].rearrange("p (a b) -> p a b", a=4), ALU.mult, [f"oT{hg}", f"bc{hg}"], [out_key])
    if defer_norm:
        return norm
    norm()
    return None


def attn_bufs(g, idb, n_s=4):
    bufs = dict(
        S_ps=[g.ps([128, 512], F32) for _ in range(n_s)], O_ps=[g.ps([128, 512], F32) for _ in range(2)],
        bc_ps=[g.ps([128, 512], F32) for _ in range(2)],
        kt_t=[g.sb([64, 8, 128], BF16) for _ in range(4)], vt_t=[g.sb([128, 520], BF16) for _ in range(4)],
        Sb=[g.sb([128, 4, 128], F32) for _ in range(2)], Pb=[g.sb([128, 4, 128], BF16) for _ in range(4)],
        oT=g.sb([128, 1024], F32), rden=g.sb([128, 1024], F32), ones=g.sb([128, 64], F32),
        irep=g.sb([128, 512], BF16))
    return bufs


def attn_consts(g, bufs, idb):
    g.op('pool', lambda e: e.memset(bufs['ones'][:], 1.0), [], ['ones'])
    for r4 in range(4):
        g.cp('pool', bufs['irep'][:, r4 * 128:(r4 + 1) * 128], idb[:], ['idb'], ['irep'])


def phase_c1(g, es_outer, P):
    S, NT, NQ, NB, KSEL = P['S'], P['NT'], P['NQ'], P['NB'], P['KSEL']
    NMAX = max(NT * 128, 9 * 128)
    with ExitStack() as es:
        g.es = es
        idf = g.sb([128, 128], F32); idb = g.sb([128, 128], BF16)
        bufs = attn_bufs(g, idb)
        score = [g.sb([128, NMAX], F32) for _ in range(2)]
        junk = [g.sb([128, NMAX], BF16) for _ in range(2)]
        table = g.sb([128, 3, 8, 128], F32)
        c15 = g.sb([128, 8], F32)
        admneg = g.sb([128, 256], F32)
        pow2 = g.sb([128, NB], F32)
        Dh = g.sb([128, 4, 128], BF16)
        qaT = g.sb([64, 8, 128], BF16); qiT = g.sb([64, 4, 128], BF16); wif = g.sb([128, 4], F32)
        kiT = [g.sb([64, 512], BF16) for _ in range(2)]
        R = g.sb([128, 4, 512], BF16)
        oaT = g.sb([64, 8, 128], BF16)
        sm = {k: g.sb([128, 1], F32) for k in ['mn', 'mx', 'lo', 'w0', 't', 'cnt', 'inc']}
        tz = {k: g.sb([128, 1], F32) for k in ['cpos', 'cnn', 'a', 'b', 'tie', 'r', 'nt', 'thr', 'carry']}
        CH = 2048
        zc = g.sb([128, CH], BF16)
        cum = g.sb([128, CH], F32)
        onesb = g.sb([128, CH], BF16)
        hw = g.sb([128, NB], F32)
        sc_ps = bufs['O_ps'][0]
        dots = [bufs['S_ps'][0], bufs['S_ps'][1], bufs['S_ps'][2], bufs['S_ps'][3]]
        dkeys = ['S0', 'S1', 'S2', 'S3']

        g.dma('sp', idf[:], P['ident'], [], ['idf'])
        g.cp('dve', idb[:], idf[:], ['idf'], ['idb'])
        g.dma('sp', pow2[:], P['pow2'].to_broadcast([128, NB]), [], ['pow2'])
        g.dma('sp', c15[:], P['c15'].to_broadcast([128, 8]), [], ['c15'])
        attn_consts(g, bufs, idb)
        g.op('pool', lambda e: e.memset(onesb[:], 1.0), [], ['onesb'])

        def geom(i):
            samp = (i == NQ)
            nkt = 9 if samp else 2 * i + 2
            return samp, nkt, nkt * 128

        def load_tables(ti, which):
            if which == 'table':
                g.dma('sp', table[:].rearrange("p a b c -> p (a b c)"), P['tdsa'][ti], [], ['table'])
                for j in range(3):
                    g.tt('pool', table[:, j, :, :], table[:, j, :, :], c15[:].unsqueeze(2).to_broadcast([128, 8, 128]),
                         ALU.subtract, ['table', 'c15'], ['table'])
            else:
                g.dma('sp', admneg[:], P['admneg'][ti], [], ['admneg'])

        def IDX(i):
            samp, nkt, N = geom(i)
            sc, sck = score[i % 2], f"score{i % 2}"
            kis = P['s_kiT'] if samp else P['p_kiT']
            g.dma('sp', qiT[:].rearrange("p a b -> p (a b)"), P['q_qiT'][i], [], ['qiT'])
            g.dma('sp', wif[:], P['q_wi'][i], [], ['wif'])
            for h in range(4):
                g.ts('pool', Dh[:, h, :], idf[:], wif[:, h:h + 1], 1.0 / 16.0, ALU.mult, ALU.mult, ['idf', 'wif'], ['Dh'])
            ngr = (nkt + 3) // 4
            for gi in range(ngr):
                n = min(512, N - gi * 512)
                b = gi % 2
                g.dma('sp', kiT[b][:, 0:n], kis[:, gi * 512:gi * 512 + n], [], [f"kiT{b}"])
                for h in range(4):
                    g.mm(dots[h][:, 0:n], qiT[:, h, :], kiT[b][:, 0:n], True, True, [f"kiT{b}", 'qiT'], [dkeys[h]])
                for h in range(4):
                    g.act(R[:, h, 0:n], dots[h][:, 0:n], AF.Relu, [dkeys[h]], [f"R{h}"])
                for h in range(4):
                    g.mm(sc_ps[:, 0:n], Dh[:, h, :], R[:, h, 0:n], h == 0, h == 3, ['Dh', f"R{h}"], ['O0'])
                g.cp('act', sc[:, gi * 512:gi * 512 + n], sc_ps[:, 0:n], ['O0'], [sck])

        def BIS(i):
            samp, nkt, N = geom(i)
            sc, sck = score[i % 2], f"score{i % 2}"
            jk, jkk = junk[i % 2], f"junk{i % 2}"
            if i == 0 or samp:
                load_tables(1 if samp else 0, 'admneg')
            g.op('dve', lambda e: e.tensor_reduce(out=sm['mn'][:], in_=sc[:, 0:N], axis=AX.X, op=ALU.min),
                 [sck], ['mn'])
            g.op('dve', lambda e: e.tensor_reduce(out=sm['mx'][:], in_=sc[:, 0:N], axis=AX.X, op=ALU.max),
                 [sck], ['mx'])
            g.tt('dve', sc[:, N - 256:N], sc[:, N - 256:N], admneg[:], ALU.add, [sck, 'admneg', 'mn', 'mx'], [sck])
            g.ts('dve', sm['lo'][:], sm['mn'][:], -1.0, None, ALU.add, None, ['mn'], ['lo'])
            g.tt('dve', sm['w0'][:], sm['mx'][:], sm['lo'][:], ALU.subtract, ['mx', 'lo'], ['w0'])
            g.ts('dve', hw[:], pow2[:], sm['w0'][:], None, ALU.mult, None, ['pow2', 'w0'], ['hw'])
            ksel = (min(TOPK, (PAST + 64) // 4) if samp else KSEL)
            for k in range(NB):
                g.tt('dve', sm['t'][:], sm['lo'][:], hw[:, k:k + 1], ALU.add, ['lo', 'hw'], ['t'])
                g.ts('dve', jk[:, 0:N], sc[:, 0:N], sm['t'][:], 0.0, ALU.is_gt, ALU.add, [sck, 't'],
                     [jkk, 'cnt'], accum_out=sm['cnt'][:])
                g.ts('dve', sm['inc'][:], sm['cnt'][:], ksel - 0.5, None, ALU.is_gt, None, ['cnt'], ['inc'])
                g.stt(sm['lo'][:], sm['inc'][:], hw[:, k:k + 1], sm['lo'][:], ALU.mult, ALU.add,
                      ['inc', 'hw', 'lo'], ['lo'])
            kf = float(ksel)
            g.ts('dve', jk[:, 0:N], sc[:, 0:N], 0.0, 0.0, ALU.is_gt, ALU.add, [sck], [jkk, 'cpos'],
                 accum_out=tz['cpos'][:])
            g.ts('dve', jk[:, 0:N], sc[:, 0:N], 0.0, 0.0, ALU.is_ge, ALU.add, [sck, 'cpos'], [jkk, 'cnn'],
                 accum_out=tz['cnn'][:])
            g.ts('dve', tz['a'][:], tz['cpos'][:], kf - 0.5, None, ALU.is_lt, None, ['cpos'], ['tz_a'])
            g.ts('dve', tz['b'][:], tz['cnn'][:], kf + 0.5, None, ALU.is_gt, None, ['cnn'], ['tz_b'])
            g.tt('dve', tz['tie'][:], tz['a'][:], tz['b'][:], ALU.mult, ['tz_a', 'tz_b'], ['tie'])
            g.ts('dve', tz['r'][:], tz['cpos'][:], -1.0, kf, ALU.mult, ALU.add, ['cpos'], ['tz_r0'])
            g.tt('dve', tz['r'][:], tz['r'][:], tz['tie'][:], ALU.mult, ['tz_r0', 'tie'], ['tz_r'])
            g.ts('dve', tz['nt'][:], tz['tie'][:], -1.0, 1.0, ALU.mult, ALU.add, ['tie'], ['tz_nt'])
            g.tt('dve', tz['thr'][:], sm['lo'][:], tz['nt'][:], ALU.mult, ['lo', 'tz_nt'], ['thr'])
            for c0 in range(0, N, CH):
                n = min(CH, N - c0)
                g.ts('dve', zc[:, 0:n], sc[:, c0:c0 + n], 0.0, None, ALU.is_equal, None, [sck], ['zc'])
                init = 0.0 if c0 == 0 else tz['carry'][:]
                g.op('dve', lambda e, n=n, init=init: e.tensor_tensor_scan(
                    out=cum[:, 0:n], data0=onesb[:, 0:n], data1=zc[:, 0:n], initial=init, op0=ALU.mult, op1=ALU.add),
                    ['zc', 'onesb', 'carry'], ['cum'])
                if c0 + n < N:
                    g.cp('dve', tz['carry'][:], cum[:, n - 1:n], ['cum'], ['carry'])
                g.stt(zc[:, 0:n], cum[:, 0:n], tz['r'][:], zc[:, 0:n], ALU.is_le, ALU.mult, ['cum', 'tz_r', 'zc'], ['zc'])
                g.stt(zc[:, 0:n], sc[:, c0:c0 + n], tz['thr'][:], zc[:, 0:n], ALU.is_gt, ALU.max, [sck, 'thr', 'zc'], ['zc'])
                g.ts('dve', jk[:, c0:c0 + n], zc[:, 0:n], -1.0, -MASKNEG, ALU.add, ALU.mult, ['zc'], [jkk])
            if 'dbg_score' in P:
                g.dma('sp', P['dbg_score'][i, :, 0:N], sc[:, 0:N], [sck], ['dbg'])
                g.dma('sp', P['dbg_junk'][i, :, 0:N], jk[:, 0:N], [jkk], ['dbg'])

        def ATT(i):
            samp, nkt, N = geom(i)
            jk, jkk = junk[i % 2], f"junk{i % 2}"
            if i == 0 or samp:
                load_tables(1 if samp else 0, 'table')
            g.dma('sp', qaT[:].rearrange("p a b -> p (a b)"), P['q_qaT'][i], [], ['qaT'])
            kas = P['s_KaT'] if samp else P['p_KaT']
            vas = P['s_Va'] if samp else P['p_Va']
            near = [kt for kt in range(nkt - 3, nkt) if kt >= 0]
            order = near + [kt for kt in range(nkt) if kt not in near]
            return attn_core(g, bufs, order, lambda kt: kas[kt], lambda kt: vas[kt], qaT, 'qaT', table,
                             lambda kt: (kt - (nkt - 3)) if kt >= nkt - 3 else None,
                             lambda kt: (jk[:, kt * 128:(kt + 1) * 128], jkk), oaT, 'oaT',
                             mid_at=2 * len(near), mid_fn=(lambda: BIS(i + 1)) if i + 1 <= NQ else None,
                             defer_norm=True)

        IDX(0)
        BIS(0)
        if NQ >= 1:
            IDX(1)
        for i in range(NQ + 1):
            norm = ATT(i)
            if i + 2 <= NQ:
                IDX(i + 2)
            norm()
            g.dma('sp', P['q_oaT'][i], oaT[:].rearrange("p a b -> p (a b)"), ['oaT'], ['oaT'])
        g.es = es_outer
        g.emit()


def phase_c2(g, es_outer, P):
    S, NT, NQ = P['S'], P['NT'], P['NQ']
    with ExitStack() as es:
        g.es = es
        idf = g.sb([128, 128], F32); idb = g.sb([128, 128], BF16)
        bufs = attn_bufs(g, idb, n_s=2)
        table = g.sb([128, 6, 8, 128], F32)
        idx_sb = g.sb([128, NQ + 1], I32)
        wba = g.sb([64, 8, D], BF16); wbb = g.sb([64, 8, D], BF16); wout = g.sb([128, 8, D], BF16)
        stage = [g.sb([128, 2048], F32) for _ in range(2)]
        gbc = g.sb([128, D], F32); bbc = g.sb([128, D], F32)
        qbT = [g.sb([64, 8, 128], BF16) for _ in range(2)]
        oaT = [g.sb([64, 8, 128], BF16) for _ in range(2)]
        obT = g.sb([64, 8, 128], BF16)
        sga = [g.sb([128, D], F32) for _ in range(2)]
        sgb = [g.sb([128, D], F32) for _ in range(2)]
        hown = [g.sb([128, D], F32) for _ in range(2)]
        mpb = g.sb([128, D], BF16); mixT = g.sb([128, 8, 128], BF16)
        small = [make_small(g) for _ in range(2)]
        y_ps = [g.ps([128, 512], F32) for _ in range(2)]
        tp = bufs['bc_ps'][1][:].bitcast(BF16)

        g.dma('sp', idf[:], P['ident'], [], ['idf'])
        g.cp('dve', idb[:], idf[:], ['idf'], ['idb'])
        g.dma('sp', idx_sb[:], P['own_idx'], [], ['idx'])
        g.dma('sp', gbc[:], P['ln2_g'].to_broadcast([128, D]), [], ['lng'])
        g.dma('sp', bbc[:], P['ln2_b'].to_broadcast([128, D]), [], ['lnb'])
        attn_consts(g, bufs, idb)
        for wsrc, wdst, key in [(P['w_branch_a'], wba, 'wba'), (P['w_branch_b'], wbb, 'wbb')]:
            for h in range(8):
                st = stage[h % 2]
                g.dma('sp', st[0:64, 0:D], wsrc[h * 64:(h + 1) * 64, :], [], [f"stg{h % 2}"])
                g.cp(g.cast_eng(), wdst[:, h, :], st[0:64, 0:D], [f"stg{h % 2}"], [key])
        load_weight_bf16(g, wout, P['w_out'], 8, D, [st[:] for st in stage], ['stg0', 'stg1'], 'wout_p')
        g.op('pe', lambda e: e.nop(), reads=[f"wout_p_{kc}_0" for kc in range(8)], writes=['wout'])

        def loads(i):
            b = i % 2
            g.dma('sp', qbT[b][:].rearrange("p a b -> p (a b)"), P['q_qbT'][i], [], [f"qbT{b}"])
            g.dma('sp', oaT[b][:].rearrange("p a b -> p (a b)"), P['q_oaT'][i], [], [f"oaT{b}"])
            g.dma('sp', sga[b][:], P['q_sga'][i], [], [f"sga{b}"])
            g.dma('sp', sgb[b][:], P['q_sgb'][i], [], [f"sgb{b}"])
            if i == NQ:
                g.dma('pool', hown[b][:], P['h_samp'], [], [f"hown{b}"])
                return
            g.op('pool', lambda e: e.indirect_dma_start(
                out=hown[b][:], out_offset=None, in_=P['h_all'],
                in_offset=bass.IndirectOffsetOnAxis(ap=idx_sb[:, i:i + 1], axis=0)),
                reads=['idx'], writes=[f"hown{b}"], dma=True)

        loads(0)
        for i in range(NQ + 1):
            samp = (i == NQ)
            b = i % 2
            if i == 0 or samp:
                g.dma('sp', table[:].rearrange("p a b c -> p (a b c)"), P['tband'][1 if samp else 0], [], ['table'])
            if samp:
                kts = list(range(5)); base = 0
                kbs, vbs = P['s_KbT'], P['s_Vb']
            else:
                base = 2 * i - 4
                kts = [kt for kt in range(base, 2 * i + 2) if kt >= 0]
                kbs, vbs = P['p_KbT'], P['p_Vb']
            attn_core(g, bufs, kts, lambda kt: kbs[kt], lambda kt: vbs[kt], qbT[b], f"qbT{b}", table,
                      lambda kt, base=base: kt - base, None, obT, 'obT')
            if i + 1 <= NQ:
                loads(i + 1)
            for (oT_, okey, w_, wkey, sg, sgk) in [(oaT[b], f"oaT{b}", wba, 'wba', sga[b], f"sga{b}"),
                                                   (obT, 'obT', wbb, 'wbb', sgb[b], f"sgb{b}")]:
                for hh in range(2):
                    for h in range(8):
                        g.mm(y_ps[hh][:], oT_[:, h, :], w_[:, h, hh * 512:(hh + 1) * 512], h == 0, h == 7,
                             [okey, wkey], [f"y{hh}"])
                    g.tt('dve', sg[:, hh * 512:(hh + 1) * 512], sg[:, hh * 512:(hh + 1) * 512], y_ps[hh][:], ALU.mult,
                         [sgk, f"y{hh}"], [sgk])
            g.tt('pool', mpb[:], sga[b][:], sgb[b][:], ALU.add, [f"sga{b}", f"sgb{b}"], ['mpb'])
            for k in range(8):
                g.tr(tp[:, k * 128:(k + 1) * 128], mpb[:, k * 128:(k + 1) * 128], idb[:], ['mpb', 'idb'], ['bc1'])
            g.cp('act', mixT[:].rearrange("p a b -> p (a b)"), tp[:, 0:1024], ['bc1'], ['mixT'])
            hk = f"hown{b}"
            for hh in range(2):
                for k in range(8):
                    g.mm(y_ps[hh][:], mixT[:, k, :], wout[:, k, hh * 512:(hh + 1) * 512], k == 0, k == 7,
                         ['mixT', 'wout'], [f"y{hh}"])
                hs = hown[b][:, hh * 512:(hh + 1) * 512]
                g.stt(hs, hs, ALPHA, y_ps[hh][:], ALU.mult, ALU.add, [hk, f"y{hh}"], [hk])
            layer_norm_tile(g, hown[b][:], hk, gbc, bbc, small[b], f"ln2{b}")
            g.dma('pool', P['h2_own'][i * 128:(i + 1) * 128, :], hown[b][:], [hk], [hk])
        g.es = es_outer
        g.emit()


def build_program(S, debug=None, nphases=5, bparts=('roll', 'cache', 'b2'), npairs=4, cc_inc=16):
    NT = S // 128
    NQ = S // 256
    NTA = NT + 1
    NQA = NQ + 1
    nc = bass.Bass("TRN2", target_bir_lowering=False)
    dbg = set(debug or [])

    def din(name, shape, dt=F32):
        return nc.dram_tensor(name, list(shape), dt, kind="ExternalInput").ap()

    def dout(name, shape, dt=F32):
        return nc.dram_tensor(name, list(shape), dt, kind="ExternalOutput").ap()

    def dscr(name, shape, dt=F32):
        if name in dbg:
            return dout(name, shape, dt)
        return nc.dram_tensor(name, list(shape), dt, kind="Internal").ap()

    P = dict(S=S, NT=NT, NQ=NQ, NB=N_BISECT, KSEL=min(TOPK, S // 4), bparts=set(bparts), npairs=npairs,
             CHR=min(512, (NT // 2) * 128))
    NH = NT // 2
    P['x_all'] = din("x_all", [(NH + 1) * 128, D])
    P['ident'] = din("ident", [128, 128])
    P['own_idx'] = din("own_idx", [128, NQA], I32)
    for nm, shp in [("ffn1_wi", [D, 2 * FF]), ("ffn1_wo", [FF, D]), ("ffn2_wi", [D, 2 * FF]), ("ffn2_wo", [FF, D]),
                    ("w_in", [D, N_IN]), ("w_branch_a", [512, D]), ("w_branch_b", [512, D]), ("w_out", [D, D]),
                    ("ln1_g", [1, D]), ("ln1_b", [1, D]), ("ln2_g", [1, D]), ("ln2_b", [1, D]),
                    ("ln3_g", [1, D]), ("ln3_b", [1, D]),
                    ("c_ka", [PAST, 512]), ("c_va", [PAST, 512]), ("c_ki", [PAST, 64]),
                    ("c_kb", [BAND, 512]), ("c_vb", [BAND, 512]),
                    ("admneg", [2, 128, 256]), ("tdsa", [2, 128, 3 * 8 * 128]), ("tband", [2, 128, 6 * 8 * 128]),
                    ("c15", [1, 8]), ("pow2", [1, N_BISECT])]:
        P[nm] = din(nm, shp)
    P['h_all'] = dscr("h_all", [NT * 128, D])
    P['h_half'] = dscr("h_half", [NH * 128, D])
    P['h_samp'] = dscr("h_samp", [128, D])
    for pre, n_a, n_b in [('p_', NT, NT), ('s_', 9, 5)]:
        P[pre + 'KaT'] = dscr(pre + "KaT", [n_a, 64, 1024], BF16)
        P[pre + 'Va'] = dscr(pre + "Va", [n_a, 128, 520], BF16)
        P[pre + 'kiT'] = dscr(pre + "kiT", [64, n_a * 128], BF16)
        P[pre + 'KbT'] = dscr(pre + "KbT", [n_b, 64, 1024], BF16)
        P[pre + 'Vb'] = dscr(pre + "Vb", [n_b, 128, 520], BF16)
    P['q_qaT'] = dscr("q_qaT", [NQA, 64, 1024], BF16)
    P['q_qiT'] = dscr("q_qiT", [NQA, 64, 512], BF16)
    P['q_wi'] = dscr("q_wi", [NQA, 128, 4])
    P['q_qbT'] = dscr("q_qbT", [NQA, 64, 1024], BF16)
    P['q_sga'] = dscr("q_sga", [NQA, 128, D])
    P['q_sgb'] = dscr("q_sgb", [NQA, 128, D])
    P['q_oaT'] = dscr("q_oaT", [NQA, 64, 1024], BF16)
    P['h2_own'] = dscr("h2_own", [NQA * 128, D])
    if 'dbg_score' in dbg:
        NMAX = max(NT * 128, 9 * 128)
        P['dbg_score'] = dscr("dbg_score", [NQA, 128, NMAX])
        P['dbg_junk'] = dscr("dbg_junk", [NQA, 128, NMAX], BF16)
    P['y_own'] = dout("y_own", [NQA * 128, D])
    P['kA_out'] = dout("kA_out", [S, 512]); P['vA_out'] = dout("vA_out", [S, 512])
    P['kidx_out'] = dout("kidx_out", [S, 64])
    P['kB_out'] = dout("kB_out", [512, 512]); P['vB_out'] = dout("vB_out", [512, 512])
    P['skA'] = dout("skA", [128, 512]); P['svA'] = dout("svA", [128, 512]); P['skidx'] = dout("skidx", [128, 64])
    P['skB'] = dout("skB", [512, 512]); P['svB'] = dout("svB", [512, 512])

    with ExitStack() as es:
        g = G(nc, es)
        CHR = P['CHR']
        tpc = CHR // 128

        def after_tile(t):
            if t < NH and (t + 1) % tpc == 0:
                k = t // tpc
                g.op('pool', lambda e: e.collective_compute(
                    "AllGather", ALU.bypass, replica_groups=[[0, 1], [2, 3], [4, 5], [6, 7]][:P['npairs']],
                    ins=[P['h_half'][k * CHR:(k + 1) * CHR, :]],
                    outs=[P['h_all'][2 * k * CHR:2 * (k + 1) * CHR, :]]),
                    reads=[f"dst{tt}" for tt in range(k * tpc, (k + 1) * tpc)], writes=[f"hall{k}"], dma='cc')
        ffn_phase(g, nc, es, P['x_all'],
                  lambda t: P['h_half'][t * 128:(t + 1) * 128, :] if t < NH else P['h_samp'],
                  P['ffn1_wi'], P['ffn1_wo'], P['ln1_g'], P['ln1_b'], P['ident'], NH + 1, after_tile=after_tile)
        if nphases >= 2:
            phase_b(g, es, P)
        if nphases >= 3:
            phase_c1(g, es, P)
        if nphases >= 4:
            phase_c2(g, es, P)
        if nphases >= 5:
            ffn_phase(g, nc, es, P['h2_own'], P['y_own'], P['ffn2_wi'], P['ffn2_wo'], P['ln3_g'], P['ln3_b'],
                      P['ident'], NQA)
    return nc


def _t5_bucket_np(rel):
    import math
    rel = np.asarray(rel, np.int64)
    half, exact = 16, 8
    n = np.abs(rel)
    lr = np.log(np.maximum(n, 1).astype(np.float32) / np.float32(exact)) / np.float32(math.log(128 / exact))
    large = np.minimum(exact + (lr.astype(np.float32) * np.float32(half - exact)).astype(np.int32), half - 1)
    return (rel > 0).astype(np.int32) * half + np.where(n < exact, n, large)


def _hrow(tok, S):
    half = S // 2
    chr_ = min(512, half)
    tok = np.asarray(tok)
    r, w = tok // half, tok % half
    return (2 * (w // chr_) + r) * chr_ + (w % chr_)


def _qpos(r, i):
    p = np.arange(128)
    return np.where(p < 64, (4 * i + r) * 64 + p, (4 * i + 2 + r) * 64 + (p - 64))


def _tables(r, t5_bias, rel_bias):
    kk = np.arange(128)
    i = 2
    qp = _qpos(r, i)
    adm = np.zeros((2, 128, 256), np.float32)
    kc = (2 * i * 128 + np.arange(256)) // 64
    adm[0] = np.where(kc[None, :] <= (qp // 64)[:, None], 0.0, NEG)
    tdsa = np.zeros((2, 128, 3, 8, 128), np.float32)
    for j in range(3):
        kpos = (2 * i - 1 + j) * 128 + kk
        bk = _t5_bucket_np(kpos[:, None] - qp[None, :])
        tdsa[0, :, j] = np.transpose(t5_bias[:, bk], (1, 0, 2))
    tband = np.full((2, 128, 6, 8, 128), MASKNEG, np.float32)
    for j in range(6):
        kpos = (2 * i - 4 + j) * 128 + kk
        rel = kpos[:, None] - qp[None, :]
        vis = ((kpos // 64)[:, None] <= (qp // 64)[None, :]) & ((kpos // 64)[:, None] >= (qp // 64)[None, :] - 8)
        ridx = np.clip(rel, -128, 63) + 128
        vals = np.transpose(rel_bias[:, ridx], (1, 0, 2))
        tband[0, :, j] = np.where(vis[:, None, :], vals, MASKNEG)
    qs = PAST + np.minimum(np.arange(128), 63)
    cols = 7 * 128 + np.arange(256)
    adm[1] = np.where(cols[None, :] < PAST + 64, 0.0, NEG) + np.zeros((128, 1), np.float32)
    for j in range(3):
        kpos = (6 + j) * 128 + kk
        bk = _t5_bucket_np(kpos[:, None] - qs[None, :])
        tdsa[1, :, j] = np.transpose(t5_bias[:, bk], (1, 0, 2))
    for j in range(5):
        kpos = (PAST - BAND) + j * 128 + kk
        rel = kpos[:, None] - qs[None, :]
        vis = (kpos < PAST + 64)[:, None] & np.ones((1, 128), bool)
        ridx = np.clip(rel, -128, 63) + 128
        vals = np.transpose(rel_bias[:, ridx], (1, 0, 2))
        tband[1, :, j] = np.where(vis[:, None, :], vals, MASKNEG)
    return adm, tdsa.reshape(2, 128, -1), tband.reshape(2, 128, -1)


_PROG = {}


def _prep_inputs(inputs, S, n_cores=8):
    NT, NQ = S // 128, S // 256
    f = lambda a: np.ascontiguousarray(a, dtype=np.float32)
    common = {
        "ident": np.eye(128, dtype=np.float32),
        "ffn1_wi": f(inputs['ffn1_wi'][0]), "ffn1_wo": f(inputs['ffn1_wo'][0]),
        "ffn2_wi": f(inputs['ffn2_wi'][0]), "ffn2_wo": f(inputs['ffn2_wo'][0]),
        "w_in": f(inputs['w_in'][0]), "w_branch_a": f(inputs['w_branch_a'][0]),
        "w_branch_b": f(inputs['w_branch_b'][0]), "w_out": f(inputs['w_out'][0]),
        "ln1_g": f(inputs['ln1_g']), "ln1_b": f(inputs['ln1_b']), "ln2_g": f(inputs['ln2_g']),
        "ln2_b": f(inputs['ln2_b']), "ln3_g": f(inputs['ln3_g']), "ln3_b": f(inputs['ln3_b']),
        "c15": f(np.asarray(inputs['t5_bias'])[:, 15][None, :]),
        "pow2": (0.5 ** np.arange(1, N_BISECT + 1, dtype=np.float64)).astype(np.float32)[None, :],
    }
    tabs = [_tables(r, np.asarray(inputs['t5_bias'], np.float32), np.asarray(inputs['rel_bias_b'][0], np.float32))
            for r in range(2)]
    maps = []
    for c in range(n_cores):
        b, r = c // 2, c % 2
        xs = np.zeros((128, D), np.float32)
        xs[:64] = inputs['x_sample'][c]
        own = np.zeros((128, NQ + 1), np.int32)
        for i in range(NQ):
            own[:, i] = _hrow(_qpos(r, i), S)
        own[:, NQ] = 0
        m = dict(common)
        m.update({
            "x_all": np.concatenate([f(inputs['x_prompt'][b, r * (S // 2):(r + 1) * (S // 2)]), xs], 0),
            "own_idx": own,
            "c_ka": f(inputs['cache_k_a'][0, c]).reshape(PAST, 512),
            "c_va": f(inputs['cache_v_a'][0, c]).reshape(PAST, 512),
            "c_ki": f(inputs['cache_kidx_a'][0, c]),
            "c_kb": f(inputs['cache_k_b'][0, c]).reshape(BAND, 512),
            "c_vb": f(inputs['cache_v_b'][0, c]).reshape(BAND, 512),
            "admneg": tabs[r][0], "tdsa": tabs[r][1], "tband": tabs[r][2],
        })
        maps.append(m)
    return maps


def _assemble(results, S, B=4):
    NQ = S // 256
    keep = min(BAND, S)
    y_prompt = np.zeros((B, S, D), np.float32)
    y_sample = np.zeros((8, 64, D), np.float32)
    nka = np.zeros((1, B, S, 8, 64), np.float32); nva = np.zeros_like(nka)
    nki = np.zeros((1, B, S, 64), np.float32)
    nkb = np.zeros((1, B, keep, 8, 64), np.float32); nvb = np.zeros_like(nkb)
    ska = np.zeros((1, 8, 64, 8, 64), np.float32); sva = np.zeros_like(ska)
    ski = np.zeros((1, 8, 64, 64), np.float32)
    skb = np.zeros((1, 8, BAND, 8, 64), np.float32); svb = np.zeros_like(skb)
    for c, res in enumerate(results):
        b, r = c // 2, c % 2
        yo = res['y_own']
        for i in range(NQ):
            qp = _qpos(r, i)
            y_prompt[b, qp] = yo[i * 128:(i + 1) * 128]
        y_sample[c] = yo[NQ * 128:NQ * 128 + 64]
        if r == 0:
            nka[0, b] = res['kA_out'].reshape(S, 8, 64)
            nva[0, b] = res['vA_out'].reshape(S, 8, 64)
            nki[0, b] = res['kidx_out']
            nkb[0, b] = res['kB_out'].reshape(BAND, 8, 64)[-keep:]
            nvb[0, b] = res['vB_out'].reshape(BAND, 8, 64)[-keep:]
        ska[0, c] = res['skA'][:64].reshape(64, 8, 64)
        sva[0, c] = res['svA'][:64].reshape(64, 8, 64)
        ski[0, c] = res['skidx'][:64]
        skb[0, c] = res['skB'].reshape(BAND, 8, 64)
        svb[0, c] = res['svB'].reshape(BAND, 8, 64)
    return (y_prompt, y_sample, nka, nva, nki, nkb, nvb, ska, sva, ski, skb, svb)


def kernel(**inputs):
    S = int(np.asarray(inputs['x_prompt']).shape[1])
    if S not in _PROG:
        _PROG[S] = build_program(S)
    nc = _PROG[S]
    maps = _prep_inputs(inputs, S)
    res = run_bass_kernel_spmd(nc, maps, core_ids=list(range(8)))
    return _assemble(res.results, S)
```

```python
import numpy as np
import concourse.bass as bass
import concourse.mybir as mybir
from concourse.bass_utils import run_bass_kernel_spmd
from contextlib import ExitStack

F32 = mybir.dt.float32
BF16 = mybir.dt.bfloat16
I32 = mybir.dt.int32
AF = mybir.ActivationFunctionType
ALU = mybir.AluOpType
AX = mybir.AxisListType
ENGS = ['pe', 'act', 'dve', 'pool', 'sp']

D = 1024
FF = 2816
NFF = 22
HD = 64
NH = 8
CHUNK = 64
PAST = 1024
BAND = 512
TOPK = 256
LN_EPS = 1e-5
ALPHA = 2.0 ** 0.25
NEG = -1e30
MASKNEG = -30000.0
N_BISECT = 22
C_QA, C_KA, C_VA, C_QI, C_KI, C_WI, C_QB, C_KB, C_VB, C_GA, C_GB = (
    0, 512, 1024, 1536, 1792, 1856, 1860, 2372, 2884, 3396, 4420)
N_IN = 5444


class G:
    NDS = {'sp': 40, 'pool': 24, 'act': 4, 'pe': 0, 'dve': 0}

    def __init__(self, nc, es):
        self.nc = nc
        self.es = es
        self.ops = {e: [] for e in ENGS}
        self.last_w = {}
        self.readers = {}
        self.nsb = 0
        self.sems = {}
        for e in ENGS:
            self.sems[('c', e)] = es.enter_context(nc.semaphore(f"c_{e}"))
            for j in range(self.NDS[e]):
                self.sems[('d', e, j)] = es.enter_context(nc.semaphore(f"d_{e}_{j}"))
        self.sems[('cc',)] = es.enter_context(nc.semaphore("cc_sem"))
        self.ncc = 0
        self.ccount = {e: 0 for e in ENGS}
        self.dnum = {e: 0 for e in ENGS}
        self.rr = 0

    def sb(self, shape, dt, name=None):
        self.nsb += 1
        return self.es.enter_context(self.nc.sbuf_tensor(name or f"sb{self.nsb}", list(shape), dt))

    def ps(self, shape, dt=F32, name=None):
        self.nsb += 1
        return self.es.enter_context(self.nc.psum_tensor(name or f"ps{self.nsb}", list(shape), dt))

    def op(self, eng, fn, reads=(), writes=(), dma=False):
        idx = len(self.ops[eng])
        deps = set()
        for k in reads:
            w = self.last_w.get(k)
            if w is not None:
                deps.add(w)
        for k in writes:
            w = self.last_w.get(k)
            if w is not None:
                deps.add(w)
            for r in self.readers.get(k, ()):
                deps.add(r)
        deps.discard((eng, idx))
        for k in writes:
            self.last_w[k] = (eng, idx)
            self.readers[k] = []
        for k in reads:
            self.readers.setdefault(k, []).append((eng, idx))
        self.ops[eng].append(dict(fn=fn, deps=deps, dma=dma, signal=False, hval=None, pre=None))
        return (eng, idx)

    def emit(self):
        nc = self.nc
        sems = self.sems
        ops = self.ops
        for e in ENGS:
            for o in ops[e]:
                for (de, di) in o['deps']:
                    d = ops[de][di]
                    if de == 'pe' and e == 'pe' and not d['dma']:
                        continue
                    d['signal'] = True
            for o in reversed(ops[e]):
                if not o['dma']:
                    o['signal'] = True
                    break
        for e in ENGS:
            for o in ops[e]:
                if o['dma'] == 'cc':
                    self.ncc += 1
                    o['hval'] = (('cc',), self.ncc)
                    o['pre'] = None
                elif o['dma']:
                    m = self.dnum[e]
                    self.dnum[e] += 1
                    j, rnd = m % self.NDS[e], m // self.NDS[e]
                    o['hval'] = (('d', e, j), 16 * (rnd + 1))
                    o['pre'] = (('d', e, j), 16 * rnd) if rnd > 0 else None
                elif o['signal']:
                    self.ccount[e] += 1
                    o['hval'] = (('c', e), self.ccount[e])
        ccount = dict(self.ccount)
        dfinal = {}
        for e in ENGS:
            for j in range(self.NDS[e]):
                n_j = (self.dnum[e] - j + self.NDS[e] - 1) // self.NDS[e] if self.dnum[e] > j else 0
                if n_j > 0:
                    dfinal[('d', e, j)] = 16 * n_j
        if self.ncc > 0:
            dfinal[('cc',)] = self.ncc
        with nc.Block() as block:
            def mk(e):
                def body(engine):
                    seen = {}

                    def wait(sk, v):
                        if v <= 0 or seen.get(sk, 0) >= v:
                            return
                        engine.wait_ge(sems[sk], v)
                        seen[sk] = v
                    for o in ops[e]:
                        for (de, di) in sorted(o['deps']):
                            d = ops[de][di]
                            if de == 'pe' and e == 'pe' and not d['dma']:
                                continue
                            wait(*d['hval'])
                        if o['pre'] is not None:
                            wait(*o['pre'])
                        ins = o['fn'](engine)
                        if o['dma'] == 'cc':
                            ins.then_inc(sems[o['hval'][0]])
                        elif o['dma']:
                            ins.then_inc(sems[o['hval'][0]], 16)
                        elif o['signal']:
                            ins.then_inc(sems[o['hval'][0]], 1)
                    for e2 in ENGS:
                        if ccount[e2] > 0:
                            wait(('c', e2), ccount[e2])
                    for sk, v in dfinal.items():
                        wait(sk, v)
                return body
            block.tensor(mk('pe'))
            block.scalar(mk('act'))
            block.vector(mk('dve'))
            block.gpsimd(mk('pool'))
            block.sync(mk('sp'))
        self.ops = {e: [] for e in ENGS}
        self.last_w = {}
        self.readers = {}

    def dma(self, q, out, in_, r, w):
        return self.op(q, lambda e: e.dma_start(out=out, in_=in_), reads=r, writes=w, dma=True)

    def mm(self, out, lhsT, rhs, start, stop, r, w, skip=False):
        return self.op('pe', lambda e: e.matmul(out=out, lhsT=lhsT, rhs=rhs, start=start, stop=stop,
                                                skip_group_check=skip), reads=r, writes=w)

    def tr(self, out, in_, ident, r, w):
        return self.op('pe', lambda e: e.transpose(out=out, in_=in_, identity=ident), reads=r, writes=w)

    def act(self, out, in_, func, r, w, scale=None, bias=None, accum_out=None):
        kw = {}
        if scale is not None:
            kw['scale'] = scale
        if bias is not None:
            kw['bias'] = bias
        if accum_out is not None:
            kw['accum_out'] = accum_out
        return self.op('act', lambda e: e.activation(out=out, in_=in_, func=func, **kw), reads=r, writes=w)

    def ts(self, eng, out, in0, s1, s2, op0, op1, r, w, accum_out=None):
        kw = {}
        if op1 is not None:
            kw['op1'] = op1
        if accum_out is not None:
            kw['accum_out'] = accum_out
        return self.op(eng, lambda e: e.tensor_scalar(out=out, in0=in0, scalar1=s1, scalar2=s2, op0=op0, **kw),
                       reads=r, writes=w)

    def tt(self, eng, out, in0, in1, op, r, w):
        return self.op(eng, lambda e: e.tensor_tensor(out=out, in0=in0, in1=in1, op=op), reads=r, writes=w)

    def stt(self, out, in0, scalar, in1, op0, op1, r, w):
        return self.op('dve', lambda e: e.scalar_tensor_tensor(out=out, in0=in0, scalar=scalar, in1=in1,
                                                               op0=op0, op1=op1), reads=r, writes=w)

    def cp(self, eng, out, in_, r, w):
        if eng == 'act':
            return self.op('act', lambda e: e.activation(out=out, in_=in_, func=AF.Copy), reads=r, writes=w)
        return self.op(eng, lambda e: e.tensor_copy(out=out, in_=in_), reads=r, writes=w)

    def cast_eng(self):
        self.rr += 1
        return ['dve', 'act', 'dve'][self.rr % 3]


def load_weight_bf16(g, dst, src, kchunks, ncols, stage, stage_keys, dst_key, piece=2048):
    i = 0
    for kc in range(kchunks):
        for c0 in range(0, ncols, piece):
            n = min(piece, ncols - c0)
            st, sk = stage[i % len(stage)], stage_keys[i % len(stage)]
            i += 1
            g.dma('sp', st[:, 0:n], src[kc * 128:(kc + 1) * 128, c0:c0 + n], [], [sk])
            g.cp(g.cast_eng(), dst[:, kc, c0:c0 + n], st[:, 0:n], [sk], [dst_key + f"_{kc}_{c0}"])


def rsqrt_newton(g, var_ap, rs, small, rkeys, tag):
    v, ti, ui, b, t = small['v'], small['ti'], small['ui'], small['b'], small['t']
    g.ts('dve', v[:], var_ap, LN_EPS, None, ALU.add, None, rkeys, [tag + 'v'])
    g.ts('dve', ti[:], v[:].bitcast(I32), 1, None, ALU.arith_shift_right, None, [tag + 'v'], [tag + 'ti'])
    g.ts('dve', ui[:], ti[:], -1.0, 1597463007.0, ALU.mult, ALU.add, [tag + 'ti'], [tag + 'y'])
    y = ui[:].bitcast(F32)
    for it in range(3):
        g.stt(b[:], y, v[:], y, ALU.mult, ALU.mult, [tag + 'y', tag + 'v'], [tag + 'b'])
        g.stt(t[:], b[:], -0.5, y, ALU.mult, ALU.mult, [tag + 'b', tag + 'y'], [tag + 't'])
        dst = rs[:] if it == 2 else y
        g.stt(dst, y, 1.5, t[:], ALU.mult, ALU.add, [tag + 'y', tag + 't'], [tag + ('rs' if it == 2 else 'y')])


def make_small(g):
    return dict(st=g.sb([128, 2, 6], F32), mv=g.sb([128, 2], F32), rs=g.sb([128, 1], F32),
                nb=g.sb([128, 1], F32), v=g.sb([128, 1], F32), ti=g.sb([128, 1], I32), ui=g.sb([128, 1], I32),
                b=g.sb([128, 1], F32), t=g.sb([128, 1], F32))


def layer_norm_tile(g, y, ykey, gbc, bbc, small, tag):
    st, mv, rs, nb = small['st'], small['mv'], small['rs'], small['nb']
    for hh in range(2):
        g.op('dve', lambda e, hh=hh: e.bn_stats(out=st[:, hh, :], in_=y[:, hh * 512:(hh + 1) * 512]),
             reads=[ykey], writes=[tag + 'st'])
    g.op('dve', lambda e: e.bn_aggr(out=mv[:], in_=st[:].rearrange('p a b -> p (a b)')), reads=[tag + 'st'], writes=[tag + 'mv'])
    rsqrt_newton(g, mv[:, 1:2], rs, small, [tag + 'mv'], tag)
    g.stt(nb[:], mv[:, 0:1], -1.0, rs[:], ALU.mult, ALU.mult, [tag + 'mv', tag + 'rs'], [tag + 'nb'])
    g.act(y, y, AF.Identity, [ykey, tag + 'rs', tag + 'nb'], [ykey], scale=rs[:], bias=nb[:])
    g.tt('dve', y, y, gbc[:], ALU.mult, [ykey, 'lng'], [ykey])
    g.tt('pool', y, y, bbc[:], ALU.add, [ykey, 'lnb'], [ykey])


def ffn_phase(g, nc, es_outer, src, dst, wi, wo, lng, lnb, ident_d, ntiles, T=4, after_tile=None):
    TN = T * 128
    with ExitStack() as es:
        g.es = es
        wi_b = g.sb([128, 8, 2 * FF], BF16)
        wo_b = g.sb([128, NFF, D], BF16)
        xin = g.sb([128, T, D], F32)
        xb = g.sb([128, D], BF16)
        xT = g.sb([128, 8, TN], BF16)
        gT = g.sb([128, NFF, TN], BF16)
        sil = g.sb([128, TN], F32)
        gbc = g.sb([128, D], F32)
        bbc = g.sb([128, D], F32)
        idf = g.sb([128, 128], F32)
        idb = g.sb([128, 128], BF16)
        small = [make_small(g) for _ in range(2)]
        up = [g.ps([128, 512], F32) for _ in range(4)]
        dn = [g.ps([128, 512], F32) for _ in range(2)]
        tp = g.ps([128, 8, 128], BF16)

        g.dma('sp', idf[:], ident_d, [], ['idf'])
        g.cp('dve', idb[:], idf[:], ['idf'], ['idb'])
        g.dma('sp', gbc[:], lng.to_broadcast([128, D]), [], ['lng'])
        g.dma('sp', bbc[:], lnb.to_broadcast([128, D]), [], ['lnb'])
        xflat = xin[:].rearrange("p a b -> p (a b)")
        stages = [xflat[:, q * 1024:(q + 1) * 1024] for q in range(T)]
        skeys = [f"xin_t{q}" for q in range(T)]
        load_weight_bf16(g, wi_b, wi, 8, 2 * FF, stages, skeys, 'wi', piece=1024)
        load_weight_bf16(g, wo_b, wo, NFF, D, stages, skeys, 'wo', piece=1024)
        wi_keys = [f"wi_{kc}_{c0}" for kc in range(8) for c0 in range(0, 2 * FF, 1024)]
        wo_keys = [f"wo_{kc}_0" for kc in range(NFF)]
        g.op('pool', lambda e: e.memset(sil[:, 0:1], 0.0), reads=wi_keys + wo_keys,
             writes=[f"xin_t{j}" for j in range(T)] + ['sil'])

        nst = (ntiles + T - 1) // T
        for st_i in range(nst):
            t0 = st_i * T
            Tc = min(T, ntiles - t0)
            Tn = Tc * 128
            for j in range(Tc):
                xk = f"xin_t{j}"
                g.dma('sp', xin[:, j, :], src[(t0 + j) * 128:(t0 + j + 1) * 128, :], [], [xk])
            for j in range(Tc):
                xk = f"xin_t{j}"
                g.cp('pool', xb[:], xin[:, j, :], [xk], ['xb'])
                g.act(xin[:, j, :], xin[:, j, :], AF.Copy, [xk, 'xb'], [xk], scale=ALPHA)
                for k in range(8):
                    g.tr(tp[:, k, :], xb[:, k * 128:(k + 1) * 128], idb[:], ['xb', 'idb'], ['tp'])
                g.cp('dve', xT[:, :, j * 128:(j + 1) * 128], tp[:], ['tp'], [f"xT{j}"])
            xTk = [f"xT{j}" for j in range(Tc)]
            for c in range(NFF):
                pa, pu = up[2 * (c % 2)], up[2 * (c % 2) + 1]
                ka, ku = f"up{2 * (c % 2)}", f"up{2 * (c % 2) + 1}"
                for k in range(8):
                    g.mm(pa[:, 0:Tn], wi_b[:, k, c * 128:(c + 1) * 128], xT[:, k, 0:Tn], k == 0, k == 7,
                         xTk + (wi_keys if st_i == 0 else []), [ka])
                for k in range(8):
                    g.mm(pu[:, 0:Tn], wi_b[:, k, FF + c * 128:FF + (c + 1) * 128], xT[:, k, 0:Tn],
                         k == 0, k == 7, xTk, [ku])
                g.act(sil[:, 0:Tn], pa[:, 0:Tn], AF.Tanh, [ka], ['sil'], scale=0.5)
                g.stt(sil[:, 0:Tn], sil[:, 0:Tn], 1.0, pa[:, 0:Tn], ALU.add, ALU.mult, ['sil', ka], ['sil'])
                g.stt(gT[:, c, 0:Tn], sil[:, 0:Tn], 0.5, pu[:, 0:Tn], ALU.mult, ALU.mult, ['sil', ku], [f"gT{c}"])
            gk = [f"gT{c}" for c in range(NFF)]
            for j in range(Tc):
                yk = f"xin_t{j}"
                for hh in range(2):
                    for c in range(NFF):
                        g.mm(dn[hh][:], gT[:, c, j * 128:(j + 1) * 128], wo_b[:, c, hh * 512:(hh + 1) * 512],
                             c == 0, c == NFF - 1, gk + (wo_keys if st_i == 0 else []), [f"dn{hh}"])
                    ysl = xin[:, j, hh * 512:(hh + 1) * 512]
                    g.stt(ysl, dn[hh][:], 0.5, ysl, ALU.mult, ALU.add, [f"dn{hh}", yk], [yk])
                layer_norm_tile(g, xin[:, j, :], yk, gbc, bbc, small[j % 2], f"ln{j % 2}")
                dap = dst(t0 + j) if callable(dst) else dst[(t0 + j) * 128:(t0 + j + 1) * 128, :]
                g.dma('pool', dap, xin[:, j, :], [yk], [yk, f"dst{t0 + j}"])
                if after_tile is not None:
                    after_tile(t0 + j)
        g.es = es_outer
        g.emit()


def proj_tile(g, b, bufs, w_b, groups):
    hf, hb, hT, tp, idb = bufs['hf'][b], bufs['hb'][b], bufs['hT'][b], bufs['tp'], bufs['idb']
    g.cp('pool', hb[:], hf[:], [f"hf{b}"], [f"hb{b}"])
    for k in range(8):
        g.tr(tp[:, k * 128:(k + 1) * 128], hb[:, k * 128:(k + 1) * 128], idb[:], [f"hb{b}", 'idb'], ['tp'])
    g.cp('dve', hT[:].rearrange("p a b -> p (a b)"), tp[:, 0:1024], ['tp'], [f"hT{b}"])
    for (pap, key, c0, n) in groups:
        for k in range(8):
            g.mm(pap, hT[:, k, :], w_b[:, k, c0:c0 + n], k == 0, k == 7, [f"hT{b}", 'w_in'], [key])


def headT(g, src_bf, src_key, nheads, tp, idb, dstT, dst_key):
    for h in range(nheads):
        g.tr(tp[0:64, h * 128:(h + 1) * 128], src_bf[:, h * 64:(h + 1) * 64], idb[:], [src_key, 'idb'], ['tp'])
    g.cp('act', dstT[:].rearrange("p a b -> p (a b)"), tp[0:64, 0:nheads * 128], ['tp'], [dst_key])


def phase_b(g, es_outer, P):
    S, NT, NQ = P['S'], P['NT'], P['NQ']
    with ExitStack() as es:
        g.es = es
        w_b = g.sb([128, 8, N_IN], BF16)
        stage = [g.sb([128, 2048], F32) for _ in range(2)]
        idf = g.sb([128, 128], F32)
        idb = g.sb([128, 128], BF16)
        idx_sb = g.sb([128, NQ + 1], I32)
        bufs = dict(hf=[g.sb([128, D], F32) for _ in range(2)], hb=[g.sb([128, D], BF16) for _ in range(2)],
                    hT=[g.sb([128, 8, 128], BF16) for _ in range(2)], idb=idb)
        kaf = g.sb([128, 512], F32); vaf = g.sb([128, 512], F32); kif = g.sb([128, 64], F32)
        kbf = g.sb([128, 512], F32); vbf = g.sb([128, 512], F32)
        kab = g.sb([128, 512], BF16); kib = g.sb([128, 64], BF16); kbb = g.sb([128, 512], BF16)
        vaug = g.sb([128, 8, 65], BF16); vbug = g.sb([128, 8, 65], BF16)
        kaT = g.sb([64, 8, 128], BF16); kbT = g.sb([64, 8, 128], BF16); kiT = g.sb([64, 1, 128], BF16)
        qab = g.sb([128, 512], BF16); qib = g.sb([128, 256], BF16); qbb = g.sb([128, 512], BF16)
        qaT = g.sb([64, 8, 128], BF16); qiT = g.sb([64, 4, 128], BF16); qbT = g.sb([64, 8, 128], BF16)
        wif = g.sb([128, 4], F32)
        sga = g.sb([128, D], F32); sgb = g.sb([128, D], F32)
        pb = [g.ps([128, 512], F32) for _ in range(7)]
        tpf = g.ps([128, 512], F32)
        tp = tpf[:].bitcast(BF16)
        bufs['tp'] = tp

        g.dma('sp', idf[:], P['ident'], [], ['idf'])
        g.cp('dve', idb[:], idf[:], ['idf'], ['idb'])
        g.dma('sp', idx_sb[:], P['own_idx'], [], ['idx'])
        st4 = [stage[q // 2][:, (q % 2) * 1024:(q % 2 + 1) * 1024] for q in range(4)]
        load_weight_bf16(g, w_b, P['w_in'], 8, N_IN, st4, ['stg0', 'stg1', 'stg2', 'stg3'], 'w_in_p', piece=1024)
        wkeys = [f"w_in_p_{kc}_{c0}" for kc in range(8) for c0 in range(0, N_IN, 1024)]
        g.op('pe', lambda e: e.nop(), reads=wkeys, writes=['w_in'])
        g.op('pool', lambda e: e.memset(vaug[:], 1.0), [], ['vaug'])
        g.op('pool', lambda e: e.memset(vbug[:], 1.0), [], ['vbug'])

        def kside_ingest(kind, t, src):
            pre = kind + '_'
            if src.get('ka') is not None:
                ap, key = src['ka']
                g.cp('act', kaf[:], ap, [key], ['kaf'])
                g.cp('dve', kab[:], kaf[:], ['kaf'], ['kab'])
                headT(g, kab, 'kab', 8, tp, idb, kaT, 'kaT')
                g.dma('sp', P[pre + 'KaT'][t], kaT[:].rearrange("p a b -> p (a b)"), ['kaT'], ['kaT', f"{pre}KaT{t}"])
                ap, key = src['va']
                g.cp('act', vaf[:], ap, [key], ['vaf'])
                g.cp('dve', vaug[:, :, 0:64], vaf[:].rearrange("p (h d) -> p h d", h=8), ['vaf'], ['vaug'])
                g.dma('sp', P[pre + 'Va'][t], vaug[:].rearrange("p a b -> p (a b)"), ['vaug'], ['vaug', f"{pre}Va{t}"])
                ap, key = src['ki']
                g.cp('act', kif[:], ap, [key], ['kif'])
                g.cp('dve', kib[:], kif[:], ['kif'], ['kib'])
                g.tr(tp[0:64, 0:128], kib[:], idb[:], ['kib', 'idb'], ['tp'])
                g.cp('act', kiT[:, 0, :], tp[0:64, 0:128], ['tp'], ['kiT'])
                kdst = P[pre + 'kiT'][t] if kind == 'l' else P[pre + 'kiT'][:, t * 128:(t + 1) * 128]
                g.dma('sp', kdst, kiT[:, 0, :], ['kiT'], ['kiT', f"{pre}kiT{t}"])
            if src.get('kb') is not None:
                tb = src['tb']
                ap, key = src['kb']
                g.cp('act', kbf[:], ap, [key], ['kbf'])
                g.cp('dve', kbb[:], kbf[:], ['kbf'], ['kbb'])
                headT(g, kbb, 'kbb', 8, tp, idb, kbT, 'kbT')
                g.dma('sp', P[pre + 'KbT'][tb], kbT[:].rearrange("p a b -> p (a b)"), ['kbT'], ['kbT', f"{pre}KbT{tb}"])
                ap, key = src['vb']
                g.cp('act', vbf[:], ap, [key], ['vbf'])
                g.cp('dve', vbug[:, :, 0:64], vbf[:].rearrange("p (h d) -> p h d", h=8), ['vbf'], ['vbug'])
                g.dma('sp', P[pre + 'Vb'][tb], vbug[:].rearrange("p a b -> p (a b)"), ['vbug'], ['vbug', f"{pre}Vb{tb}"])

        NH = NT // 2
        CT = min(8, NH)

        def ld(t):
            srcap = P['h_half'][t * 128:(t + 1) * 128, :] if t < NH else P['h_samp']
            g.dma('sp', bufs['hf'][t % 2][:], srcap, [], [f"hf{t % 2}"])

        def gather_chunk(k):
            for nm, rows in [('KaT', 64), ('Va', 128), ('kiT', 64), ('KbT', 64), ('Vb', 128)]:
                src2 = P['l_' + nm].rearrange("t p c -> (t p) c")
                dst2 = P['p_' + nm].rearrange("t p c -> (t p) c")
                g.op('pool', lambda e, src2=src2, dst2=dst2, rows=rows: e.collective_compute(
                    "AllGather", ALU.bypass, replica_groups=[[0, 1], [2, 3], [4, 5], [6, 7]][:P['npairs']],
                    ins=[src2[k * CT * rows:(k + 1) * CT * rows, :]],
                    outs=[dst2[2 * k * CT * rows:2 * (k + 1) * CT * rows, :]]),
                    reads=[f"l_{nm}{tt}" for tt in range(k * CT, (k + 1) * CT)], writes=[f"p_{nm}_c{k}"], dma='cc')
        ld(0)
        for t in range(NH + 1):
            groups = [(pb[0][:], 'pb0', C_KA, 512), (pb[1][:], 'pb1', C_VA, 512), (pb[2][:, 0:64], 'pb2', C_KI, 64),
                      (pb[3][:], 'pb3', C_KB, 512), (pb[4][:], 'pb4', C_VB, 512)]
            proj_tile(g, t % 2, bufs, w_b, groups)
            if t + 1 <= NH:
                ld(t + 1)
            src = dict(ka=(pb[0][:], 'pb0'), va=(pb[1][:], 'pb1'), ki=(pb[2][:, 0:64], 'pb2'),
                       kb=(pb[3][:], 'pb3'), vb=(pb[4][:], 'pb4'))
            if t < NH:
                src['tb'] = t
                kside_ingest('l', t, src)
                g.dma('pool', P['kA_out'][t * 128:(t + 1) * 128, :], kaf[:], ['kaf'], ['kaf'])
                g.dma('pool', P['vA_out'][t * 128:(t + 1) * 128, :], vaf[:], ['vaf'], ['vaf'])
                g.dma('pool', P['kidx_out'][t * 128:(t + 1) * 128, :], kif[:], ['kif'], ['kif'])
                if t >= NH - 4:
                    o = (t - (NH - 4)) * 128
                    g.dma('pool', P['kB_out'][o:o + 128, :], kbf[:], ['kbf'], ['kbf'])
                    g.dma('pool', P['vB_out'][o:o + 128, :], vbf[:], ['vbf'], ['vbf'])
                if (t + 1) % CT == 0:
                    gather_chunk(t // CT)
            else:
                src['tb'] = 4
                kside_ingest('s', 8, src)
                g.dma('pool', P['skA'], kaf[:], ['kaf'], ['kaf'])
                g.dma('pool', P['svA'], vaf[:], ['vaf'], ['vaf'])
                g.dma('pool', P['skidx'], kif[:], ['kif'], ['kif'])
                g.dma('pool', P['skB'][448:512, :], kbf[0:64, :], ['kbf'], ['kbf'])
                g.dma('pool', P['svB'][448:512, :], vbf[0:64, :], ['vbf'], ['vbf'])
        if 'roll' in P['bparts']:
            g.dma('pool', P['skB'][0:448, :], P['c_kb'][64:512, :], [], ['skB_roll'])
            g.dma('pool', P['svB'][0:448, :], P['c_vb'][64:512, :], [], ['svB_roll'])
        for t in range(8 if 'cache' in P['bparts'] else 0):
            g.dma('sp', kaf[:], P['c_ka'][t * 128:(t + 1) * 128, :], [], ['kaf'])
            g.dma('sp', vaf[:], P['c_va'][t * 128:(t + 1) * 128, :], [], ['vaf'])
            g.dma('sp', kif[:], P['c_ki'][t * 128:(t + 1) * 128, :], [], ['kif'])
            g.cp('dve', kab[:], kaf[:], ['kaf'], ['kab'])
            headT(g, kab, 'kab', 8, tp, idb, kaT, 'kaT')
            g.dma('sp', P['s_KaT'][t], kaT[:].rearrange("p a b -> p (a b)"), ['kaT'], ['kaT'])
            g.cp('pool', vaug[:, :, 0:64], vaf[:].rearrange("p (h d) -> p h d", h=8), ['vaf'], ['vaug'])
            g.dma('sp', P['s_Va'][t], vaug[:].rearrange("p a b -> p (a b)"), ['vaug'], ['vaug'])
            g.cp('dve', kib[:], kif[:], ['kif'], ['kib'])
            g.tr(tp[0:64, 0:128], kib[:], idb[:], ['kib', 'idb'], ['tp'])
            g.cp('act', kiT[:, 0, :], tp[0:64, 0:128], ['tp'], ['kiT'])
            g.dma('sp', P['s_kiT'][:, t * 128:(t + 1) * 128], kiT[:, 0, :], ['kiT'], ['kiT'])
        for t in range(4 if 'cache' in P['bparts'] else 0):
            g.dma('sp', kbf[:], P['c_kb'][t * 128:(t + 1) * 128, :], [], ['kbf'])
            g.dma('sp', vbf[:], P['c_vb'][t * 128:(t + 1) * 128, :], [], ['vbf'])
            g.cp('dve', kbb[:], kbf[:], ['kbf'], ['kbb'])
            headT(g, kbb, 'kbb', 8, tp, idb, kbT, 'kbT')
            g.dma('sp', P['s_KbT'][t], kbT[:].rearrange("p a b -> p (a b)"), ['kbT'], ['kbT'])
            g.cp('pool', vbug[:, :, 0:64], vbf[:].rearrange("p (h d) -> p h d", h=8), ['vbf'], ['vbug'])
            g.dma('sp', P['s_Vb'][t], vbug[:].rearrange("p a b -> p (a b)"), ['vbug'], ['vbug'])

        def ldq(i):
            hf = bufs['hf'][i % 2]
            if i == NQ:
                g.dma('pool', hf[:], P['h_samp'], [], [f"hf{i % 2}"])
                return
            g.op('pool', lambda e: e.indirect_dma_start(
                out=hf[:], out_offset=None, in_=P['h_all'],
                in_offset=bass.IndirectOffsetOnAxis(ap=idx_sb[:, i:i + 1], axis=0)),
                reads=['idx'], writes=[f"hf{i % 2}"], dma=True)
        nb2 = NQ + 1 if 'b2' in P['bparts'] else 0
        if nb2:
            ldq(0)
        for i in range(nb2):
            groups = [(pb[0][:], 'pb0', C_QA, 512), (pb[1][:, 0:256], 'pb1', C_QI, 256),
                      (pb[1][:, 256:260], 'pb1', C_WI, 4), (pb[2][:], 'pb2', C_QB, 512),
                      (pb[3][:], 'pb3', C_GA, 512), (pb[4][:], 'pb4', C_GA + 512, 512),
                      (pb[5][:], 'pb5', C_GB, 512), (pb[6][:], 'pb6', C_GB + 512, 512)]
            proj_tile(g, i % 2, bufs, w_b, groups)
            if i + 1 < nb2:
                ldq(i + 1)
            g.cp('dve', qab[:], pb[0][:], ['pb0'], ['qab'])
            headT(g, qab, 'qab', 8, tp, idb, qaT, 'qaT')
            g.dma('sp', P['q_qaT'][i], qaT[:].rearrange("p a b -> p (a b)"), ['qaT'], ['qaT'])
            g.cp('dve', qib[:], pb[1][:, 0:256], ['pb1'], ['qib'])
            g.cp('dve', wif[:], pb[1][:, 256:260], ['pb1'], ['wif'])
            headT(g, qib, 'qib', 4, tp, idb, qiT, 'qiT')
            g.dma('sp', P['q_qiT'][i], qiT[:].rearrange("p a b -> p (a b)"), ['qiT'], ['qiT'])
            g.dma('sp', P['q_wi'][i], wif[:], ['wif'], ['wif'])
            g.cp('dve', qbb[:], pb[2][:], ['pb2'], ['qbb'])
            headT(g, qbb, 'qbb', 8, tp, idb, qbT, 'qbT')
            g.dma('sp', P['q_qbT'][i], qbT[:].rearrange("p a b -> p (a b)"), ['qbT'], ['qbT'])
            for hh in range(2):
                g.act(sga[:, hh * 512:(hh + 1) * 512], pb[3 + hh][:], AF.Tanh, [f"pb{3 + hh}"], ['sga'], scale=0.5)
                g.act(sgb[:, hh * 512:(hh + 1) * 512], pb[5 + hh][:], AF.Tanh, [f"pb{5 + hh}"], ['sgb'], scale=0.5)
            g.ts('pool', sga[:], sga[:], 1.0, 0.5, ALU.add, ALU.mult, ['sga'], ['sga'])
            g.ts('pool', sgb[:], sgb[:], 1.0, 0.5, ALU.add, ALU.mult, ['sgb'], ['sgb'])
            g.dma('sp', P['q_sga'][i], sga[:], ['sga'], ['sga'])
            g.dma('sp', P['q_sgb'][i], sgb[:], ['sgb'], ['sgb'])
        g.es = es_outer
        g.emit()


def attn_core(g, bufs, kts, kt_src, vt_src, qT, qkey, table, tslot_fn, mask_fn, outT, out_key, mid_at=None,
              mid_fn=None, defer_norm=False):
    S_ps, O_ps, bc_ps = bufs['S_ps'], bufs['O_ps'], bufs['bc_ps']
    kt_t, vt_t, Sb, Pb = bufs['kt_t'], bufs['vt_t'], bufs['Sb'], bufs['Pb']
    oT, rden, ones, irep = bufs['oT'], bufs['rden'], bufs['ones'], bufs['irep']
    NBUF = len(kt_t)
    NS = len(S_ps)
    n = len(kts)
    items = [(p, hg) for p in range(n) for hg in range(2)]

    def load(p):
        b = p % NBUF
        g.dma('sp', kt_t[b][:].rearrange("p a b -> p (a b)"), kt_src(kts[p]), [], [f"kt{b}"])
        g.dma('sp', vt_t[b][:], vt_src(kts[p]), [], [f"vt{b}"])

    def stage_s(j):
        p, hg = items[j]
        kt = kts[p]
        b = p % NBUF
        sp_ = S_ps[j % NS]
        sk = f"S{j % NS}"
        mk = mask_fn(kt) if mask_fn is not None else None
        if mk is not None:
            g.mm(sp_[:, 0:512], mk[0], irep[:], True, False, [mk[1], 'irep'], [sk], skip=True)
        for hh in range(4):
            h = hg * 4 + hh
            g.mm(sp_[:, hh * 128:(hh + 1) * 128], kt_t[b][:, h, :], qT[:, h, :], mk is None and hh == 0, True,
                 [f"kt{b}", qkey], [sk], skip=True)

    def stage_e(j):
        p, hg = items[j]
        slot = tslot_fn(kts[p])
        sp_ = S_ps[j % NS]
        sk = f"S{j % NS}"
        pk = f"P{j % NS}"
        if slot is not None:
            sb_ = Sb[j % 2]
            g.stt(sb_[:], sp_[:].rearrange("p (a b) -> p a b", a=4), 0.125,
                  table[:, slot, hg * 4:(hg + 1) * 4, :], ALU.mult, ALU.add, [sk, 'table'], [f"Sb{j % 2}"])
            g.act(Pb[j % NS][:], sb_[:], AF.Exp, [f"Sb{j % 2}"], [pk])
        else:
            g.act(Pb[j % NS][:], sp_[:].rearrange("p (a b) -> p a b", a=4), AF.Exp, [sk], [pk], scale=0.125)

    def stage_v(j):
        p, hg = items[j]
        b = p % NBUF
        for hh in range(4):
            h = hg * 4 + hh
            g.mm(O_ps[hg][0:65, hh * 128:(hh + 1) * 128], vt_t[b][:, h * 65:(h + 1) * 65], Pb[j % NS][:, hh, :],
                 p == 0 and hh == 0, p == n - 1, [f"vt{b}", f"P{j % NS}"], [f"O{hg}"], skip=True)

    for p in range(min(NBUF - 1, n)):
        load(p)
    stage_s(0)
    if len(items) > 1:
        stage_s(1)
    mid_done = False
    for j in range(len(items)):
        p, hg = items[j]
        if hg == 0 and p + NBUF - 1 < n:
            load(p + NBUF - 1)
        stage_e(j)
        if j + 2 < len(items):
            stage_s(j + 2)
        stage_v(j)
        if mid_fn is not None and not mid_done and j + 1 >= min(mid_at, len(items)):
            mid_fn()
            mid_done = True
    for hg in range(2):
        g.cp('act', oT[0:65, hg * 512:(hg + 1) * 512], O_ps[hg][0:65, :], [f"O{hg}"], [f"oT{hg}"])

    def norm():
        g.op('dve', lambda e: e.reciprocal(out=rden[64:65, :], in_=oT[64:65, :]), ['oT0', 'oT1'], ['rden'])
        for hg in range(2):
            g.mm(bc_ps[hg][0:64, :], ones[64:65, 0:64], rden[64:65, hg * 512:(hg + 1) * 512], True, True,
                 ['ones', 'rden'], [f"bc{hg}"])
            g.tt('dve', outT[:, hg * 4:(hg + 1) * 4, :],
                 oT[0:64, hg * 512:(hg + 1) * 512].rearrange("p (a b) -> p a b", a=4),
                 bc_ps[hg][0:64, :].rearrange("p (a b) -> p a b", a=4), ALU.mult, [f"oT{hg}", f"bc{hg}"], [out_key])
    if defer_norm:
        return norm
    norm()
    return None


def attn_bufs(g, idb, n_s=4):
    bufs = dict(
        S_ps=[g.ps([128, 512], F32) for _ in range(n_s)], O_ps=[g.ps([128, 512], F32) for _ in range(2)],
        bc_ps=[g.ps([128, 512], F32) for _ in range(2)],
        kt_t=[g.sb([64, 8, 128], BF16) for _ in range(4)], vt_t=[g.sb([128, 520], BF16) for _ in range(4)],
        Sb=[g.sb([128, 4, 128], F32) for _ in range(2)], Pb=[g.sb([128, 4, 128], BF16) for _ in range(4)],
        oT=g.sb([128, 1024], F32), rden=g.sb([128, 1024], F32), ones=g.sb([128, 64], F32),
        irep=g.sb([128, 512], BF16))
    return bufs


def attn_consts(g, bufs, idb):
    g.op('pool', lambda e: e.memset(bufs['ones'][:], 1.0), [], ['ones'])
    for r4 in range(4):
        g.cp('pool', bufs['irep'][:, r4 * 128:(r4 + 1) * 128], idb[:], ['idb'], ['irep'])


def phase_c1(g, es_outer, P):
    S, NT, NQ, NB, KSEL = P['S'], P['NT'], P['NQ'], P['NB'], P['KSEL']
    NMAX = max(NT * 128, 9 * 128)
    with ExitStack() as es:
        g.es = es
        idf = g.sb([128, 128], F32); idb = g.sb([128, 128], BF16)
        bufs = attn_bufs(g, idb)
        score = [g.sb([128, NMAX], F32) for _ in range(2)]
        junk = [g.sb([128, NMAX], BF16) for _ in range(2)]
        table = g.sb([128, 3, 8, 128], F32)
        c15 = g.sb([128, 8], F32)
        admneg = g.sb([128, 256], F32)
        pow2 = g.sb([128, NB], F32)
        Dh = g.sb([128, 4, 128], BF16)
        qaT = g.sb([64, 8, 128], BF16); qiT = g.sb([64, 4, 128], BF16); wif = g.sb([128, 4], F32)
        kiT = [g.sb([64, 512], BF16) for _ in range(2)]
        R = g.sb([128, 4, 512], BF16)
        oaT = g.sb([64, 8, 128], BF16)
        sm = {k: g.sb([128, 1], F32) for k in ['mn', 'mx', 'lo', 'w0', 't', 'cnt', 'inc']}
        tz = {k: g.sb([128, 1], F32) for k in ['cpos', 'cnn', 'a', 'b', 'tie', 'r', 'nt', 'thr', 'carry']}
        CH = 2048
        zc = g.sb([128, CH], BF16)
        cum = g.sb([128, CH], F32)
        onesb = g.sb([128, CH], BF16)
        hw = g.sb([128, NB], F32)
        sc_ps = bufs['O_ps'][0]
        dots = [bufs['S_ps'][0], bufs['S_ps'][1], bufs['S_ps'][2], bufs['S_ps'][3]]
        dkeys = ['S0', 'S1', 'S2', 'S3']

        g.dma('sp', idf[:], P['ident'], [], ['idf'])
        g.cp('dve', idb[:], idf[:], ['idf'], ['idb'])
        g.dma('sp', pow2[:], P['pow2'].to_broadcast([128, NB]), [], ['pow2'])
        g.dma('sp', c15[:], P['c15'].to_broadcast([128, 8]), [], ['c15'])
        attn_consts(g, bufs, idb)
        g.op('pool', lambda e: e.memset(onesb[:], 1.0), [], ['onesb'])

        def geom(i):
            samp = (i == NQ)
            nkt = 9 if samp else 2 * i + 2
            return samp, nkt, nkt * 128

        def load_tables(ti, which):
            if which == 'table':
                g.dma('sp', table[:].rearrange("p a b c -> p (a b c)"), P['tdsa'][ti], [], ['table'])
                for j in range(3):
                    g.tt('pool', table[:, j, :, :], table[:, j, :, :], c15[:].unsqueeze(2).to_broadcast([128, 8, 128]),
                         ALU.subtract, ['table', 'c15'], ['table'])
            else:
                g.dma('sp', admneg[:], P['admneg'][ti], [], ['admneg'])

        def IDX(i):
            samp, nkt, N = geom(i)
            sc, sck = score[i % 2], f"score{i % 2}"
            kis = P['s_kiT'] if samp else P['p_kiT']
            g.dma('sp', qiT[:].rearrange("p a b -> p (a b)"), P['q_qiT'][i], [], ['qiT'])
            g.dma('sp', wif[:], P['q_wi'][i], [], ['wif'])
            for h in range(4):
                g.ts('pool', Dh[:, h, :], idf[:], wif[:, h:h + 1], 1.0 / 16.0, ALU.mult, ALU.mult, ['idf', 'wif'], ['Dh'])
            ngr = (nkt + 3) // 4
            for gi in range(ngr):
                n = min(512, N - gi * 512)
                b = gi % 2
                if samp:
                    g.dma('sp', kiT[b][:, 0:n], kis[:, gi * 512:gi * 512 + n], [], [f"kiT{b}"])
                else:
                    for tt in range(n // 128):
                        g.dma('sp', kiT[b][:, tt * 128:(tt + 1) * 128], kis[_ktmap(gi * 4 + tt, NT)], [], [f"kiT{b}"])
                for h in range(4):
                    g.mm(dots[h][:, 0:n], qiT[:, h, :], kiT[b][:, 0:n], True, True, [f"kiT{b}", 'qiT'], [dkeys[h]])
                for h in range(4):
                    g.act(R[:, h, 0:n], dots[h][:, 0:n], AF.Relu, [dkeys[h]], [f"R{h}"])
                for h in range(4):
                    g.mm(sc_ps[:, 0:n], Dh[:, h, :], R[:, h, 0:n], h == 0, h == 3, ['Dh', f"R{h}"], ['O0'])
                g.cp('act', sc[:, gi * 512:gi * 512 + n], sc_ps[:, 0:n], ['O0'], [sck])

        def BIS(i):
            samp, nkt, N = geom(i)
            sc, sck = score[i % 2], f"score{i % 2}"
            jk, jkk = junk[i % 2], f"junk{i % 2}"
            if i == 0 or samp:
                load_tables(1 if samp else 0, 'admneg')
            g.op('dve', lambda e: e.tensor_reduce(out=sm['mn'][:], in_=sc[:, 0:N], axis=AX.X, op=ALU.min),
                 [sck], ['mn'])
            g.op('dve', lambda e: e.tensor_reduce(out=sm['mx'][:], in_=sc[:, 0:N], axis=AX.X, op=ALU.max),
                 [sck], ['mx'])
            g.tt('dve', sc[:, N - 256:N], sc[:, N - 256:N], admneg[:], ALU.add, [sck, 'admneg', 'mn', 'mx'], [sck])
            g.ts('dve', sm['lo'][:], sm['mn'][:], -1.0, None, ALU.add, None, ['mn'], ['lo'])
            g.tt('dve', sm['w0'][:], sm['mx'][:], sm['lo'][:], ALU.subtract, ['mx', 'lo'], ['w0'])
            g.ts('dve', hw[:], pow2[:], sm['w0'][:], None, ALU.mult, None, ['pow2', 'w0'], ['hw'])
            ksel = (min(TOPK, (PAST + 64) // 4) if samp else KSEL)
            for k in range(NB):
                g.tt('dve', sm['t'][:], sm['lo'][:], hw[:, k:k + 1], ALU.add, ['lo', 'hw'], ['t'])
                g.ts('dve', jk[:, 0:N], sc[:, 0:N], sm['t'][:], 0.0, ALU.is_gt, ALU.add, [sck, 't'],
                     [jkk, 'cnt'], accum_out=sm['cnt'][:])
                g.ts('dve', sm['inc'][:], sm['cnt'][:], ksel - 0.5, None, ALU.is_gt, None, ['cnt'], ['inc'])
                g.stt(sm['lo'][:], sm['inc'][:], hw[:, k:k + 1], sm['lo'][:], ALU.mult, ALU.add,
                      ['inc', 'hw', 'lo'], ['lo'])
            kf = float(ksel)
            g.ts('dve', jk[:, 0:N], sc[:, 0:N], 0.0, 0.0, ALU.is_gt, ALU.add, [sck], [jkk, 'cpos'],
                 accum_out=tz['cpos'][:])
            g.ts('dve', jk[:, 0:N], sc[:, 0:N], 0.0, 0.0, ALU.is_ge, ALU.add, [sck, 'cpos'], [jkk, 'cnn'],
                 accum_out=tz['cnn'][:])
            g.ts('dve', tz['a'][:], tz['cpos'][:], kf - 0.5, None, ALU.is_lt, None, ['cpos'], ['tz_a'])
            g.ts('dve', tz['b'][:], tz['cnn'][:], kf + 0.5, None, ALU.is_gt, None, ['cnn'], ['tz_b'])
            g.tt('dve', tz['tie'][:], tz['a'][:], tz['b'][:], ALU.mult, ['tz_a', 'tz_b'], ['tie'])
            g.ts('dve', tz['r'][:], tz['cpos'][:], -1.0, kf, ALU.mult, ALU.add, ['cpos'], ['tz_r0'])
            g.tt('dve', tz['r'][:], tz['r'][:], tz['tie'][:], ALU.mult, ['tz_r0', 'tie'], ['tz_r'])
            g.ts('dve', tz['nt'][:], tz['tie'][:], -1.0, 1.0, ALU.mult, ALU.add, ['tie'], ['tz_nt'])
            g.tt('dve', tz['thr'][:], sm['lo'][:], tz['nt'][:], ALU.mult, ['lo', 'tz_nt'], ['thr'])
            for c0 in range(0, N, CH):
                n = min(CH, N - c0)
                g.ts('dve', zc[:, 0:n], sc[:, c0:c0 + n], 0.0, None, ALU.is_equal, None, [sck], ['zc'])
                init = 0.0 if c0 == 0 else tz['carry'][:]
                g.op('dve', lambda e, n=n, init=init: e.tensor_tensor_scan(
                    out=cum[:, 0:n], data0=onesb[:, 0:n], data1=zc[:, 0:n], initial=init, op0=ALU.mult, op1=ALU.add),
                    ['zc', 'onesb', 'carry'], ['cum'])
                if c0 + n < N:
                    g.cp('dve', tz['carry'][:], cum[:, n - 1:n], ['cum'], ['carry'])
                g.stt(zc[:, 0:n], cum[:, 0:n], tz['r'][:], zc[:, 0:n], ALU.is_le, ALU.mult, ['cum', 'tz_r', 'zc'], ['zc'])
                g.stt(zc[:, 0:n], sc[:, c0:c0 + n], tz['thr'][:], zc[:, 0:n], ALU.is_gt, ALU.max, [sck, 'thr', 'zc'], ['zc'])
                g.ts('dve', jk[:, c0:c0 + n], zc[:, 0:n], -1.0, -MASKNEG, ALU.add, ALU.mult, ['zc'], [jkk])
            if 'dbg_score' in P:
                g.dma('sp', P['dbg_score'][i, :, 0:N], sc[:, 0:N], [sck], ['dbg'])
                g.dma('sp', P['dbg_junk'][i, :, 0:N], jk[:, 0:N], [jkk], ['dbg'])

        def ATT(i):
            samp, nkt, N = geom(i)
            jk, jkk = junk[i % 2], f"junk{i % 2}"
            if i == 0 or samp:
                load_tables(1 if samp else 0, 'table')
            g.dma('sp', qaT[:].rearrange("p a b -> p (a b)"), P['q_qaT'][i], [], ['qaT'])
            kas = P['s_KaT'] if samp else P['p_KaT']
            vas = P['s_Va'] if samp else P['p_Va']
            near = [kt for kt in range(nkt - 3, nkt) if kt >= 0]
            order = near + [kt for kt in range(nkt) if kt not in near]
            km = (lambda kt: kt) if samp else (lambda kt: _ktmap(kt, NT))
            return attn_core(g, bufs, order, lambda kt: kas[km(kt)], lambda kt: vas[km(kt)], qaT, 'qaT', table,
                             lambda kt: (kt - (nkt - 3)) if kt >= nkt - 3 else None,
                             lambda kt: (jk[:, kt * 128:(kt + 1) * 128], jkk), oaT, 'oaT',
                             mid_at=2 * len(near), mid_fn=(lambda: BIS(i + 1)) if i + 1 <= NQ else None,
                             defer_norm=True)

        IDX(0)
        BIS(0)
        if NQ >= 1:
            IDX(1)
        for i in range(NQ + 1):
            norm = ATT(i)
            if i + 2 <= NQ:
                IDX(i + 2)
            norm()
            g.dma('sp', P['q_oaT'][i], oaT[:].rearrange("p a b -> p (a b)"), ['oaT'], ['oaT'])
        g.es = es_outer
        g.emit()


def phase_c2(g, es_outer, P):
    S, NT, NQ = P['S'], P['NT'], P['NQ']
    with ExitStack() as es:
        g.es = es
        idf = g.sb([128, 128], F32); idb = g.sb([128, 128], BF16)
        bufs = attn_bufs(g, idb, n_s=2)
        table = g.sb([128, 6, 8, 128], F32)
        idx_sb = g.sb([128, NQ + 1], I32)
        wba = g.sb([64, 8, D], BF16); wbb = g.sb([64, 8, D], BF16); wout = g.sb([128, 8, D], BF16)
        stage = [g.sb([128, 2048], F32) for _ in range(2)]
        gbc = g.sb([128, D], F32); bbc = g.sb([128, D], F32)
        qbT = [g.sb([64, 8, 128], BF16) for _ in range(2)]
        oaT = [g.sb([64, 8, 128], BF16) for _ in range(2)]
        obT = g.sb([64, 8, 128], BF16)
        sga = [g.sb([128, D], F32) for _ in range(2)]
        sgb = [g.sb([128, D], F32) for _ in range(2)]
        hown = [g.sb([128, D], F32) for _ in range(2)]
        mpb = g.sb([128, D], BF16); mixT = g.sb([128, 8, 128], BF16)
        small = [make_small(g) for _ in range(2)]
        y_ps = [g.ps([128, 512], F32) for _ in range(2)]
        tp = bufs['bc_ps'][1][:].bitcast(BF16)

        g.dma('sp', idf[:], P['ident'], [], ['idf'])
        g.cp('dve', idb[:], idf[:], ['idf'], ['idb'])
        g.dma('sp', idx_sb[:], P['own_idx'], [], ['idx'])
        g.dma('sp', gbc[:], P['ln2_g'].to_broadcast([128, D]), [], ['lng'])
        g.dma('sp', bbc[:], P['ln2_b'].to_broadcast([128, D]), [], ['lnb'])
        attn_consts(g, bufs, idb)
        for wsrc, wdst, key in [(P['w_branch_a'], wba, 'wba'), (P['w_branch_b'], wbb, 'wbb')]:
            for h in range(8):
                st = stage[h % 2]
                g.dma('sp', st[0:64, 0:D], wsrc[h * 64:(h + 1) * 64, :], [], [f"stg{h % 2}"])
                g.cp(g.cast_eng(), wdst[:, h, :], st[0:64, 0:D], [f"stg{h % 2}"], [key])
        load_weight_bf16(g, wout, P['w_out'], 8, D, [st[:] for st in stage], ['stg0', 'stg1'], 'wout_p')
        g.op('pe', lambda e: e.nop(), reads=[f"wout_p_{kc}_0" for kc in range(8)], writes=['wout'])

        def loads(i):
            b = i % 2
            g.dma('sp', qbT[b][:].rearrange("p a b -> p (a b)"), P['q_qbT'][i], [], [f"qbT{b}"])
            g.dma('sp', oaT[b][:].rearrange("p a b -> p (a b)"), P['q_oaT'][i], [], [f"oaT{b}"])
            g.dma('sp', sga[b][:], P['q_sga'][i], [], [f"sga{b}"])
            g.dma('sp', sgb[b][:], P['q_sgb'][i], [], [f"sgb{b}"])
            if i == NQ:
                g.dma('pool', hown[b][:], P['h_samp'], [], [f"hown{b}"])
                return
            g.op('pool', lambda e: e.indirect_dma_start(
                out=hown[b][:], out_offset=None, in_=P['h_all'],
                in_offset=bass.IndirectOffsetOnAxis(ap=idx_sb[:, i:i + 1], axis=0)),
                reads=['idx'], writes=[f"hown{b}"], dma=True)

        loads(0)
        for i in range(NQ + 1):
            samp = (i == NQ)
            b = i % 2
            if i == 0 or samp:
                g.dma('sp', table[:].rearrange("p a b c -> p (a b c)"), P['tband'][1 if samp else 0], [], ['table'])
            if samp:
                kts = list(range(5)); base = 0
                kbs, vbs = P['s_KbT'], P['s_Vb']
            else:
                base = 2 * i - 4
                kts = [kt for kt in range(base, 2 * i + 2) if kt >= 0]
                kbs, vbs = P['p_KbT'], P['p_Vb']
            km = (lambda kt: kt) if samp else (lambda kt: _ktmap(kt, NT))
            attn_core(g, bufs, kts, lambda kt: kbs[km(kt)], lambda kt: vbs[km(kt)], qbT[b], f"qbT{b}", table,
                      lambda kt, base=base: kt - base, None, obT, 'obT')
            if i + 1 <= NQ:
                loads(i + 1)
            for (oT_, okey, w_, wkey, sg, sgk) in [(oaT[b], f"oaT{b}", wba, 'wba', sga[b], f"sga{b}"),
                                                   (obT, 'obT', wbb, 'wbb', sgb[b], f"sgb{b}")]:
                for hh in range(2):
                    for h in range(8):
                        g.mm(y_ps[hh][:], oT_[:, h, :], w_[:, h, hh * 512:(hh + 1) * 512], h == 0, h == 7,
                             [okey, wkey], [f"y{hh}"])
                    g.tt('dve', sg[:, hh * 512:(hh + 1) * 512], sg[:, hh * 512:(hh + 1) * 512], y_ps[hh][:], ALU.mult,
                         [sgk, f"y{hh}"], [sgk])
            g.tt('pool', mpb[:], sga[b][:], sgb[b][:], ALU.add, [f"sga{b}", f"sgb{b}"], ['mpb'])
            for k in range(8):
                g.tr(tp[:, k * 128:(k + 1) * 128], mpb[:, k * 128:(k + 1) * 128], idb[:], ['mpb', 'idb'], ['bc1'])
            g.cp('act', mixT[:].rearrange("p a b -> p (a b)"), tp[:, 0:1024], ['bc1'], ['mixT'])
            hk = f"hown{b}"
            for hh in range(2):
                for k in range(8):
                    g.mm(y_ps[hh][:], mixT[:, k, :], wout[:, k, hh * 512:(hh + 1) * 512], k == 0, k == 7,
                         ['mixT', 'wout'], [f"y{hh}"])
                hs = hown[b][:, hh * 512:(hh + 1) * 512]
                g.stt(hs, hs, ALPHA, y_ps[hh][:], ALU.mult, ALU.add, [hk, f"y{hh}"], [hk])
            layer_norm_tile(g, hown[b][:], hk, gbc, bbc, small[b], f"ln2{b}")
            g.dma('pool', P['h2_own'][i * 128:(i + 1) * 128, :], hown[b][:], [hk], [hk])
        g.es = es_outer
        g.emit()


def build_program(S, debug=None, nphases=5, bparts=('roll', 'cache', 'b2'), npairs=4, cc_inc=16):
    NT = S // 128
    NQ = S // 256
    NTA = NT + 1
    NQA = NQ + 1
    nc = bass.Bass("TRN2", target_bir_lowering=False)
    dbg = set(debug or [])

    def din(name, shape, dt=F32):
        return nc.dram_tensor(name, list(shape), dt, kind="ExternalInput").ap()

    def dout(name, shape, dt=F32):
        return nc.dram_tensor(name, list(shape), dt, kind="ExternalOutput").ap()

    def dscr(name, shape, dt=F32):
        if name in dbg:
            return dout(name, shape, dt)
        return nc.dram_tensor(name, list(shape), dt, kind="Internal").ap()

    P = dict(S=S, NT=NT, NQ=NQ, NB=N_BISECT, KSEL=min(TOPK, S // 4), bparts=set(bparts), npairs=npairs,
             CHR=min(512, (NT // 2) * 128))
    NH = NT // 2
    P['x_all'] = din("x_all", [(NH + 1) * 128, D])
    P['ident'] = din("ident", [128, 128])
    P['own_idx'] = din("own_idx", [128, NQA], I32)
    for nm, shp in [("ffn1_wi", [D, 2 * FF]), ("ffn1_wo", [FF, D]), ("ffn2_wi", [D, 2 * FF]), ("ffn2_wo", [FF, D]),
                    ("w_in", [D, N_IN]), ("w_branch_a", [512, D]), ("w_branch_b", [512, D]), ("w_out", [D, D]),
                    ("ln1_g", [1, D]), ("ln1_b", [1, D]), ("ln2_g", [1, D]), ("ln2_b", [1, D]),
                    ("ln3_g", [1, D]), ("ln3_b", [1, D]),
                    ("c_ka", [PAST, 512]), ("c_va", [PAST, 512]), ("c_ki", [PAST, 64]),
                    ("c_kb", [BAND, 512]), ("c_vb", [BAND, 512]),
                    ("admneg", [2, 128, 256]), ("tdsa", [2, 128, 3 * 8 * 128]), ("tband", [2, 128, 6 * 8 * 128]),
                    ("c15", [1, 8]), ("pow2", [1, N_BISECT])]:
        P[nm] = din(nm, shp)
    P['h_all'] = dscr("h_all", [NT * 128, D])
    P['h_half'] = dscr("h_half", [NH * 128, D])
    P['h_samp'] = dscr("h_samp", [128, D])
    for pre, n_a, n_b in [('p_', NT, NT), ('l_', NT // 2, NT // 2), ('s_', 9, 5)]:
        P[pre + 'KaT'] = dscr(pre + "KaT", [n_a, 64, 1024], BF16)
        P[pre + 'Va'] = dscr(pre + "Va", [n_a, 128, 520], BF16)
        P[pre + 'kiT'] = dscr(pre + "kiT", [64, n_a * 128], BF16) if pre == 's_' else dscr(pre + "kiT", [n_a, 64, 128], BF16)
        P[pre + 'KbT'] = dscr(pre + "KbT", [n_b, 64, 1024], BF16)
        P[pre + 'Vb'] = dscr(pre + "Vb", [n_b, 128, 520], BF16)
    P['q_qaT'] = dscr("q_qaT", [NQA, 64, 1024], BF16)
    P['q_qiT'] = dscr("q_qiT", [NQA, 64, 512], BF16)
    P['q_wi'] = dscr("q_wi", [NQA, 128, 4])
    P['q_qbT'] = dscr("q_qbT", [NQA, 64, 1024], BF16)
    P['q_sga'] = dscr("q_sga", [NQA, 128, D])
    P['q_sgb'] = dscr("q_sgb", [NQA, 128, D])
    P['q_oaT'] = dscr("q_oaT", [NQA, 64, 1024], BF16)
    P['h2_own'] = dscr("h2_own", [NQA * 128, D])
    if 'dbg_score' in dbg:
        NMAX = max(NT * 128, 9 * 128)
        P['dbg_score'] = dscr("dbg_score", [NQA, 128, NMAX])
        P['dbg_junk'] = dscr("dbg_junk", [NQA, 128, NMAX], BF16)
    P['y_own'] = dout("y_own", [NQA * 128, D])
    P['kA_out'] = dout("kA_out", [S // 2, 512]); P['vA_out'] = dout("vA_out", [S // 2, 512])
    P['kidx_out'] = dout("kidx_out", [S // 2, 64])
    P['kB_out'] = dout("kB_out", [512, 512]); P['vB_out'] = dout("vB_out", [512, 512])
    P['skA'] = dout("skA", [128, 512]); P['svA'] = dout("svA", [128, 512]); P['skidx'] = dout("skidx", [128, 64])
    P['skB'] = dout("skB", [512, 512]); P['svB'] = dout("svB", [512, 512])

    with ExitStack() as es:
        g = G(nc, es)
        CHR = P['CHR']
        tpc = CHR // 128

        def after_tile(t):
            if t < NH and (t + 1) % tpc == 0:
                k = t // tpc
                g.op('pool', lambda e: e.collective_compute(
                    "AllGather", ALU.bypass, replica_groups=[[0, 1], [2, 3], [4, 5], [6, 7]][:P['npairs']],
                    ins=[P['h_half'][k * CHR:(k + 1) * CHR, :]],
                    outs=[P['h_all'][2 * k * CHR:2 * (k + 1) * CHR, :]]),
                    reads=[f"dst{tt}" for tt in range(k * tpc, (k + 1) * tpc)], writes=[f"hall{k}"], dma='cc')
        ffn_phase(g, nc, es, P['x_all'],
                  lambda t: P['h_half'][t * 128:(t + 1) * 128, :] if t < NH else P['h_samp'],
                  P['ffn1_wi'], P['ffn1_wo'], P['ln1_g'], P['ln1_b'], P['ident'], NH + 1, after_tile=after_tile)
        if nphases >= 2:
            phase_b(g, es, P)
        if nphases >= 3:
            phase_c1(g, es, P)
        if nphases >= 4:
            phase_c2(g, es, P)
        if nphases >= 5:
            ffn_phase(g, nc, es, P['h2_own'], P['y_own'], P['ffn2_wi'], P['ffn2_wo'], P['ln3_g'], P['ln3_b'],
                      P['ident'], NQA)
    return nc


def _t5_bucket_np(rel):
    import math
    rel = np.asarray(rel, np.int64)
    half, exact = 16, 8
    n = np.abs(rel)
    lr = np.log(np.maximum(n, 1).astype(np.float32) / np.float32(exact)) / np.float32(math.log(128 / exact))
    large = np.minimum(exact + (lr.astype(np.float32) * np.float32(half - exact)).astype(np.int32), half - 1)
    return (rel > 0).astype(np.int32) * half + np.where(n < exact, n, large)


def _hrow(tok, S):
    half = S // 2
    chr_ = min(512, half)
    tok = np.asarray(tok)
    r, w = tok // half, tok % half
    return (2 * (w // chr_) + r) * chr_ + (w % chr_)


def _ktmap(t, NT):
    NH = NT // 2
    CT = min(8, NH)
    r, tt = t // NH, t % NH
    return (2 * (tt // CT) + r) * CT + (tt % CT)


def _qpos(r, i):
    p = np.arange(128)
    return np.where(p < 64, (4 * i + r) * 64 + p, (4 * i + 2 + r) * 64 + (p - 64))


def _tables(r, t5_bias, rel_bias):
    kk = np.arange(128)
    i = 2
    qp = _qpos(r, i)
    adm = np.zeros((2, 128, 256), np.float32)
    kc = (2 * i * 128 + np.arange(256)) // 64
    adm[0] = np.where(kc[None, :] <= (qp // 64)[:, None], 0.0, NEG)
    tdsa = np.zeros((2, 128, 3, 8, 128), np.float32)
    for j in range(3):
        kpos = (2 * i - 1 + j) * 128 + kk
        bk = _t5_bucket_np(kpos[:, None] - qp[None, :])
        tdsa[0, :, j] = np.transpose(t5_bias[:, bk], (1, 0, 2))
    tband = np.full((2, 128, 6, 8, 128), MASKNEG, np.float32)
    for j in range(6):
        kpos = (2 * i - 4 + j) * 128 + kk
        rel = kpos[:, None] - qp[None, :]
        vis = ((kpos // 64)[:, None] <= (qp // 64)[None, :]) & ((kpos // 64)[:, None] >= (qp // 64)[None, :] - 8)
        ridx = np.clip(rel, -128, 63) + 128
        vals = np.transpose(rel_bias[:, ridx], (1, 0, 2))
        tband[0, :, j] = np.where(vis[:, None, :], vals, MASKNEG)
    qs = PAST + np.minimum(np.arange(128), 63)
    cols = 7 * 128 + np.arange(256)
    adm[1] = np.where(cols[None, :] < PAST + 64, 0.0, NEG) + np.zeros((128, 1), np.float32)
    for j in range(3):
        kpos = (6 + j) * 128 + kk
        bk = _t5_bucket_np(kpos[:, None] - qs[None, :])
        tdsa[1, :, j] = np.transpose(t5_bias[:, bk], (1, 0, 2))
    for j in range(5):
        kpos = (PAST - BAND) + j * 128 + kk
        rel = kpos[:, None] - qs[None, :]
        vis = (kpos < PAST + 64)[:, None] & np.ones((1, 128), bool)
        ridx = np.clip(rel, -128, 63) + 128
        vals = np.transpose(rel_bias[:, ridx], (1, 0, 2))
        tband[1, :, j] = np.where(vis[:, None, :], vals, MASKNEG)
    return adm, tdsa.reshape(2, 128, -1), tband.reshape(2, 128, -1)


_PROG = {}


def _prep_inputs(inputs, S, n_cores=8):
    NT, NQ = S // 128, S // 256
    f = lambda a: np.ascontiguousarray(a, dtype=np.float32)
    common = {
        "ident": np.eye(128, dtype=np.float32),
        "ffn1_wi": f(inputs['ffn1_wi'][0]), "ffn1_wo": f(inputs['ffn1_wo'][0]),
        "ffn2_wi": f(inputs['ffn2_wi'][0]), "ffn2_wo": f(inputs['ffn2_wo'][0]),
        "w_in": f(inputs['w_in'][0]), "w_branch_a": f(inputs['w_branch_a'][0]),
        "w_branch_b": f(inputs['w_branch_b'][0]), "w_out": f(inputs['w_out'][0]),
        "ln1_g": f(inputs['ln1_g']), "ln1_b": f(inputs['ln1_b']), "ln2_g": f(inputs['ln2_g']),
        "ln2_b": f(inputs['ln2_b']), "ln3_g": f(inputs['ln3_g']), "ln3_b": f(inputs['ln3_b']),
        "c15": f(np.asarray(inputs['t5_bias'])[:, 15][None, :]),
        "pow2": (0.5 ** np.arange(1, N_BISECT + 1, dtype=np.float64)).astype(np.float32)[None, :],
    }
    tabs = [_tables(r, np.asarray(inputs['t5_bias'], np.float32), np.asarray(inputs['rel_bias_b'][0], np.float32))
            for r in range(2)]
    maps = []
    for c in range(n_cores):
        b, r = c // 2, c % 2
        xs = np.zeros((128, D), np.float32)
        xs[:64] = inputs['x_sample'][c]
        own = np.zeros((128, NQ + 1), np.int32)
        for i in range(NQ):
            own[:, i] = _hrow(_qpos(r, i), S)
        own[:, NQ] = 0
        m = dict(common)
        m.update({
            "x_all": np.concatenate([f(inputs['x_prompt'][b, r * (S // 2):(r + 1) * (S // 2)]), xs], 0),
            "own_idx": own,
            "c_ka": f(inputs['cache_k_a'][0, c]).reshape(PAST, 512),
            "c_va": f(inputs['cache_v_a'][0, c]).reshape(PAST, 512),
            "c_ki": f(inputs['cache_kidx_a'][0, c]),
            "c_kb": f(inputs['cache_k_b'][0, c]).reshape(BAND, 512),
            "c_vb": f(inputs['cache_v_b'][0, c]).reshape(BAND, 512),
            "admneg": tabs[r][0], "tdsa": tabs[r][1], "tband": tabs[r][2],
        })
        maps.append(m)
    return maps


def _assemble(results, S, B=4):
    NQ = S // 256
    keep = min(BAND, S)
    y_prompt = np.zeros((B, S, D), np.float32)
    y_sample = np.zeros((8, 64, D), np.float32)
    nka = np.zeros((1, B, S, 8, 64), np.float32); nva = np.zeros_like(nka)
    nki = np.zeros((1, B, S, 64), np.float32)
    nkb = np.zeros((1, B, keep, 8, 64), np.float32); nvb = np.zeros_like(nkb)
    ska = np.zeros((1, 8, 64, 8, 64), np.float32); sva = np.zeros_like(ska)
    ski = np.zeros((1, 8, 64, 64), np.float32)
    skb = np.zeros((1, 8, BAND, 8, 64), np.float32); svb = np.zeros_like(skb)
    for c, res in enumerate(results):
        b, r = c // 2, c % 2
        yo = res['y_own']
        for i in range(NQ):
            qp = _qpos(r, i)
            y_prompt[b, qp] = yo[i * 128:(i + 1) * 128]
        y_sample[c] = yo[NQ * 128:NQ * 128 + 64]
        hs = slice(r * (S // 2), (r + 1) * (S // 2))
        nka[0, b, hs] = res['kA_out'].reshape(S // 2, 8, 64)
        nva[0, b, hs] = res['vA_out'].reshape(S // 2, 8, 64)
        nki[0, b, hs] = res['kidx_out']
        if r == 1:
            nkb[0, b] = res['kB_out'].reshape(BAND, 8, 64)[-keep:]
            nvb[0, b] = res['vB_out'].reshape(BAND, 8, 64)[-keep:]
        ska[0, c] = res['skA'][:64].reshape(64, 8, 64)
        sva[0, c] = res['svA'][:64].reshape(64, 8, 64)
        ski[0, c] = res['skidx'][:64]
        skb[0, c] = res['skB'].reshape(BAND, 8, 64)
        svb[0, c] = res['svB'].reshape(BAND, 8, 64)
    return (y_prompt, y_sample, nka, nva, nki, nkb, nvb, ska, sva, ski, skb, svb)


def kernel(**inputs):
    S = int(np.asarray(inputs['x_prompt']).shape[1])
    if S not in _PROG:
        _PROG[S] = build_program(S)
    nc = _PROG[S]
    maps = _prep_inputs(inputs, S)
    res = run_bass_kernel_spmd(nc, maps, core_ids=list(range(8)))
    return _assemble(res.results, S)
```
